# Optimizing a Trainium2 kernel written in Bass

```python
import math
import jax, jax.numpy as jnp
from jax import lax
import numpy as np

D_MODEL = 1024
BATCH = 8
SEQ = 2048
DEPTH = 4
DEC_BATCH = 1
DEC_SEQ = 16384
PAST_LEN = 128

N_Q_HEADS = 8
N_KV_HEADS = 2
Q_PER_KV = N_Q_HEADS // N_KV_HEADS
HEAD_DIM = D_MODEL // N_Q_HEADS
DIFF_DIM = HEAD_DIM // 2
D_FF = 2816
BLOCK_Q = 128
WINDOW = 128
GRID_W = 64
N_BUCKETS = 32
MAX_DISTANCE = 128
ROPE_THETA = 10000.0
EPS = 1e-6
N_SUBLAYERS = 3
N_BRANCH = 3
N_BIAS_HEADS = 2 * N_Q_HEADS
Q_W = N_Q_HEADS * HEAD_DIM
KV_W = N_KV_HEADS * HEAD_DIM
BRANCH_IN = Q_W + 2 * KV_W
W_IN_COLS = N_BRANCH * BRANCH_IN + N_BRANCH * D_MODEL
NEG = -1e30

kernel_name = "hybrid_axial_window_diff_encoder"

F32 = jnp.float32


def rms_norm(x, g):
    x32 = x.astype(F32)
    y = x32 * lax.rsqrt(jnp.mean(x32 * x32, axis=-1, keepdims=True) + EPS)
    return (y * g.astype(F32)).astype(x.dtype)


def t5_bucket(rel):
    half = N_BUCKETS // 2
    max_exact = half // 2
    ret = jnp.where(rel > 0, half, 0)
    n = jnp.abs(rel)
    large = max_exact + (jnp.log(jnp.maximum(n, 1).astype(F32) / max_exact)
                         / math.log(MAX_DISTANCE / max_exact) * (half - max_exact)).astype(jnp.int32)
    large = jnp.minimum(large, half - 1)
    return ret + jnp.where(n < max_exact, n, large)


def axial_rope_tables(seq):
    rows = seq // GRID_W
    row = jnp.repeat(jnp.arange(rows), GRID_W).astype(F32)
    col = jnp.tile(jnp.arange(GRID_W), rows).astype(F32)
    nfreq = HEAD_DIM // 4
    inv = ROPE_THETA ** (-jnp.arange(nfreq, dtype=F32) / nfreq)
    ang_r = row[:, None] * inv
    ang_c = col[:, None] * inv
    ang = jnp.concatenate([ang_r, ang_r, ang_c, ang_c], axis=-1)
    return jnp.cos(ang), jnp.sin(ang)


def rotate_half(u):
    u1, u2 = jnp.split(u, 2, axis=-1)
    return jnp.concatenate([-u2, u1], axis=-1)


def apply_axial_rope(x, cos, sin):
    x32 = x.astype(F32)
    xr, xc = jnp.split(x32, 2, axis=-1)
    rot = jnp.concatenate([rotate_half(xr), rotate_half(xc)], axis=-1)
    return (x32 * cos[None, :, None, :] + rot * sin[None, :, None, :]).astype(x.dtype)


def to_blocks(q):
    b, s = q.shape[:2]
    return q.reshape(b, s // BLOCK_Q, BLOCK_Q, N_KV_HEADS, Q_PER_KV, -1).transpose(1, 0, 2, 3, 4, 5)


def from_blocks(o):
    nb, b = o.shape[:2]
    return o.transpose(1, 0, 2, 3, 4, 5).reshape(b, nb * BLOCK_Q, -1)


def axial_rope_attention(q, k, v, cos, sin):
    q = apply_axial_rope(q, cos, sin)
    k = apply_axial_rope(k, cos, sin)
    scale = HEAD_DIM ** -0.5

    def block(qb):
        sc = jnp.einsum('bqgrd,bkgd->bgrqk', qb, k).astype(F32) * scale
        p = jax.nn.softmax(sc, axis=-1).astype(v.dtype)
        return jnp.einsum('bgrqk,bkgd->bqgrd', p, v)

    return from_blocks(lax.map(block, to_blocks(q)))


def sink_window_attention(q, k, v, sink, bias_table):
    b, s = q.shape[:2]
    nb = s // BLOCK_Q
    pad = ((0, 0), (BLOCK_Q, BLOCK_Q), (0, 0), (0, 0))

    def band(t):
        tp = jnp.pad(t, pad).reshape(b, nb + 2, BLOCK_Q, N_KV_HEADS, HEAD_DIM)
        return jnp.concatenate([tp[:, :-2], tp[:, 1:-1], tp[:, 2:]], axis=2)

    kb, vb = band(k), band(v)
    qb = q.reshape(b, nb, BLOCK_Q, N_KV_HEADS, Q_PER_KV, HEAD_DIM)
    i = jnp.arange(BLOCK_Q)
    j = jnp.arange(3 * BLOCK_Q)
    rel = j[None, :] - BLOCK_Q - i[:, None]
    kpos = jnp.arange(nb)[:, None] * BLOCK_Q - BLOCK_Q + j[None, :]
    allowed = (jnp.abs(rel) <= WINDOW)[None] & ((kpos >= 0) & (kpos < s))[:, None, :]
    bias = bias_table[:, :N_Q_HEADS][t5_bucket(rel)]
    bias = bias.transpose(2, 0, 1).reshape(N_KV_HEADS, Q_PER_KV, BLOCK_Q, 3 * BLOCK_Q).astype(F32)
    scale = HEAD_DIM ** -0.5
    sc = jnp.einsum('bnqgrd,bnkgd->bngrqk', qb, kb).astype(F32) * scale + bias
    sc = jnp.where(allowed[None, :, None, None], sc, NEG)
    sink_col = jnp.broadcast_to(sink.astype(F32).reshape(N_KV_HEADS, Q_PER_KV, 1, 1), sc.shape[:-1] + (1,))
    p = jax.nn.softmax(jnp.concatenate([sc, sink_col], axis=-1), axis=-1)[..., :-1].astype(v.dtype)
    o = jnp.einsum('bngrqk,bnkgd->bnqgrd', p, vb)
    return o.reshape(b, s, -1)


def diff_attention(q, k, v, lam, lam_init, g_subln, bias_table):
    b, s = q.shape[:2]
    nb = s // BLOCK_Q
    q1, q2 = jnp.split(q, 2, axis=-1)
    k1, k2 = jnp.split(k, 2, axis=-1)
    scale = DIFF_DIM ** -0.5
    keypos = jnp.arange(s)
    table_c = bias_table[:, N_Q_HEADS:]

    def block(args):
        idx, q1b, q2b = args
        qpos = idx * BLOCK_Q + jnp.arange(BLOCK_Q)
        rel = keypos[None, :] - qpos[:, None]
        bias = table_c[t5_bucket(rel)]
        bias = bias.transpose(2, 0, 1).reshape(N_KV_HEADS, Q_PER_KV, BLOCK_Q, s).astype(F32)
        p1 = jax.nn.softmax(jnp.einsum('bqgrd,bkgd->bgrqk', q1b, k1).astype(F32) * scale + bias, axis=-1)
        p2 = jax.nn.softmax(jnp.einsum('bqgrd,bkgd->bgrqk', q2b, k2).astype(F32) * scale + bias, axis=-1)
        w = (p1 - lam * p2).astype(v.dtype)
        return jnp.einsum('bgrqk,bkgd->bqgrd', w, v)

    o = from_blocks(lax.map(block, (jnp.arange(nb), to_blocks(q1), to_blocks(q2))))
    o = rms_norm(o.reshape(b, s, N_Q_HEADS, HEAD_DIM), g_subln) * (1.0 - lam_init)
    return o.reshape(b, s, -1)


def swiglu(h, w_in_ff, w_out_ff):
    gate, up = jnp.split(h @ w_in_ff, 2, axis=-1)
    return (jax.nn.silu(gate) * up) @ w_out_ff


def trunk(x, c, p):
    b, s, _ = x.shape
    cos, sin = axial_rope_tables(s)
    split_idx = [int(v) for v in np.cumsum([Q_W, KV_W, KV_W] * N_BRANCH)]
    for l in range(DEPTH):
        mod = (jax.nn.silu(c) @ p['w_ada'][l] + p['b_ada'][l]).reshape(b, 3 * N_SUBLAYERS, 1, D_MODEL)

        def modulate(h, jj):
            return rms_norm(h, p['g_norm'][l, jj]) * (1.0 + mod[:, 3 * jj + 1]) + mod[:, 3 * jj]

        x = x + 0.5 * mod[:, 2] * swiglu(modulate(x, 0), p['w_ff_in'][l, 0], p['w_ff_out'][l, 0])

        n = modulate(x, 1)
        qa, ka, va, qb, kb, vb, qc, kc, vc, gates = jnp.split(n @ p['w_in'][l], split_idx, axis=-1)
        hq = (b, s, N_Q_HEADS, HEAD_DIM)
        hk = (b, s, N_KV_HEADS, HEAD_DIM)
        qa = rms_norm(qa.reshape(hq), p['g_qa'][l])
        ka = rms_norm(ka.reshape(hk), p['g_ka'][l])
        out_a = axial_rope_attention(qa, ka, va.reshape(hk), cos, sin)

        qb = rms_norm(qb.reshape(hq), p['g_qb'][l])
        kb = rms_norm(kb.reshape(hk), p['g_kb'][l])
        out_b = sink_window_attention(qb, kb, vb.reshape(hk), p['sink'][l], p['rel_bias'])

        qc = rms_norm(qc.reshape(b, s, N_Q_HEADS, 2, DIFF_DIM), p['g_qc'][l]).reshape(hq)
        kc = rms_norm(kc.reshape(b, s, N_KV_HEADS, 2, DIFF_DIM), p['g_kc'][l]).reshape(hk)
        lam_init = 0.8 - 0.6 * math.exp(-0.3 * l)
        lam = (jnp.exp(jnp.sum(p['lam_q1'][l].astype(F32) * p['lam_k1'][l].astype(F32)))
               - jnp.exp(jnp.sum(p['lam_q2'][l].astype(F32) * p['lam_k2'][l].astype(F32))) + lam_init)
        out_c = diff_attention(qc, kc, vc.reshape(hk), lam, lam_init, p['g_subln'][l], p['rel_bias'])

        g_a, g_b, g_c = jnp.split(jax.nn.sigmoid(gates), N_BRANCH, axis=-1)
        merged = g_a * out_a + g_b * out_b + g_c * out_c
        x = x + mod[:, 5] * (merged @ p['w_o'][l])

        x = x + 0.5 * mod[:, 8] * swiglu(modulate(x, 2), p['w_ff_in'][l, 1], p['w_ff_out'][l, 1])
    return x


def setup_inputs(seed: int = 0) -> dict:
    key = jax.random.key(seed)
    ks = jax.random.split(key, 24)

    def nrm(k, shape, scale):
        return jax.random.normal(k, shape, F32) * scale

    return {
        'x_prompt': nrm(ks[0], (BATCH, SEQ, D_MODEL), 1.0),
        'x_sample': nrm(ks[1], (DEC_BATCH, DEC_SEQ, D_MODEL), 1.0),
        'c_prompt': nrm(ks[2], (BATCH, D_MODEL), 1.0),
        'c_sample': nrm(ks[3], (DEC_BATCH, D_MODEL), 1.0),
        'w_ada': nrm(ks[4], (DEPTH, D_MODEL, 3 * N_SUBLAYERS * D_MODEL), 0.5 * D_MODEL ** -0.5),
        'b_ada': nrm(ks[5], (DEPTH, 3 * N_SUBLAYERS * D_MODEL), 0.02),
        'g_norm': 1.0 + nrm(ks[6], (DEPTH, N_SUBLAYERS, D_MODEL), 0.02),
        'w_ff_in': nrm(ks[7], (DEPTH, 2, D_MODEL, 2 * D_FF), D_MODEL ** -0.5),
        'w_ff_out': nrm(ks[8], (DEPTH, 2, D_FF, D_MODEL), D_FF ** -0.5),
        'w_in': nrm(ks[9], (DEPTH, D_MODEL, W_IN_COLS), D_MODEL ** -0.5),
        'w_o': nrm(ks[10], (DEPTH, D_MODEL, D_MODEL), D_MODEL ** -0.5),
        'g_qa': 1.0 + nrm(ks[11], (DEPTH, HEAD_DIM), 0.02),
        'g_ka': 1.0 + nrm(ks[12], (DEPTH, HEAD_DIM), 0.02),
        'g_qb': 1.0 + nrm(ks[13], (DEPTH, HEAD_DIM), 0.02),
        'g_kb': 1.0 + nrm(ks[14], (DEPTH, HEAD_DIM), 0.02),
        'g_qc': 1.0 + nrm(ks[15], (DEPTH, DIFF_DIM), 0.02),
        'g_kc': 1.0 + nrm(ks[16], (DEPTH, DIFF_DIM), 0.02),
        'sink': nrm(ks[17], (DEPTH, N_Q_HEADS), 0.5),
        'lam_q1': nrm(ks[18], (DEPTH, DIFF_DIM), 0.1),
        'lam_k1': nrm(ks[19], (DEPTH, DIFF_DIM), 0.1),
        'lam_q2': nrm(ks[20], (DEPTH, DIFF_DIM), 0.1),
        'lam_k2': nrm(ks[21], (DEPTH, DIFF_DIM), 0.1),
        'g_subln': 1.0 + nrm(ks[22], (DEPTH, HEAD_DIM), 0.02),
        'rel_bias': nrm(ks[23], (N_BUCKETS, N_BIAS_HEADS), 0.5),
    }


def reference(x_prompt, x_sample, c_prompt, c_sample, w_ada, b_ada, g_norm, w_ff_in, w_ff_out,
              w_in, w_o, g_qa, g_ka, g_qb, g_kb, g_qc, g_kc, sink, lam_q1, lam_k1, lam_q2, lam_k2,
              g_subln, rel_bias):
    params = dict(w_ada=w_ada, b_ada=b_ada, g_norm=g_norm, w_ff_in=w_ff_in, w_ff_out=w_ff_out,
                  w_in=w_in, w_o=w_o, g_qa=g_qa, g_ka=g_ka, g_qb=g_qb, g_kb=g_kb, g_qc=g_qc,
                  g_kc=g_kc, sink=sink, lam_q1=lam_q1, lam_k1=lam_k1, lam_q2=lam_q2, lam_k2=lam_k2,
                  g_subln=g_subln, rel_bias=rel_bias)
    y_prompt = trunk(x_prompt, c_prompt, params)
    y_sample = trunk(x_sample, c_sample, params)
    return (y_prompt, y_sample)
```

```python
import math
import numpy as np
import concourse.bass as bass
import concourse.mybir as mybir
from concourse.bass_utils import run_bass_kernel_spmd

F32 = mybir.dt.float32
BF16 = mybir.dt.bfloat16
AF = mybir.ActivationFunctionType
ALU = mybir.AluOpType

D = 1024
DFF = 2816
NCORE = 8
TT = 512
NT = 8
EPS = 1e-6
NEGM = -30000.0
QBASE = [0, 1536, 3072]
KBASE = [1024, 2560, 4096]
VBASE = [1280, 2816, 4352]
GBASE = [4608, 5632, 6656]
SC_A = 128.0 ** -0.5
SC_C = 64.0 ** -0.5


class Buf:
    __slots__ = ("name", "w", "r")

    def __init__(self, name):
        self.name = name
        self.w = {}
        self.r = {}


class Eng:
    def __init__(self, key, eng, sem):
        self.key = key
        self.eng = eng
        self.sem = sem
        self.cnt = 0
        self.waited = {}


class DSem:
    def __init__(self, key, h):
        self.key = key
        self.h = h
        self.val = 0


class B:
    def __init__(self, nc, stack):
        self.nc = nc
        self.semh = {}
        self.E = {}
        for key, eng in (("pe", nc.tensor), ("act", nc.scalar), ("dve", nc.vector), ("pool", nc.gpsimd)):
            h = stack.enter_context(nc.semaphore("s_" + key))
            self.semh[key] = h
            self.E[key] = Eng(key, eng, h)
        self.E["sp"] = Eng("sp", nc.sync, None)
        self.pools = {}
        for q, n in (("sp", 20), ("pool", 20)):
            lst = []
            for i in range(n):
                k = "d_%s_%d" % (q, i)
                h = stack.enter_context(nc.semaphore(k))
                self.semh[k] = h
                lst.append(DSem(k, h))
            self.pools[q] = [lst, 0]
        k = "cc"
        h = stack.enter_context(nc.semaphore(k))
        self.semh[k] = h
        self.ccsem = DSem(k, h)

    def _deps(self, E, reads, writes, partial):
        deps = {}
        for b in reads:
            for k, v in b.w.items():
                if deps.get(k, 0) < v:
                    deps[k] = v
        for b in writes:
            for k, v in b.r.items():
                if deps.get(k, 0) < v:
                    deps[k] = v
            if not partial:
                for k, v in b.w.items():
                    if deps.get(k, 0) < v:
                        deps[k] = v
        for k, v in deps.items():
            if E.waited.get(k, 0) >= v:
                continue
            if k == E.key and k == "pe":
                continue
            E.eng.wait_ge(self.semh[k], v)
            E.waited[k] = v

    def _upd(self, key, tok, reads, writes, partial):
        for b in reads:
            if b.r.get(key, 0) < tok:
                b.r[key] = tok
        for b in writes:
            if partial:
                if b.w.get(key, 0) < tok:
                    b.w[key] = tok
            else:
                b.w = {key: tok}
                b.r = {}

    def op(self, ek, fn, reads=(), writes=(), mark=True):
        E = self.E[ek]
        pr = tuple(b for b in reads if b.name.startswith("bank") and b not in writes)
        if pr:
            writes = tuple(writes) + pr
        self._deps(E, reads, writes, False)
        ins = fn(E.eng)
        if mark:
            E.cnt += 1
            ins.then_inc(E.sem, 1)
            tok = E.cnt
        else:
            tok = E.cnt + 1
        self._upd(E.key, tok, reads, writes, False)
        return ins

    def dma(self, qk, out, in_, reads=(), writes=(), partial=False):
        Q = self.E[qk]
        pool = self.pools[qk]
        s = pool[0][pool[1]]
        pool[1] = (pool[1] + 1) % len(pool[0])
        if s.val > 0 and Q.waited.get(s.key, 0) < s.val:
            Q.eng.wait_ge(s.h, s.val)
            Q.waited[s.key] = s.val
        self._deps(Q, reads, writes, partial)
        ins = Q.eng.dma_start(out=out, in_=in_)
        s.val += 16
        ins.then_inc(s.h, 16)
        self._upd(s.key, s.val, reads, writes, partial)
        return ins

    def collective(self, fn, reads=(), writes=()):
        Q = self.E["pool"]
        s = self.ccsem
        if s.val > 0 and Q.waited.get(s.key, 0) < s.val:
            Q.eng.wait_ge(s.h, s.val)
            Q.waited[s.key] = s.val
        self._deps(Q, reads, writes, False)
        ins = fn(Q.eng)
        s.val += 16
        ins.then_inc(s.h, 16)
        self._upd(s.key, s.val, reads, writes, False)

    def wait_all(self, ek, bufs):
        E = self.E[ek]
        self._deps(E, bufs, (), False)


def t5_bucket_np(rel):
    rel = np.asarray(rel, np.int64)
    half, max_exact = 16, 8
    ret = np.where(rel > 0, half, 0)
    n = np.abs(rel)
    lg = np.log(np.maximum(n, 1).astype(np.float32) / np.float32(max_exact)).astype(np.float32)
    large = max_exact + (lg / np.float32(math.log(128 / 8)) * np.float32(half - max_exact)).astype(np.int32)
    large = np.minimum(large, half - 1)
    return ret + np.where(n < max_exact, n, large)


_OPTS = {"skip": (), "ntiles": NT}


def build(kind, ntiles=None):
    skip = _OPTS["skip"]
    ntiles = _OPTS["ntiles"] if ntiles is None else ntiles
    from contextlib import ExitStack
    nc = bass.Bass("TRN2", target_bir_lowering=False)

    def din(name, shape, dt=F32):
        return nc.dram_tensor(name, list(shape), dt, kind="ExternalInput").ap()

    def dscr(name, shape, dt):
        return nc.dram_tensor(name, list(shape), dt).ap()

    has_attn = kind in ("mid", "last")
    has_kv = kind in ("first", "mid")
    if kind == "first":
        xin = din("xin", [4096, D])
    else:
        xT_i = din("xT_i", [NT, D, TT])
    if has_kv:
        xT_o = nc.dram_tensor("xT_o", [NT, D, TT], F32, kind="ExternalOutput").ap()
        KT_S_o = nc.dram_tensor("KT_S_o", [768, 2048], BF16, kind="ExternalOutput").ap()
        V_S_o = nc.dram_tensor("V_S_o", [768, 2048], BF16, kind="ExternalOutput").ap()
        KT_P_o = nc.dram_tensor("KT_P_o", [768, 2048], BF16, kind="ExternalOutput").ap()
        V_P_o = nc.dram_tensor("V_P_o", [768, 2048], BF16, kind="ExternalOutput").ap()
    else:
        y = nc.dram_tensor("y", [4096, D], F32, kind="ExternalOutput").ap()
    WIN = {}
    if has_attn:
        WIN["Aq"] = (din("wAq", [D, 6144]), [D, 6144])
        WIN["Ao"] = (din("wAo", [D, D]), [D, D])
        WIN["Afi"] = (din("wAfi", [D, 2 * DFF]), [D, 2 * DFF])
        WIN["Afo"] = (din("wAfo", [DFF, D]), [DFF, D])
        KT_all_i = din("KT_all_i", [8 * 768, 2048], BF16)
        V_all_i = din("V_all_i", [8 * 768, 2048], BF16)
        KT_P_i = din("KT_P_i", [768, 2048], BF16)
        V_P_i = din("V_P_i", [768, 2048], BF16)
        KT_Sx = din("KT_Sx", [768, 18 * 128], BF16)
        V_Sx = din("V_Sx", [768, 18 * 128], BF16)
    if has_kv:
        WIN["Bfi"] = (din("wBfi", [D, 2 * DFF]), [D, 2 * DFF])
        WIN["Bfo"] = (din("wBfo", [DFF, D]), [DFF, D])
        WIN["Bkv"] = (din("wBkv", [D, 1536]), [D, 1536])
    modT_i = din("modT", [128, 2 * 72 * 2])
    g_normT = din("g_normT", [128, 48])
    gvec = din("gvec", [128, 16])
    sinkrep = din("sinkrep", [128, 8])
    lamrep = din("lamrep", [128, 256])
    laminit = din("laminit", [128, 2])
    rb33 = din("rb33", [33, 16])
    rbrep = din("rbrep", [128, 512])
    ohrev = din("ohrev", [33, 512])
    ident_i = din("ident", [128, 128])
    perm_i = din("perm", [128, 128])
    ropeT = din("ropeT", [2, 128, 4096])
    indS = din("indS", [128, 3 * 4 * 128])
    hmask = din("hmask", [128, 2])
    WSC = {k: dscr("w%s_b" % k, shp, BF16) for k, (ap_, shp) in WIN.items()}
    G_d = dscr("G_d", [16, 128, 512], F32)

    with ExitStack() as st:
        bb = B(nc, st)

        def sb(name, shape, dt):
            return st.enter_context(nc.sbuf_tensor(name, list(shape), dt))

        bank = [st.enter_context(nc.psum_tensor("bank%d" % i, [128, 512], F32)) for i in range(8)]
        bankB = [Buf("bank%d" % i) for i in range(8)]

        xT = sb("xT", [128, 8, TT], F32); xTB = Buf("xT")
        hT = sb("hT", [128, 8, TT], BF16); hTB = Buf("hT")
        scr = sb("scr", [128, 22, TT], BF16); scrB = Buf("scr")
        big = sb("big", [128, 8, TT], F32); bigB = Buf("big")
        NW = 3
        wring = [sb("wring%d" % i, [128, 4096], BF16) for i in range(NW)]
        wringB = [Buf("wring%d" % i) for i in range(NW)]
        wstate = [0]
        kvstate = [0]
        kblk = [sb("kblk%d" % i, [128, 2048], BF16) for i in range(2)]
        kblkB = [Buf("kblk%d" % i) for i in range(2)]
        vblk = [sb("vblk%d" % i, [128, 2048], BF16) for i in range(2)]
        vblkB = [Buf("vblk%d" % i) for i in range(2)]
        nearK = sb("nearK", [128, 4, 768], BF16); nearKB = Buf("nearK")
        nearV = sb("nearV", [128, 4, 768], BF16); nearVB = Buf("nearV")
        NP_ = 4
        pT = [sb("pT%d" % i, [128, TT], BF16) for i in range(NP_)]
        pTB = [Buf("pT%d" % i) for i in range(NP_)]
        pstate = [0]
        NTMP = 4
        tmp = [sb("tmp%d" % i, [128, TT], F32) for i in range(NTMP)]
        tmpB = [Buf("tmp%d" % i) for i in range(NTMP)]
        tstate = [0]
        sq = [sb("sq%d" % i, [128, TT], BF16) for i in range(2)]
        sqB = [Buf("sq%d" % i) for i in range(2)]
        sqstate = [0]
        rstd = sb("rstd", [128, TT], F32); rstdB = Buf("rstd")
        qraw = sb("qraw", [128, TT], F32); qrawB = Buf("qraw")
        qn32 = sb("qn32", [128, TT], F32); qn32B = Buf("qn32")
        qnb = sb("qnb", [128, TT], BF16); qnbB = Buf("qnb")
        ropeC = sb("ropeC", [128, TT], F32); ropeS = sb("ropeS", [128, TT], F32); ropeB = Buf("rope")
        MBt = sb("MBt", [128, 8, 384], F32)
        TCt = sb("TCt", [128, 8, 384], F32)
        biasS = sb("biasS", [128, 128, 8], F32); biasSB = Buf("biasS")
        vtok = [sb("vtok%d" % i, [128, 768], BF16) for i in range(2)]
        vtokB = [Buf("vtok%d" % i) for i in range(2)]
        kst = [sb("kst%d" % i, [128, TT], BF16) for i in range(2)]
        kstB = [Buf("kst%d" % i) for i in range(2)]
        constB = Buf("const")
        ident = sb("ident_s", [128, 128], F32)
        ones_bf = sb("ones_bf", [128, 128], BF16)
        onesC_bf = sb("onesC_bf", [128, 128], BF16)
        perm_bf = sb("perm_bf", [128, 128], BF16)
        eps_t = sb("eps_t", [128, 1], F32)
        zero_t = sb("zero_t", [128, 1], F32)
        g_normT_s = sb("g_normT_s", [128, 48], F32)
        gvec_s = sb("gvec_s", [128, 16], F32)
        sink_s = sb("sink_s", [128, 8], F32)
        esink = sb("esink", [128, 8], F32)
        lam_s = sb("lam_s", [128, 256], F32)
        laminit_s = sb("laminit_s", [128, 2], F32)
        neglam = sb("neglam", [128, 1], F32)
        lamtmp = sb("lamtmp", [128, 64], F32)
        lamacc = sb("lamacc", [128, 8], F32)
        rb33_s = sb("rb33_s", [33, 16], F32)
        rbrep_s = sb("rbrep_s", [128, 512], F32)
        ohrev_s = sb("ohrev_s", [33, 512], F32)
        lhsG = sb("lhsG", [33, 128], F32)
        ones33 = sb("ones33", [33, 128], F32)
        indS_s = sb("indS_s", [128, 3 * 512], F32)
        hmask_s = sb("hmask_s", [128, 2], F32)
        modT = sb("modT_s", [128, 2, 72, 2], F32); modB = Buf("modT")
        Avec = sb("Avec", [128, 2 * 2 * 3 * 8], F32)
        Gvec = sb("Gvec", [128, 2 * 2 * 3 * 8], F32)
        miscB = Buf("misc")

        def nxt(state, n):
            i = state[0]
            state[0] = (i + 1) % n
            return i

        wB = {}
        inB = Buf("inputs")
        outB = Buf("outputs")
        GdB = Buf("G_d")

        def ld(dst, src, q="sp"):
            bb.dma(q, dst, src, reads=(), writes=(constB,), partial=True)

        ld(ident[:], ident_i)
        ld(modT[:].rearrange("p a f s -> p (a f s)"), modT_i); ld(g_normT_s[:], g_normT); ld(gvec_s[:], gvec)
        ld(sink_s[:], sinkrep); ld(lam_s[:], lamrep); ld(laminit_s[:], laminit)
        ld(rb33_s[:], rb33); ld(rbrep_s[:], rbrep); ld(ohrev_s[:], ohrev)
        ld(indS_s[:], indS); ld(hmask_s[:], hmask)
        ld(perm_bf[:], perm_i, q="pool")
        c2B = Buf("const2")
        bb.op("dve", lambda e: e.memset(ones_bf[:], 1.0), writes=(c2B,))
        bb.op("dve", lambda e: e.memset(onesC_bf[:], 0.0), writes=(c2B,))
        bb.op("dve", lambda e: e.memset(onesC_bf[0:64, 0:64], 1.0), writes=(c2B,))
        bb.op("dve", lambda e: e.memset(onesC_bf[64:128, 64:128], 1.0), writes=(c2B,))
        bb.op("dve", lambda e: e.memset(eps_t[:], EPS), writes=(c2B,))
        bb.op("dve", lambda e: e.memset(zero_t[:], 0.0), writes=(c2B,))
        bb.op("dve", lambda e: e.memset(ones33[:], 1.0), writes=(c2B,))
        CONST = (constB, c2B)

        order = [k for k in ("Bfi", "Bfo", "Bkv", "Aq", "Ao", "Afi", "Afo") if k in WIN]
        if has_attn:
            order = [k for k in ("Aq", "Ao", "Afi", "Afo", "Bfi", "Bfo", "Bkv") if k in WIN]
        for k in (order if 'cast' not in skip else []):
            src, shp = WIN[k]
            wb_ = wB.setdefault(k, Buf("w" + k))
            r0 = 0
            while r0 < shp[0]:
                r1 = min(r0 + 128, shp[0])
                bb.dma("pool", WSC[k][r0:r1, :], src[r0:r1, :], writes=(wb_,), partial=True)
                r0 = r1

        for l in range(2):
            for s in range(2):
                for j in range(3):
                    o = ((l * 2 + s) * 3 + j) * 8
                    bb.op("dve", lambda e, l=l, s=s, j=j, o=o: e.scalar_tensor_tensor(
                        out=Avec[:, o:o + 8], in0=modT[:, l, (3 * j + 1) * 8:(3 * j + 2) * 8, s], scalar=1.0,
                        in1=g_normT_s[:, (l * 3 + j) * 8:(l * 3 + j) * 8 + 8], op0=ALU.add, op1=ALU.mult),
                        reads=CONST, writes=(miscB,))
                    bb.op("dve", lambda e, l=l, s=s, j=j, o=o: e.tensor_scalar(
                        out=Gvec[:, o:o + 8], in0=modT[:, l, (3 * j + 2) * 8:(3 * j + 3) * 8, s],
                        scalar1=(1.0 if j == 1 else 0.5), scalar2=None, op0=ALU.mult),
                        reads=CONST, writes=(miscB,))
        for m in range(2):
            a0 = 2 * m * 64
            bb.op("dve", lambda e, a0=a0: e.tensor_tensor(out=lamtmp[:], in0=lam_s[:, a0:a0 + 64], in1=lam_s[:, a0 + 64:a0 + 128], op=ALU.mult),
                  reads=CONST + (miscB,), writes=(miscB,))
            bb.op("dve", lambda e, m=m: e.tensor_reduce(out=lamacc[:, m:m + 1], in_=lamtmp[:], axis=mybir.AxisListType.X, op=ALU.add),
                  reads=(miscB,), writes=(miscB,))
        bb.op("act", lambda e: e.activation(out=lamacc[:, 2:4], in_=lamacc[:, 0:2], func=AF.Exp), reads=(miscB,), writes=(miscB,))
        bb.op("dve", lambda e: e.tensor_tensor(out=lamacc[:, 4:5], in0=lamacc[:, 3:4], in1=lamacc[:, 2:3], op=ALU.subtract),
              reads=(miscB,), writes=(miscB,))
        bb.op("dve", lambda e: e.tensor_tensor(out=neglam[:, 0:1], in0=lamacc[:, 4:5], in1=laminit_s[:, 0:1], op=ALU.subtract),
              reads=(miscB,) + CONST, writes=(miscB,))
        bb.op("act", lambda e: e.activation(out=esink[:], in_=sink_s[:], func=AF.Exp), reads=CONST, writes=(miscB,))

        for hh in range(16 if 't5' not in skip else 0):
            bb.op("dve", lambda e, hh=hh: e.tensor_scalar(out=lhsG[:], in0=ones33[:], scalar1=rb33_s[:, hh:hh + 1], scalar2=None, op0=ALU.mult),
                  reads=CONST + (miscB,), writes=(miscB,))
            bi = 2 + hh % 2
            bb.op("pe", lambda e, bi=bi: e.matmul(bank[bi][:], lhsT=lhsG[:], rhs=ohrev_s[:], start=True, stop=True),
                  reads=(miscB,) + CONST, writes=(bankB[bi],))
            ti_ = nxt(tstate, NTMP)
            bb.op("dve", lambda e, bi=bi, ti_=ti_: e.tensor_copy(out=tmp[ti_][:], in_=bank[bi][:]), reads=(bankB[bi],), writes=(tmpB[ti_],))
            bb.dma("pool", G_d[hh], tmp[ti_][:], reads=(tmpB[ti_],), writes=(GdB,), partial=True)
        for hh in range(16 if 't5b' not in skip else 0):
            dst = MBt if hh < 8 else TCt
            for dd in range(3):
                d = dd - 1
                src = bass.AP(G_d.tensor, hh * 128 * 512 + 255 - d * 128, [[511, 128], [1, 128]])
                bb.dma("sp", dst[:, hh % 8, dd * 128:(dd + 1) * 128], src, reads=(GdB,), writes=(constB,), partial=True)

        def sclr(l, s, j, c, which):
            o = ((l * 2 + s) * 3 + j) * 8 + c
            return (Avec if which == "A" else Gvec)[:, o:o + 1]

        def rstd_from_bank(bk, dim):
            bb.op("act", lambda e: e.activation(out=rstd[:], in_=bank[bk][:], func=AF.Ln, bias=eps_t[:], scale=1.0 / dim),
                  reads=(bankB[bk],) + CONST, writes=(rstdB,))
            bb.op("act", lambda e: e.activation(out=rstd[:], in_=rstd[:], func=AF.Exp, scale=-0.5), reads=(rstdB,), writes=(rstdB,))

        def norm_mod(l, s, j, bk=7):
            for kc in range(8):
                si = nxt(sqstate, 2)
                bb.op("act", lambda e, kc=kc, si=si: e.activation(out=sq[si][:], in_=xT[:, kc, :], func=AF.Square),
                      reads=(xTB,), writes=(sqB[si],))
                bb.op("pe", lambda e, kc=kc, si=si: e.matmul(bank[bk][:], lhsT=ones_bf[:], rhs=sq[si][:], start=(kc == 0), stop=(kc == 7)),
                      reads=(sqB[si],) + CONST, writes=(bankB[bk],), mark=True)
            rstd_from_bank(bk, 1024.0)
            for kc in range(8):
                ti_ = nxt(tstate, NTMP)
                bb.op("dve", lambda e, kc=kc, ti_=ti_: e.tensor_tensor(out=tmp[ti_][:], in0=xT[:, kc, :], in1=rstd[:], op=ALU.mult),
                      reads=(xTB, rstdB), writes=(tmpB[ti_],))
                mo = 3 * j * 8 + kc
                bb.op("pool", lambda e, kc=kc, ti_=ti_, mo=mo: e.tensor_scalar(
                    out=hT[:, kc, :], in0=tmp[ti_][:], scalar1=sclr(l, s, j, kc, "A"), scalar2=modT[:, l, mo, s:s + 1],
                    op0=ALU.mult, op1=ALU.add), reads=(tmpB[ti_], miscB) + CONST, writes=(hTB,))

        def wload(srcs, wb):
            wi = nxt(wstate, NW)
            first = True
            for (o, a, b_, src) in srcs:
                dst = wring[wi][:, o:o + a * b_].rearrange("p (a b) -> p a b", a=a)
                bb.dma("sp", dst, src, reads=(wb,), writes=(wringB[wi],), partial=(not first))
                first = False
            return wi

        def resid_update(l, s, j, m, bk):
            bb.op("dve", lambda e: e.scalar_tensor_tensor(out=xT[:, m, :], in0=bank[bk][:], scalar=sclr(l, s, j, m, "G"),
                                                          in1=xT[:, m, :], op0=ALU.mult, op1=ALU.add),
                  reads=(bankB[bk], miscB, xTB), writes=(xTB,))

        def ffn(l, which, s, j):
            norm_mod(l, s, j)
            wk = "A" if l == 0 else "B"
            wsrc = WSC[wk + "fi"].rearrange("(kc p) n -> p kc n", p=128)
            wb = wB[wk + "fi"]
            groups = [(g0, min(2, 22 - g0)) for g0 in range(0, 22, 2)]
            loads = []

            def issue(gi):
                g0, n = groups[gi]
                return wload([(0, 8, n * 128, wsrc[:, :, g0 * 128:(g0 + n) * 128]),
                              (2048, 8, n * 128, wsrc[:, :, DFF + g0 * 128:DFF + (g0 + n) * 128])], wb)

            PF = 2
            for gi in range(min(PF, len(groups))):
                loads.append(issue(gi))
            for gi, (g0, n) in enumerate(groups):
                if gi + PF < len(groups):
                    loads.append(issue(gi + PF))
                wi = loads[gi]
                for jj in range(n):
                    jp = g0 + jj
                    bg, bu = (jp % 3) * 2, (jp % 3) * 2 + 1
                    for part, bk in ((0, bg), (1, bu)):
                        for kc in range(8):
                            o = part * 2048 + kc * n * 128 + jj * 128
                            bb.op("pe", lambda e, o=o, kc=kc, bk=bk, wi=wi: e.matmul(
                                bank[bk][:], lhsT=wring[wi][:, o:o + 128], rhs=hT[:, kc, :], start=(kc == 0), stop=(kc == 7)),
                                reads=(wringB[wi], hTB), writes=(bankB[bk],), mark=(kc == 7))
                    ti_ = nxt(tstate, NTMP)
                    bb.op("act", lambda e, bg=bg, ti_=ti_: e.activation(out=tmp[ti_][:], in_=bank[bg][:], func=AF.Silu),
                          reads=(bankB[bg],), writes=(tmpB[ti_],))
                    bb.op("dve", lambda e, bu=bu, ti_=ti_, jp=jp: e.tensor_tensor(out=scr[:, jp, :], in0=tmp[ti_][:], in1=bank[bu][:], op=ALU.mult),
                          reads=(tmpB[ti_], bankB[bu]), writes=(scrB,))
            wsrc2 = WSC[wk + "fo"].rearrange("(kc p) n -> p kc n", p=128)
            wb2 = wB[wk + "fo"]
            og = [(k0, min(4, 22 - k0)) for k0 in range(0, 22, 4)]
            loads = []

            def issue2(gi):
                k0, n = og[gi]
                return wload([(0, n, 1024, wsrc2[:, k0:k0 + n, :])], wb2)

            for gi in range(min(PF, len(og))):
                loads.append(issue2(gi))
            for gi, (k0, n) in enumerate(og):
                if gi + PF < len(og):
                    loads.append(issue2(gi + PF))
                wi = loads[gi]
                for kk in range(n):
                    kc = k0 + kk
                    for m in range(8):
                        bb.op("pe", lambda e, kk=kk, kc=kc, m=m, wi=wi: e.matmul(
                            bank[m][:], lhsT=wring[wi][:, kk * 1024 + m * 128:kk * 1024 + (m + 1) * 128], rhs=scr[:, kc, :],
                            start=(kc == 0), stop=(kc == 21)),
                            reads=(wringB[wi], scrB), writes=(bankB[m],), mark=(kc == 21 or (kk == n - 1 and m == 7)))
            for m in range(8):
                resid_update(l, s, j, m, m)

        def proj_chunks(wkey, cols, consumer, bks):
            wsrc = WSC[wkey].rearrange("(kc p) n -> p kc n", p=128)
            wb = wB[wkey]
            groups = [cols[i:i + 4] for i in range(0, len(cols), 4)]
            loads = []

            def issue(gi):
                g = groups[gi]
                srcs = []
                for ci, c0 in enumerate(g):
                    srcs.append((ci * 1024, 8, 128, wsrc[:, :, c0:c0 + 128]))
                return wload(srcs, wb)

            PF = 2
            for gi in range(min(PF, len(groups))):
                loads.append(issue(gi))
            n = 0
            for gi, g in enumerate(groups):
                if gi + PF < len(groups):
                    loads.append(issue(gi + PF))
                wi = loads[gi]
                for ci, c0 in enumerate(g):
                    bk = bks[n % len(bks)]
                    for kc in range(8):
                        o = ci * 1024 + kc * 128
                        bb.op("pe", lambda e, o=o, kc=kc, bk=bk, wi=wi: e.matmul(
                            bank[bk][:], lhsT=wring[wi][:, o:o + 128], rhs=hT[:, kc, :], start=(kc == 0), stop=(kc == 7)),
                            reads=(wringB[wi], hTB), writes=(bankB[bk],), mark=(kc == 7))
                    consumer(n, bk)
                    n += 1

        def qk_post(l, bk, gcol, br, out_ap, outB, bk2):
            si = nxt(sqstate, 2)
            bb.op("act", lambda e: e.activation(out=sq[si][:], in_=bank[bk][:], func=AF.Square), reads=(bankB[bk],), writes=(sqB[si],))
            bb.op("dve", lambda e: e.tensor_copy(out=qraw[:], in_=bank[bk][:]), reads=(bankB[bk],), writes=(qrawB,))
            on = onesC_bf if br == 2 else ones_bf
            bb.op("pe", lambda e: e.matmul(bank[bk2][:], lhsT=on[:], rhs=sq[si][:], start=True, stop=True),
                  reads=(sqB[si],) + CONST, writes=(bankB[bk2],))
            rstd_from_bank(bk2, 64.0 if br == 2 else 128.0)
            gap = gvec_s[:, l * 8 + gcol:l * 8 + gcol + 1]
            if br != 0:
                bb.op("dve", lambda e: e.scalar_tensor_tensor(out=out_ap, in0=qraw[:], scalar=gap, in1=rstd[:], op0=ALU.mult, op1=ALU.mult),
                      reads=(qrawB, rstdB) + CONST, writes=(outB,))
                return
            bb.op("dve", lambda e: e.scalar_tensor_tensor(out=qn32[:], in0=qraw[:], scalar=gap, in1=rstd[:], op0=ALU.mult, op1=ALU.mult),
                  reads=(qrawB, rstdB) + CONST, writes=(qn32B,))
            bb.op("act", lambda e: e.activation(out=qnb[:], in_=qn32[:], func=AF.Copy), reads=(qn32B,), writes=(qnbB,))
            bb.op("pe", lambda e: e.matmul(bank[bk2][:], lhsT=perm_bf[:], rhs=qnb[:], start=True, stop=True),
                  reads=(qnbB,) + CONST, writes=(bankB[bk2],))
            t1 = nxt(tstate, NTMP)
            bb.op("pool", lambda e: e.tensor_tensor(out=tmp[t1][:], in0=qn32[:], in1=ropeC[:], op=ALU.mult),
                  reads=(qn32B, ropeB), writes=(tmpB[t1],))
            t2 = nxt(tstate, NTMP)
            bb.op("dve", lambda e: e.tensor_tensor(out=tmp[t2][:], in0=bank[bk2][:], in1=ropeS[:], op=ALU.mult),
                  reads=(bankB[bk2], ropeB), writes=(tmpB[t2],))
            bb.op("dve", lambda e: e.tensor_tensor(out=out_ap, in0=tmp[t1][:], in1=tmp[t2][:], op=ALU.add),
                  reads=(tmpB[t1], tmpB[t2]), writes=(outB,))

        def load_rope(i):
            bb.dma("sp", ropeC[:], ropeT[0, :, i * TT:(i + 1) * TT], writes=(ropeB,))
            bb.dma("sp", ropeS[:], ropeT[1, :, i * TT:(i + 1) * TT], writes=(ropeB,), partial=True)

        def kv_part(l, i):
            seg_s = i < 4
            s = 0 if seg_s else 1
            ti = i % 4
            KT = KT_S_o if seg_s else KT_P_o
            VV = V_S_o if seg_s else V_P_o
            KTb = outB
            VVb = outB
            norm_mod(l, s, 1)
            load_rope(i)
            cols = [n * 128 for n in range(6)]

            def cons(n, bk):
                br = n // 2
                ki = nxt(sqstate, 2) if False else (n % 2)
                qk_post(l, bk, [1, 3, 5][br], br, kst[ki][:], kstB[ki], 6)
                bb.dma("pool", KT[n * 128:(n + 1) * 128, ti * TT:(ti + 1) * TT], kst[ki][:], reads=(kstB[ki],), writes=(KTb,), partial=True)

            if "kvK" not in skip:
                proj_chunks("Bkv", cols, cons, [4, 5])
            if "kvV" in skip:
                return
            wsrc = WSC["Bkv"].rearrange("(kc p) n -> p kc n", p=128)
            wi = wload([(0, 8, 256, wsrc[:, :, 768:1024]), (2048, 8, 256, wsrc[:, :, 1024:1280])], wB["Bkv"])
            wi2 = wload([(0, 8, 256, wsrc[:, :, 1280:1536])], wB["Bkv"])
            for a in range(4):
                b0, b1 = 4 + (a % 2) * 2, 5 + (a % 2) * 2
                for kc in range(8):
                    bb.op("pe", lambda e, a=a, kc=kc, b0=b0: e.matmul(bank[b0][:, 0:256], lhsT=hT[:, kc, a * 128:(a + 1) * 128],
                                                                      rhs=wring[wi][:, kc * 256:(kc + 1) * 256], start=(kc == 0), stop=(kc == 7)),
                          reads=(wringB[wi], hTB), writes=(bankB[b0],), mark=False)
                for kc in range(8):
                    bb.op("pe", lambda e, a=a, kc=kc, b0=b0: e.matmul(bank[b0][:, 256:512], lhsT=hT[:, kc, a * 128:(a + 1) * 128],
                                                                      rhs=wring[wi][:, 2048 + kc * 256:2048 + (kc + 1) * 256], start=(kc == 0), stop=(kc == 7)),
                          reads=(wringB[wi], hTB), writes=(bankB[b0],), mark=(kc == 7))
                for kc in range(8):
                    bb.op("pe", lambda e, a=a, kc=kc, b1=b1: e.matmul(bank[b1][:, 0:256], lhsT=hT[:, kc, a * 128:(a + 1) * 128],
                                                                      rhs=wring[wi2][:, kc * 256:(kc + 1) * 256], start=(kc == 0), stop=(kc == 7)),
                          reads=(wringB[wi2], hTB), writes=(bankB[b1],), mark=(kc == 7))
                vi = a % 2
                bb.op("act", lambda e, vi=vi, b0=b0: e.activation(out=vtok[vi][:, 0:512], in_=bank[b0][:], func=AF.Copy),
                      reads=(bankB[b0],), writes=(vtokB[vi],))
                bb.op("dve", lambda e, vi=vi, b1=b1: e.tensor_copy(out=vtok[vi][:, 512:768], in_=bank[b1][:, 0:256]),
                      reads=(bankB[b1], vtokB[vi]), writes=(vtokB[vi],))
                kt = ti * 4 + a
                dst = VV.rearrange("(g p) c -> p g c", p=128)[:, :, kt * 128:(kt + 1) * 128]
                bb.dma("pool", dst, vtok[vi][:].rearrange("p (g d) -> p g d", g=6), reads=(vtokB[vi],), writes=(VVb,), partial=True)

        def attn_part(L, i):
            seg_s = i < 4
            s = 0 if seg_s else 1
            ti = i % 4
            par = 0
            nblk = 8 if seg_s else 1
            norm_mod(L, s, 1)
            load_rope(i)
            if seg_s:
                t0, t1 = 4 * ti - 1, 4 * ti + 4
                KTl, VVl, off = KT_Sx, V_Sx, 128
            else:
                t0, t1 = max(4 * ti - 1, 0), min(4 * ti + 4, 15)
                KTl, VVl, off = KT_P_i, V_P_i, 0
            ncol = (t1 - t0 + 1) * 128
            for q4 in range(4):
                brg = 2 + q4
                bb.dma("sp", nearK[:, q4, 0:ncol], KTl[brg * 128:(brg + 1) * 128, off + t0 * 128:off + t0 * 128 + ncol], reads=(inB,), writes=(nearKB,), partial=(q4 > 0))
                bb.dma("sp", nearV[:, q4, 0:ncol], VVl[brg * 128:(brg + 1) * 128, off + t0 * 128:off + t0 * 128 + ncol], reads=(inB,), writes=(nearVB,), partial=(q4 > 0))
            if seg_s:
                for h in range(8):
                    ta = nxt(tstate, NTMP)
                    bb.op("pool", lambda e, ta=ta, h=h: e.tensor_scalar(out=tmp[ta][:, 0:128], in0=indS_s[:, ti * 128:(ti + 1) * 128],
                                                                        scalar1=rbrep_s[:, 15 * 16 + 8 + h:15 * 16 + 9 + h], scalar2=None, op0=ALU.mult),
                          reads=CONST, writes=(tmpB[ta],))
                    bb.op("dve", lambda e, ta=ta, h=h: e.scalar_tensor_tensor(out=tmp[ta][:, 128:256], in0=indS_s[:, 512 + ti * 128:512 + (ti + 1) * 128],
                                                                               scalar=rbrep_s[:, 31 * 16 + 8 + h:31 * 16 + 9 + h], in1=tmp[ta][:, 0:128],
                                                                               op0=ALU.mult, op1=ALU.add),
                          reads=CONST + (tmpB[ta],), writes=(tmpB[ta],))
                    bb.op("pool", lambda e, ta=ta, h=h: e.tensor_tensor(out=biasS[:, :, h], in0=tmp[ta][:, 128:256],
                                                                        in1=indS_s[:, 1024 + ti * 128:1024 + (ti + 1) * 128], op=ALU.add),
                          reads=CONST + (tmpB[ta],), writes=(biasSB,))

            def near_src(t, q4):
                if t < t0 or t > t1:
                    return None
                o = (t - t0) * 128
                if 0 <= t <= 15:
                    bias = zero_t[:]
                else:
                    side = 0 if t < 0 else 1
                    bias = hmask_s[:, side:side + 1]
                return nearK[:, q4, o:o + 128], nearV[:, q4, o:o + 128], bias, (nearKB, nearVB)

            for br in range(3):
                qcols = [br * 1024 + h * 128 for h in range(8)]
                gcols = [3072 + br * 1024 + h * 128 for h in range(8)]

                def consq(n, bk):
                    qk_post(L, bk, [0, 2, 4][br], br, scr[:, n, :], scrB, 6)

                def consg(n, bk):
                    ta = nxt(tstate, NTMP)
                    bb.op("act", lambda e: e.activation(out=tmp[ta][:], in_=bank[bk][:], func=AF.Exp, scale=-1.0), reads=(bankB[bk],), writes=(tmpB[ta],))
                    bb.op("pool", lambda e: e.tensor_scalar(out=tmp[ta][:], in0=tmp[ta][:], scalar1=1.0, scalar2=None, op0=ALU.add),
                          reads=(tmpB[ta],), writes=(tmpB[ta],))
                    bb.op("dve", lambda e: e.reciprocal(out=tmp[ta][:], in_=tmp[ta][:]), reads=(tmpB[ta],), writes=(tmpB[ta],))
                    bb.op("pool", lambda e: e.tensor_copy(out=scr[:, 8 + n, :], in_=tmp[ta][:]), reads=(tmpB[ta],), writes=(scrB,))

                proj_chunks("Aq", qcols, consq, [4, 5])
                proj_chunks("Aq", gcols, consg, [4, 5])
                for h in range(8):
                    g = h // 4
                    qTh = scr[:, h, :]
                    sgh = scr[:, 8 + h, :]
                    if br == 0:
                        attn_A(L, par, seg_s, nblk, h, g, qTh, sgh)
                    elif br == 1:
                        attn_B(L, ti, h, g, qTh, sgh, near_src)
                    else:
                        attn_C(L, par, seg_s, nblk, ti, h, g, qTh, sgh, near_src)
            for h in range(8):
                bb.op("act", lambda e, h=h: e.activation(out=hT[:, h, :], in_=big[:, h, :], func=AF.Copy), reads=(bigB,), writes=(hTB,))
            wsrc = WSC["Ao"].rearrange("(kc p) n -> p kc n", p=128)
            for half in range(2):
                wi = wload([(0, 8, 512, wsrc[:, :, half * 512:(half + 1) * 512])], wB["Ao"])
                for mm in range(4):
                    m = half * 4 + mm
                    bk = m % 4
                    for kc in range(8):
                        bb.op("pe", lambda e, kc=kc, mm=mm, bk=bk, wi=wi: e.matmul(
                            bank[bk][:], lhsT=wring[wi][:, kc * 512 + mm * 128:kc * 512 + (mm + 1) * 128], rhs=hT[:, kc, :],
                            start=(kc == 0), stop=(kc == 7)), reads=(wringB[wi], hTB), writes=(bankB[bk],), mark=(kc == 7))
                    resid_update(L, s, 1, m, bk)

        def kv_stream(par, seg_s, nblk, brg, blk):
            ki = nxt(kvstate, 2)
            if seg_s:
                ksrc = KT_all_i[blk * 768 + brg * 128:blk * 768 + (brg + 1) * 128, :]
                vsrc = V_all_i[blk * 768 + brg * 128:blk * 768 + (brg + 1) * 128, :]
            else:
                ksrc = KT_P_i[brg * 128:(brg + 1) * 128, :]
                vsrc = V_P_i[brg * 128:(brg + 1) * 128, :]
            bb.dma("sp", kblk[ki][:], ksrc, reads=(inB,), writes=(kblkB[ki],))
            bb.dma("sp", vblk[ki][:], vsrc, reads=(inB,), writes=(vblkB[ki],))
            return ki

        def finalize(bO, bZ, h, sgh, first, zadd=None, mulvec=None):
            tz = nxt(tstate, NTMP)
            if zadd is not None:
                bb.op("dve", lambda e: e.tensor_scalar(out=tmp[tz][:], in0=bank[bZ][:], scalar1=zadd, scalar2=None, op0=ALU.add),
                      reads=(bankB[bZ], miscB), writes=(tmpB[tz],))
                bb.op("dve", lambda e: e.reciprocal(out=tmp[tz][:], in_=tmp[tz][:]), reads=(tmpB[tz],), writes=(tmpB[tz],))
            else:
                bb.op("dve", lambda e: e.reciprocal(out=tmp[tz][:], in_=bank[bZ][:]), reads=(bankB[bZ],), writes=(tmpB[tz],))
            to = nxt(tstate, NTMP)
            bb.op("dve", lambda e: e.tensor_tensor(out=tmp[to][:], in0=bank[bO][:], in1=tmp[tz][:], op=ALU.mult),
                  reads=(bankB[bO], tmpB[tz]), writes=(tmpB[to],))
            if first:
                bb.op("pool", lambda e: e.tensor_tensor(out=big[:, h, :], in0=tmp[to][:], in1=sgh, op=ALU.mult),
                      reads=(tmpB[to], scrB), writes=(bigB,))
            else:
                bb.op("pool", lambda e: e.tensor_tensor(out=tmp[to][:], in0=tmp[to][:], in1=sgh, op=ALU.mult),
                      reads=(tmpB[to], scrB), writes=(tmpB[to],))
                bb.op("pool", lambda e: e.tensor_tensor(out=big[:, h, :], in0=big[:, h, :], in1=tmp[to][:], op=ALU.add),
                      reads=(tmpB[to], bigB), writes=(bigB,))

        def attn_A(L, par, seg_s, nblk, h, g, qTh, sgh):
            bO, bZ = 2, 3
            nk = nblk * 16
            items = []
            kis = {}
            kis[0] = kv_stream(par, seg_s, nblk, g, 0)
            pend = []
            for kt in range(nk):
                blk, kk = kt // 16, kt % 16
                if kk == 0 and blk + 1 < nblk:
                    kis[blk + 1] = kv_stream(par, seg_s, nblk, g, blk + 1)
                ki = kis[blk]
                bs = kt % 2
                bb.op("pe", lambda e, ki=ki, kk=kk, bs=bs: e.matmul(bank[bs][:], lhsT=kblk[ki][:, kk * 128:(kk + 1) * 128], rhs=qTh, start=True, stop=True),
                      reads=(kblkB[ki], scrB), writes=(bankB[bs],))
                pi = nxt(pstate, NP_)
                bb.op("act", lambda e, bs=bs, pi=pi: e.activation(out=pT[pi][:], in_=bank[bs][:], func=AF.Exp, scale=SC_A),
                      reads=(bankB[bs],), writes=(pTB[pi],))
                pend.append((kt, ki, kk, pi))
                if len(pend) > 1:
                    _pv_A(pend.pop(0), nk, bO, bZ)
            while pend:
                _pv_A(pend.pop(0), nk, bO, bZ)
            finalize(bO, bZ, h, sgh, True)

        def _pv_A(it, nk, bO, bZ):
            kt, ki, kk, pi = it
            bb.op("pe", lambda e: e.matmul(bank[bO][:], lhsT=vblk[ki][:, kk * 128:(kk + 1) * 128], rhs=pT[pi][:], start=(kt == 0), stop=(kt == nk - 1)),
                  reads=(vblkB[ki], pTB[pi]), writes=(bankB[bO],), mark=False)
            bb.op("pe", lambda e: e.matmul(bank[bZ][:], lhsT=ones_bf[:], rhs=pT[pi][:], start=(kt == 0), stop=(kt == nk - 1)),
                  reads=(pTB[pi],) + CONST, writes=(bankB[bZ],), mark=True)

        def attn_B(L, ti, h, g, qTh, sgh, near_src):
            bO, bZ = 2, 3
            q4 = g
            for qb in range(4):
                srcs = []
                for dd in range(3):
                    t = 4 * ti + qb + dd - 1
                    r = near_src(t, q4)
                    if r is not None:
                        srcs.append((dd, r))
                bs = qb % 2
                dd0, dd1 = srcs[0][0], srcs[-1][0]
                for dd, r in srcs:
                    bb.op("pe", lambda e, dd=dd, r=r: e.matmul(bank[bs][:, dd * 128:(dd + 1) * 128], lhsT=r[0], rhs=qTh[:, qb * 128:(qb + 1) * 128],
                                                               start=True, stop=True),
                          reads=r[3] + (scrB,), writes=(bankB[bs],), mark=(dd == dd1))
                ta = nxt(tstate, NTMP)
                c0, c1 = dd0 * 128, (dd1 + 1) * 128
                bb.op("dve", lambda e: e.scalar_tensor_tensor(out=tmp[ta][:, c0:c1], in0=bank[bs][:, c0:c1], scalar=SC_A, in1=MBt[:, h, c0:c1],
                                                              op0=ALU.mult, op1=ALU.add), reads=(bankB[bs],) + CONST, writes=(tmpB[ta],))
                pi = nxt(pstate, NP_)
                for n_, (dd, r) in enumerate(srcs):
                    bb.op("act", lambda e, dd=dd, r=r: e.activation(out=pT[pi][:, dd * 128:(dd + 1) * 128], in_=tmp[ta][:, dd * 128:(dd + 1) * 128],
                                                                    func=AF.Exp, bias=r[2], scale=1.0),
                          reads=(tmpB[ta],) + CONST + ((pTB[pi],) if n_ > 0 else ()), writes=(pTB[pi],))
                for dd, r in srcs:
                    bb.op("pe", lambda e, dd=dd, r=r: e.matmul(bank[bO][:, qb * 128:(qb + 1) * 128], lhsT=r[1], rhs=pT[pi][:, dd * 128:(dd + 1) * 128],
                                                               start=(dd == dd0), stop=(dd == dd1)),
                          reads=r[3] + (pTB[pi],), writes=(bankB[bO],), mark=False)
                    bb.op("pe", lambda e, dd=dd: e.matmul(bank[bZ][:, qb * 128:(qb + 1) * 128], lhsT=ones_bf[:], rhs=pT[pi][:, dd * 128:(dd + 1) * 128],
                                                          start=(dd == dd0), stop=(dd == dd1)),
                          reads=(pTB[pi],) + CONST, writes=(bankB[bZ],), mark=(dd == dd1))
            finalize(bO, bZ, h, sgh, False, zadd=esink[:, h:h + 1])

        def attn_C(L, par, seg_s, nblk, ti, h, g, qTh, sgh, near_src):
            bO = (4, 5)
            bZ = (6, 7)
            brg = 4 + g
            far = []
            for kt in range(nblk * 16):
                if seg_s:
                    far.append(kt)
                elif kt < 4 * ti - 1 or kt > 4 * ti + 4:
                    far.append(kt)
            near = []
            for rel in range(-1, 5):
                r = near_src(4 * ti + rel, 2 + g)
                if r is not None:
                    near.append((rel, r))
            ntot = len(far) + len(near)
            kis = {}
            loaded = set()
            pend = []
            idx = 0

            def pv(it):
                n_, lhsV, rdV, p1, p2 = it
                for m_, pi in enumerate((p1, p2)):
                    bb.op("pe", lambda e, m_=m_, pi=pi: e.matmul(bank[bO[m_]][:], lhsT=lhsV, rhs=pT[pi][:], start=(n_ == 0), stop=(n_ == ntot - 1)),
                          reads=rdV + (pTB[pi],), writes=(bankB[bO[m_]],), mark=False)
                    bb.op("pe", lambda e, m_=m_, pi=pi: e.matmul(bank[bZ[m_]][:], lhsT=ones_bf[:], rhs=pT[pi][:], start=(n_ == 0), stop=(n_ == ntot - 1)),
                          reads=(pTB[pi],) + CONST, writes=(bankB[bZ[m_]],), mark=True)

            if far:
                kis[far[0] // 16] = kv_stream(par, seg_s, nblk, brg, far[0] // 16)
                loaded.add(far[0] // 16)
            for fi, kt in enumerate(far):
                blk, kk = kt // 16, kt % 16
                if blk + 1 < nblk and (blk + 1) not in loaded and any(f // 16 == blk + 1 for f in far):
                    kis[blk + 1] = kv_stream(par, seg_s, nblk, brg, blk + 1)
                    loaded.add(blk + 1)
                ki = kis[blk]
                par2 = idx % 2
                ps_ = []
                for m_ in range(2):
                    bs = par2 * 2 + m_
                    bb.op("pe", lambda e, m_=m_, bs=bs, ki=ki, kk=kk: e.matmul(
                        bank[bs][:], lhsT=kblk[ki][m_ * 64:(m_ + 1) * 64, kk * 128:(kk + 1) * 128], rhs=qTh[m_ * 64:(m_ + 1) * 64, :], start=True, stop=True),
                        reads=(kblkB[ki], scrB), writes=(bankB[bs],))
                    pi = nxt(pstate, NP_)
                    if seg_s:
                        bap = biasS[:, kt, h:h + 1]
                        rd = (biasSB,)
                    else:
                        col = (15 if kt < 4 * ti - 1 else 31) * 16 + 8 + h
                        bap = rbrep_s[:, col:col + 1]
                        rd = CONST
                    bb.op("act", lambda e, bs=bs, pi=pi, bap=bap: e.activation(out=pT[pi][:], in_=bank[bs][:], func=AF.Exp, bias=bap, scale=SC_C),
                          reads=(bankB[bs],) + rd, writes=(pTB[pi],))
                    ps_.append(pi)
                pend.append((idx, vblk[ki][:, kk * 128:(kk + 1) * 128], (vblkB[ki],), ps_[0], ps_[1]))
                idx += 1
                if len(pend) > 1:
                    pv(pend.pop(0))
            for rel, r in near:
                par2 = idx % 2
                ps_ = []
                for m_ in range(2):
                    bs = par2 * 2 + m_
                    bb.op("pe", lambda e, m_=m_, bs=bs, r=r: e.matmul(bank[bs][:], lhsT=r[0][m_ * 64:(m_ + 1) * 64, :], rhs=qTh[m_ * 64:(m_ + 1) * 64, :],
                                                                      start=True, stop=True),
                          reads=r[3] + (scrB,), writes=(bankB[bs],))
                    ta = nxt(tstate, NTMP)
                    for qb in range(4):
                        d = rel - qb
                        if abs(d) <= 1:
                            bb.op("dve", lambda e, qb=qb, d=d, bs=bs, ta=ta: e.scalar_tensor_tensor(
                                out=tmp[ta][:, qb * 128:(qb + 1) * 128], in0=bank[bs][:, qb * 128:(qb + 1) * 128], scalar=SC_C,
                                in1=TCt[:, h, (d + 1) * 128:(d + 2) * 128], op0=ALU.mult, op1=ALU.add),
                                reads=(bankB[bs],) + CONST + ((tmpB[ta],) if qb > 0 else ()), writes=(tmpB[ta],))
                        else:
                            col = (15 if d < 0 else 31) * 16 + 8 + h
                            bb.op("dve", lambda e, qb=qb, col=col, bs=bs, ta=ta: e.tensor_scalar(
                                out=tmp[ta][:, qb * 128:(qb + 1) * 128], in0=bank[bs][:, qb * 128:(qb + 1) * 128], scalar1=SC_C,
                                scalar2=rbrep_s[:, col:col + 1], op0=ALU.mult, op1=ALU.add),
                                reads=(bankB[bs],) + CONST + ((tmpB[ta],) if qb > 0 else ()), writes=(tmpB[ta],))
                    pi = nxt(pstate, NP_)
                    bb.op("act", lambda e, ta=ta, pi=pi, r=r: e.activation(out=pT[pi][:], in_=tmp[ta][:], func=AF.Exp, bias=r[2], scale=1.0),
                          reads=(tmpB[ta],) + CONST, writes=(pTB[pi],))
                    ps_.append(pi)
                pend.append((idx, r[1], r[3], ps_[0], ps_[1]))
                idx += 1
                if len(pend) > 1:
                    pv(pend.pop(0))
            while pend:
                pv(pend.pop(0))
            tz1 = nxt(tstate, NTMP)
            bb.op("dve", lambda e: e.reciprocal(out=tmp[tz1][:], in_=bank[bZ[0]][:]), reads=(bankB[bZ[0]],), writes=(tmpB[tz1],))
            to1 = nxt(tstate, NTMP)
            bb.op("dve", lambda e: e.tensor_tensor(out=tmp[to1][:], in0=bank[bO[0]][:], in1=tmp[tz1][:], op=ALU.mult),
                  reads=(bankB[bO[0]], tmpB[tz1]), writes=(tmpB[to1],))
            tz2 = nxt(tstate, NTMP)
            bb.op("dve", lambda e: e.reciprocal(out=tmp[tz2][:], in_=bank[bZ[1]][:]), reads=(bankB[bZ[1]],), writes=(tmpB[tz2],))
            bb.op("dve", lambda e: e.scalar_tensor_tensor(out=tmp[tz2][:], in0=bank[bO[1]][:], scalar=neglam[:, 0:1], in1=tmp[tz2][:],
                                                          op0=ALU.mult, op1=ALU.mult),
                  reads=(bankB[bO[1]], tmpB[tz2], miscB), writes=(tmpB[tz2],))
            bb.op("dve", lambda e: e.tensor_tensor(out=tmp[to1][:], in0=tmp[to1][:], in1=tmp[tz2][:], op=ALU.add),
                  reads=(tmpB[to1], tmpB[tz2]), writes=(tmpB[to1],))
            si = nxt(sqstate, 2)
            bb.op("act", lambda e: e.activation(out=sq[si][:], in_=tmp[to1][:], func=AF.Square), reads=(tmpB[to1],), writes=(sqB[si],))
            bb.op("pe", lambda e: e.matmul(bank[0][:], lhsT=ones_bf[:], rhs=sq[si][:], start=True, stop=True), reads=(sqB[si],) + CONST, writes=(bankB[0],))
            rstd_from_bank(0, 128.0)
            bb.op("dve", lambda e: e.scalar_tensor_tensor(out=tmp[to1][:], in0=tmp[to1][:], scalar=gvec_s[:, L * 8 + 6:L * 8 + 7], in1=rstd[:],
                                                          op0=ALU.mult, op1=ALU.mult), reads=(tmpB[to1], rstdB) + CONST, writes=(tmpB[to1],))
            bb.op("dve", lambda e: e.scalar_tensor_tensor(out=tmp[to1][:], in0=tmp[to1][:], scalar=laminit_s[:, 1:2], in1=sgh, op0=ALU.mult, op1=ALU.mult),
                  reads=(tmpB[to1], scrB) + CONST, writes=(tmpB[to1],))
            bb.op("pool", lambda e: e.tensor_tensor(out=big[:, h, :], in0=big[:, h, :], in1=tmp[to1][:], op=ALU.add), reads=(tmpB[to1], bigB), writes=(bigB,))

        for i in range(ntiles):
            s = 0 if i < 4 else 1
            if kind == "first":
                xt = big[:].rearrange("p a t -> p (a t)").rearrange("p (a f) -> p a f", a=4)
                bb.dma("sp", xt, xin[i * TT:(i + 1) * TT, :].rearrange("(a p) f -> p a f", p=128), writes=(bigB,))
                for kc in range(8):
                    bk = kc % 4
                    for a in range(4):
                        bb.op("pe", lambda e, kc=kc, a=a, bk=bk: e.transpose(bank[bk][:, a * 128:(a + 1) * 128], xt[:, a, kc * 128:(kc + 1) * 128], ident[:]),
                              reads=(bigB,) + CONST, writes=(bankB[bk],), mark=(a == 3))
                    if kc % 2 == 0:
                        bb.op("act", lambda e, kc=kc, bk=bk: e.activation(out=xT[:, kc, :], in_=bank[bk][:], func=AF.Copy), reads=(bankB[bk],), writes=(xTB,))
                    else:
                        bb.op("dve", lambda e, kc=kc, bk=bk: e.tensor_copy(out=xT[:, kc, :], in_=bank[bk][:]), reads=(bankB[bk], xTB), writes=(xTB,))
            else:
                bb.dma("sp", xT[:], xT_i[i].rearrange("(kc p) t -> p kc t", p=128), reads=(inB,), writes=(xTB,))
            if has_attn:
                attn_part(0, i)
                ffn(0, 1, s, 2)
            if has_kv:
                if 'ffn' not in skip:
                    ffn(1, 0, s, 0)
                if 'kv' not in skip:
                    kv_part(1, i)
                bb.dma("pool", xT_o[i].rearrange("(kc p) t -> p kc t", p=128), xT[:], reads=(xTB,), writes=(outB,), partial=True)
            else:
                yt = big[:].rearrange("p a t -> p (a t)").rearrange("p (a f) -> p a f", a=4)
                for a in range(4):
                    for half in range(2):
                        bk = (a * 2 + half) % 4
                        for kk in range(4):
                            kc = half * 4 + kk
                            bb.op("pe", lambda e, a=a, kk=kk, kc=kc, bk=bk: e.transpose(bank[bk][:, kk * 128:(kk + 1) * 128], xT[:, kc, a * 128:(a + 1) * 128], ident[:]),
                                  reads=(xTB,) + CONST, writes=(bankB[bk],), mark=(kk == 3))
                        if half == 0:
                            bb.op("act", lambda e, a=a, bk=bk: e.activation(out=yt[:, a, 0:512], in_=bank[bk][:], func=AF.Copy), reads=(bankB[bk],), writes=(bigB,))
                        else:
                            bb.op("dve", lambda e, a=a, bk=bk: e.tensor_copy(out=yt[:, a, 512:1024], in_=bank[bk][:]), reads=(bankB[bk], bigB), writes=(bigB,))
                bb.dma("pool", y[i * TT:(i + 1) * TT, :].rearrange("(a p) f -> p a f", p=128), yt, reads=(bigB,), writes=(outB,), partial=True)
        for qk in ("sp", "pool"):
            Q = bb.E[qk]
            for sdm in bb.pools[qk][0]:
                if sdm.val > 0:
                    Q.eng.wait_ge(sdm.h, sdm.val)
    return nc


def _host_consts(core):
    c = {}
    c["ident"] = np.eye(128, dtype=np.float32)
    P = np.zeros((128, 128), np.float32)
    for i in range(128):
        blk = (i % 64) // 32
        j = i + 32 if blk == 0 else i - 32
        P[j, i] = 1.0
    c["perm"] = P
    pos = np.concatenate([core * 2048 + np.arange(2048), np.arange(2048)])
    row = (pos // 64).astype(np.float32)
    col = (pos % 64).astype(np.float32)
    inv = (np.float32(10000.0) ** (-np.arange(32, dtype=np.float32) / np.float32(32))).astype(np.float32)
    ang_r = row[:, None] * inv
    ang_c = col[:, None] * inv
    ang = np.concatenate([ang_r, ang_r, ang_c, ang_c], axis=-1).astype(np.float32)
    cos = np.cos(ang).astype(np.float32).T
    sin = np.sin(ang).astype(np.float32).T
    sign = np.ones((128, 1), np.float32)
    for i in range(128):
        if (i % 64) // 32 == 0:
            sign[i] = -1.0
    c["ropeT"] = np.ascontiguousarray(np.stack([cos, sin * sign]))
    oh = np.zeros((33, 512), np.float32)
    for j in range(511):
        rel = 255 - j
        oh[int(t5_bucket_np(rel)), j] = 1.0
        oh[32, j] = NEGM if abs(rel) > 128 else 0.0
    c["ohrev"] = oh
    indm = np.zeros((4, 128), np.float32); indp = np.zeros((4, 128), np.float32); nearm = np.zeros((4, 128), np.float32)
    for ti in range(4):
        Q0 = core * 16 + 4 * ti
        for kt in range(128):
            if kt < Q0 - 1:
                indm[ti, kt] = 1.0
            elif kt > Q0 + 4:
                indp[ti, kt] = 1.0
            else:
                nearm[ti, kt] = NEGM
    ind = np.concatenate([indm.reshape(-1), indp.reshape(-1), nearm.reshape(-1)])
    c["indS"] = np.ascontiguousarray(np.broadcast_to(ind[None, :], (128, 1536))).astype(np.float32)
    hm = np.array([NEGM if core == 0 else 0.0, NEGM if core == 7 else 0.0], np.float32)
    c["hmask"] = np.ascontiguousarray(np.broadcast_to(hm[None, :], (128, 2))).astype(np.float32)
    return c


def build_mod():
    nc = bass.Bass("TRN2", target_bir_lowering=False)
    cT9 = nc.dram_tensor("cT9", [128, 72], F32, kind="ExternalInput").ap()
    wsl = nc.dram_tensor("wada_sl", [4, D, 1152], F32, kind="ExternalInput").ap()
    bada9 = nc.dram_tensor("bada9", [9, 4 * 1152], F32, kind="ExternalInput").ap()
    modp = nc.dram_tensor("modp", [9, 4 * 1152], F32, kind="ExternalOutput").ap()
    from contextlib import ExitStack
    with ExitStack() as st:
        bb = B(nc, st)
        sbt = lambda n, shp, dt: st.enter_context(nc.sbuf_tensor(n, list(shp), dt))
        c_s = sbt("c_s", [128, 72], F32); sc_s = sbt("sc_s", [128, 72], F32)
        b_s = sbt("b_s", [9, 4 * 1152], F32); o_s = sbt("o_s", [9, 4 * 1152], F32)
        wt = [sbt("wt%d" % i, [128, 8, 512], F32) for i in range(2)]
        wtB = [Buf("wt%d" % i) for i in range(2)]
        ps = [st.enter_context(nc.psum_tensor("ps%d" % i, [128, 512], F32)) for i in range(2)]
        psB = [Buf("ps%d" % i) for i in range(2)]
        cB, oB = Buf("c"), Buf("o")
        bb.dma("sp", c_s[:], cT9, writes=(cB,))
        bb.dma("sp", b_s[:], bada9, writes=(cB,), partial=True)
        bb.op("act", lambda e: e.activation(out=sc_s[:], in_=c_s[:], func=AF.Silu), reads=(cB,), writes=(cB,))
        n = 0
        for l in range(4):
            for (c0, w) in ((0, 512), (512, 512), (1024, 128)):
                k = n % 2
                n += 1
                bb.dma("sp", wt[k][:, :, 0:w], wsl[l].rearrange("(kc p) n -> p kc n", p=128)[:, :, c0:c0 + w], writes=(wtB[k],))
                for kc in range(8):
                    bb.op("pe", lambda e, kc=kc, k=k, w=w: e.matmul(ps[k][0:9, 0:w], lhsT=sc_s[:, kc * 9:(kc + 1) * 9], rhs=wt[k][:, kc, 0:w],
                                                                   start=(kc == 0), stop=(kc == 7)),
                          reads=(wtB[k], cB), writes=(psB[k],), mark=(kc == 7))
                o = l * 1152 + c0
                bb.op("dve", lambda e, k=k, w=w, o=o: e.tensor_tensor(out=o_s[:, o:o + w], in0=ps[k][0:9, 0:w], in1=b_s[:, o:o + w], op=ALU.add),
                      reads=(psB[k], cB), writes=(oB,))
        bb.dma("sp", modp, o_s[:], reads=(oB,), writes=(Buf("out"),))
        for sdm in bb.pools["sp"][0]:
            if sdm.val > 0:
                nc.sync.wait_ge(sdm.h, sdm.val)
    return nc


_NC_CACHE = {}


def _get_nc(kind):
    if kind not in _NC_CACHE:
        _NC_CACHE[kind] = build_mod() if kind == "mod" else build(kind)
    return _NC_CACHE[kind]


def _run(kind, maps):
    res = run_bass_kernel_spmd(_get_nc(kind), maps, core_ids=list(range(NCORE)))
    return res.results


def kernel(**inputs):
    import ml_dtypes
    bf = ml_dtypes.bfloat16
    f = lambda a: np.ascontiguousarray(np.asarray(a, dtype=np.float32))
    inp = {k: f(v) for k, v in inputs.items()}
    xs, xp = inp["x_sample"], inp["x_prompt"]
    cs, cp = inp["c_sample"], inp["c_prompt"]
    c9 = np.concatenate([cs, cp], axis=0)
    cT9 = np.ascontiguousarray(c9.reshape(9, 8, 128).transpose(2, 1, 0).reshape(128, 72))
    maps = []
    for core in range(NCORE):
        sl = slice(core * 1152, (core + 1) * 1152)
        maps.append({"cT9": cT9,
                     "wada_sl": np.ascontiguousarray(inp["w_ada"][:, :, sl]),
                     "bada9": np.ascontiguousarray(np.broadcast_to(inp["b_ada"][:, sl].reshape(1, 4 * 1152), (9, 4 * 1152)))})
    r = _run("mod", maps)
    mod = np.zeros((4, 9, 9 * D), np.float32)
    for core in range(NCORE):
        mp = np.asarray(r[core]["modp"], np.float32).reshape(9, 4, 1152)
        mod[:, :, core * 1152:(core + 1) * 1152] = mp.transpose(1, 0, 2)
    consts = [_host_consts(core) for core in range(NCORE)]
    rb = inp["rel_bias"]
    flag = np.concatenate([np.ones(8, np.float32), np.zeros(8, np.float32)])[None, :]
    rb33 = np.ascontiguousarray(np.concatenate([rb, flag], axis=0))
    rbrep = np.ascontiguousarray(np.broadcast_to(rb.reshape(1, 512), (128, 512)))

    def layer_small(la, lb, core):
        m = {}
        mt = np.zeros((128, 2, 72, 2), np.float32)
        for slot, l in ((0, la), (1, lb)):
            for s_, seq in ((0, 0), (1, 1 + core)):
                mt[:, slot, :, s_] = mod[l, seq].reshape(72, 128).T
        m["modT"] = mt.reshape(128, 288)
        gn = np.zeros((128, 2, 3, 8), np.float32)
        gv = np.zeros((128, 2, 8), np.float32)
        for slot, l in ((0, la), (1, lb)):
            gn[:, slot] = inp["g_norm"][l].reshape(3, 8, 128).transpose(2, 0, 1)
            gv[:, slot, 0] = inp["g_qa"][l]; gv[:, slot, 1] = inp["g_ka"][l]
            gv[:, slot, 2] = inp["g_qb"][l]; gv[:, slot, 3] = inp["g_kb"][l]
            gv[:, slot, 4] = np.tile(inp["g_qc"][l], 2); gv[:, slot, 5] = np.tile(inp["g_kc"][l], 2)
            gv[:, slot, 6] = inp["g_subln"][l]
        m["g_normT"] = gn.reshape(128, 48)
        m["gvec"] = gv.reshape(128, 16)
        m["sinkrep"] = np.ascontiguousarray(np.broadcast_to(inp["sink"][la].reshape(1, 8), (128, 8)))
        lam = np.concatenate([inp["lam_q1"][la], inp["lam_k1"][la], inp["lam_q2"][la], inp["lam_k2"][la]])
        m["lamrep"] = np.ascontiguousarray(np.broadcast_to(lam.reshape(1, 256), (128, 256)))
        li = 0.8 - 0.6 * math.exp(-0.3 * la)
        m["laminit"] = np.ascontiguousarray(np.broadcast_to(np.array([[li, 1.0 - li]], np.float32), (128, 2)))
        m["rb33"] = rb33
        m["rbrep"] = rbrep
        m.update(consts[core])
        return m

    win = inp["w_in"]

    def wA(l):
        q = np.concatenate([win[l][:, QBASE[b]:QBASE[b] + 1024] for b in range(3)] + [win[l][:, GBASE[b]:GBASE[b] + 1024] for b in range(3)], axis=1)
        return {"wAq": np.ascontiguousarray(q), "wAo": inp["w_o"][l], "wAfi": inp["w_ff_in"][l, 1], "wAfo": inp["w_ff_out"][l, 1]}

    def wBk(l):
        kv = np.concatenate([win[l][:, KBASE[b]:KBASE[b] + 256] for b in range(3)] + [win[l][:, VBASE[b]:VBASE[b] + 256] for b in range(3)], axis=1)
        return {"wBkv": np.ascontiguousarray(kv), "wBfi": inp["w_ff_in"][l, 0], "wBfo": inp["w_ff_out"][l, 0]}

    depth = _DEPTH[0]
    state = None
    for p in range(depth + 1):
        kind = "first" if p == 0 else ("last" if p == depth else "mid")
        la, lb = max(p - 1, 0), min(p, 3)
        shared = {}
        if p > 0:
            shared.update(wA(la))
        if p < depth:
            shared.update(wBk(lb))
        if p > 0:
            KT_all = np.ascontiguousarray(np.concatenate([state[c]["KT_S_o"] for c in range(NCORE)], axis=0))
            V_all = np.ascontiguousarray(np.concatenate([state[c]["V_S_o"] for c in range(NCORE)], axis=0))
            shared["KT_all_i"] = KT_all
            shared["V_all_i"] = V_all
        maps = []
        for core in range(NCORE):
            m = dict(shared)
            m.update(layer_small(la, lb, core))
            if p == 0:
                m["xin"] = np.ascontiguousarray(np.concatenate([xs[0, core * 2048:(core + 1) * 2048], xp[core]], axis=0))
            else:
                m["xT_i"] = state[core]["xT_o"]
                m["KT_P_i"] = state[core]["KT_P_o"]
                m["V_P_i"] = state[core]["V_P_o"]
                for nm, key in (("KT_Sx", "KT_S_o"), ("V_Sx", "V_S_o")):
                    ext = np.zeros((768, 18 * 128), bf)
                    ext[:, 128:2176] = state[core][key]
                    if core > 0:
                        ext[:, 0:128] = state[core - 1][key][:, 1920:2048]
                    if core < NCORE - 1:
                        ext[:, 2176:2304] = state[core + 1][key][:, 0:128]
                    m[nm] = ext
            maps.append(m)
        r = _run(kind, maps)
        state = [{k: np.asarray(v) for k, v in r[c].items()} for c in range(NCORE)]
    y_prompt = np.zeros((8, 2048, D), np.float32)
    y_sample = np.zeros((1, 16384, D), np.float32)
    for core in range(NCORE):
        yy = np.asarray(state[core]["y"], dtype=np.float32)
        y_sample[0, core * 2048:(core + 1) * 2048] = yy[:2048]
        y_prompt[core] = yy[2048:]
    return (y_prompt, y_sample)


_DEPTH = [4]
```

```python
import math
import numpy as np
import concourse.bass as bass
import concourse.mybir as mybir
from concourse.bass_utils import run_bass_kernel_spmd

F32 = mybir.dt.float32
BF16 = mybir.dt.bfloat16
AF = mybir.ActivationFunctionType
ALU = mybir.AluOpType

D = 1024
DFF = 2816
NCORE = 8
TT = 512
NT = 8
EPS = 1e-6
NEGM = -30000.0
QBASE = [0, 1536, 3072]
KBASE = [1024, 2560, 4096]
VBASE = [1280, 2816, 4352]
GBASE = [4608, 5632, 6656]
SC_A = 128.0 ** -0.5
SC_C = 64.0 ** -0.5


class Buf:
    __slots__ = ("name", "w", "r")

    def __init__(self, name):
        self.name = name
        self.w = {}
        self.r = {}


class Eng:
    def __init__(self, key, eng, sem):
        self.key = key
        self.eng = eng
        self.sem = sem
        self.cnt = 0
        self.waited = {}


class DSem:
    def __init__(self, key, h):
        self.key = key
        self.h = h
        self.val = 0


class B:
    def __init__(self, nc, stack):
        self.nc = nc
        self.semh = {}
        self.E = {}
        for key, eng in (("pe", nc.tensor), ("act", nc.scalar), ("dve", nc.vector), ("pool", nc.gpsimd)):
            h = stack.enter_context(nc.semaphore("s_" + key))
            self.semh[key] = h
            self.E[key] = Eng(key, eng, h)
        self.E["sp"] = Eng("sp", nc.sync, None)
        self.pools = {}
        for q, n in (("sp", 20), ("pool", 20)):
            lst = []
            for i in range(n):
                k = "d_%s_%d" % (q, i)
                h = stack.enter_context(nc.semaphore(k))
                self.semh[k] = h
                lst.append(DSem(k, h))
            self.pools[q] = [lst, 0]
        k = "cc"
        h = stack.enter_context(nc.semaphore(k))
        self.semh[k] = h
        self.ccsem = DSem(k, h)

    def _deps(self, E, reads, writes, partial):
        deps = {}
        for b in reads:
            for k, v in b.w.items():
                if deps.get(k, 0) < v:
                    deps[k] = v
        for b in writes:
            for k, v in b.r.items():
                if deps.get(k, 0) < v:
                    deps[k] = v
            if not partial:
                for k, v in b.w.items():
                    if deps.get(k, 0) < v:
                        deps[k] = v
        for k, v in deps.items():
            if E.waited.get(k, 0) >= v:
                continue
            if k == E.key and k == "pe":
                continue
            E.eng.wait_ge(self.semh[k], v)
            E.waited[k] = v

    def _upd(self, key, tok, reads, writes, partial):
        for b in reads:
            if b.r.get(key, 0) < tok:
                b.r[key] = tok
        for b in writes:
            if partial:
                if b.w.get(key, 0) < tok:
                    b.w[key] = tok
            else:
                b.w = {key: tok}
                b.r = {}

    def op(self, ek, fn, reads=(), writes=(), mark=True):
        E = self.E[ek]
        pr = tuple(b for b in reads if b.name.startswith("bank") and b not in writes)
        if pr:
            writes = tuple(writes) + pr
        self._deps(E, reads, writes, False)
        ins = fn(E.eng)
        if mark:
            E.cnt += 1
            ins.then_inc(E.sem, 1)
            tok = E.cnt
        else:
            tok = E.cnt + 1
        self._upd(E.key, tok, reads, writes, False)
        return ins

    def dma(self, qk, out, in_, reads=(), writes=(), partial=False):
        Q = self.E[qk]
        pool = self.pools[qk]
        s = pool[0][pool[1]]
        pool[1] = (pool[1] + 1) % len(pool[0])
        if s.val > 0 and Q.waited.get(s.key, 0) < s.val:
            Q.eng.wait_ge(s.h, s.val)
            Q.waited[s.key] = s.val
        self._deps(Q, reads, writes, partial)
        ins = Q.eng.dma_start(out=out, in_=in_)
        s.val += 16
        ins.then_inc(s.h, 16)
        self._upd(s.key, s.val, reads, writes, partial)
        return ins

    def collective(self, fn, reads=(), writes=()):
        Q = self.E["pool"]
        s = self.ccsem
        if s.val > 0 and Q.waited.get(s.key, 0) < s.val:
            Q.eng.wait_ge(s.h, s.val)
            Q.waited[s.key] = s.val
        self._deps(Q, reads, writes, False)
        ins = fn(Q.eng)
        s.val += 16
        ins.then_inc(s.h, 16)
        self._upd(s.key, s.val, reads, writes, False)

    def wait_all(self, ek, bufs):
        E = self.E[ek]
        self._deps(E, bufs, (), False)


def t5_bucket_np(rel):
    rel = np.asarray(rel, np.int64)
    half, max_exact = 16, 8
    ret = np.where(rel > 0, half, 0)
    n = np.abs(rel)
    lg = np.log(np.maximum(n, 1).astype(np.float32) / np.float32(max_exact)).astype(np.float32)
    large = max_exact + (lg / np.float32(math.log(128 / 8)) * np.float32(half - max_exact)).astype(np.int32)
    large = np.minimum(large, half - 1)
    return ret + np.where(n < max_exact, n, large)


_OPTS = {"skip": (), "ntiles": NT}


def build(kind, ntiles=None):
    skip = _OPTS["skip"]
    ntiles = _OPTS["ntiles"] if ntiles is None else ntiles
    from contextlib import ExitStack
    nc = bass.Bass("TRN2", target_bir_lowering=False)

    def din(name, shape, dt=F32):
        return nc.dram_tensor(name, list(shape), dt, kind="ExternalInput").ap()

    def dscr(name, shape, dt):
        return nc.dram_tensor(name, list(shape), dt).ap()

    has_attn = kind in ("mid", "last")
    has_kv = kind in ("first", "mid")
    if kind == "first":
        xin = din("xin", [4096, D])
    else:
        xT_i = din("xT_i", [NT, D, TT])
    if has_kv:
        xT_o = nc.dram_tensor("xT_o", [NT, D, TT], F32, kind="ExternalOutput").ap()
        KT_S_o = nc.dram_tensor("KT_S_o", [768, 2048], BF16, kind="ExternalOutput").ap()
        V_S_o = nc.dram_tensor("V_S_o", [768, 2048], BF16, kind="ExternalOutput").ap()
        KT_P_o = nc.dram_tensor("KT_P_o", [768, 2048], BF16, kind="ExternalOutput").ap()
        V_P_o = nc.dram_tensor("V_P_o", [768, 2048], BF16, kind="ExternalOutput").ap()
    else:
        y = nc.dram_tensor("y", [4096, D], F32, kind="ExternalOutput").ap()
    WIN = {}
    if has_attn:
        WIN["Aq"] = (din("wAq", [D, 6144]), [D, 6144])
        WIN["Ao"] = (din("wAo", [D, D]), [D, D])
        WIN["Afi"] = (din("wAfi", [D, 2 * DFF]), [D, 2 * DFF])
        WIN["Afo"] = (din("wAfo", [DFF, D]), [DFF, D])
        KT_all_i = din("KT_all_i", [8 * 768, 2048], BF16)
        V_all_i = din("V_all_i", [8 * 768, 2048], BF16)
        KT_P_i = din("KT_P_i", [768, 2048], BF16)
        V_P_i = din("V_P_i", [768, 2048], BF16)
        KT_Sx = din("KT_Sx", [768, 18 * 128], BF16)
        V_Sx = din("V_Sx", [768, 18 * 128], BF16)
    if has_kv:
        WIN["Bfi"] = (din("wBfi", [D, 2 * DFF]), [D, 2 * DFF])
        WIN["Bfo"] = (din("wBfo", [DFF, D]), [DFF, D])
        WIN["Bkv"] = (din("wBkv", [D, 1536]), [D, 1536])
    modT_i = din("modT", [128, 2 * 72 * 2])
    g_normT = din("g_normT", [128, 48])
    gvec = din("gvec", [128, 16])
    sinkrep = din("sinkrep", [128, 8])
    lamrep = din("lamrep", [128, 256])
    laminit = din("laminit", [128, 2])
    rb33 = din("rb33", [33, 16])
    rbrep = din("rbrep", [128, 512])
    ohrev = din("ohrev", [33, 512])
    ident_i = din("ident", [128, 128])
    perm_i = din("perm", [128, 128])
    ropeT = din("ropeT", [2, 128, 4096])
    indS = din("indS", [128, 3 * 4 * 128])
    hmask = din("hmask", [128, 2])
    WSC = {k: dscr("w%s_b" % k, shp, BF16) for k, (ap_, shp) in WIN.items()}
    G_d = dscr("G_d", [16, 128, 512], F32)

    with ExitStack() as st:
        bb = B(nc, st)

        def sb(name, shape, dt):
            return st.enter_context(nc.sbuf_tensor(name, list(shape), dt))

        bank = [st.enter_context(nc.psum_tensor("bank%d" % i, [128, 512], F32)) for i in range(8)]
        bankB = [Buf("bank%d" % i) for i in range(8)]

        xT = sb("xT", [128, 8, TT], F32); xTB = Buf("xT")
        hT = sb("hT", [128, 8, TT], BF16); hTB = Buf("hT")
        scr = sb("scr", [128, 22, TT], BF16); scrB = Buf("scr")
        big = sb("big", [128, 8, TT], F32); bigB = Buf("big")
        NW = 3
        wring = [sb("wring%d" % i, [128, 4096], BF16) for i in range(NW)]
        wringB = [Buf("wring%d" % i) for i in range(NW)]
        wstate = [0]
        kvstate = [0]
        kblk = [sb("kblk%d" % i, [128, 2048], BF16) for i in range(2)]
        kblkB = [Buf("kblk%d" % i) for i in range(2)]
        vblk = [sb("vblk%d" % i, [128, 2048], BF16) for i in range(2)]
        vblkB = [Buf("vblk%d" % i) for i in range(2)]
        nearK = sb("nearK", [128, 4, 768], BF16); nearKB = Buf("nearK")
        nearV = sb("nearV", [128, 4, 768], BF16); nearVB = Buf("nearV")
        NP_ = 8
        pT = [sb("pT%d" % i, [128, TT], BF16) for i in range(NP_)]
        pTB = [Buf("pT%d" % i) for i in range(NP_)]
        pstate = [0]
        NTMP = 4
        tmp = [sb("tmp%d" % i, [128, TT], F32) for i in range(NTMP)]
        tmpB = [Buf("tmp%d" % i) for i in range(NTMP)]
        tstate = [0]
        zacc = [[sb("zacc%d_%d" % (m, e), [128, TT], F32) for e in range(2)] for m in range(2)]
        zaccB = [[Buf("zacc%d_%d" % (m, e)) for e in range(2)] for m in range(2)]
        ones_f32 = sb("ones_f32", [128, 128], F32)
        sq = [sb("sq%d" % i, [128, TT], BF16) for i in range(2)]
        sqB = [Buf("sq%d" % i) for i in range(2)]
        sqstate = [0]
        rstd = sb("rstd", [128, TT], F32); rstdB = Buf("rstd")
        qraw = sb("qraw", [128, TT], F32); qrawB = Buf("qraw")
        qn32 = sb("qn32", [128, TT], F32); qn32B = Buf("qn32")
        qnb = sb("qnb", [128, TT], BF16); qnbB = Buf("qnb")
        ropeC = sb("ropeC", [128, TT], F32); ropeS = sb("ropeS", [128, TT], F32); ropeB = Buf("rope")
        MBt = sb("MBt", [128, 8, 384], F32)
        TCt = sb("TCt", [128, 8, 384], F32)
        biasS = sb("biasS", [128, 128, 8], F32); biasSB = Buf("biasS")
        vtok = [sb("vtok%d" % i, [128, 768], BF16) for i in range(2)]
        vtokB = [Buf("vtok%d" % i) for i in range(2)]
        kst = [sb("kst%d" % i, [128, TT], BF16) for i in range(2)]
        kstB = [Buf("kst%d" % i) for i in range(2)]
        constB = Buf("const")
        ident = sb("ident_s", [128, 128], F32)
        ones_bf = sb("ones_bf", [128, 128], BF16)
        onesC_bf = sb("onesC_bf", [128, 128], BF16)
        perm_bf = sb("perm_bf", [128, 128], BF16)
        eps_t = sb("eps_t", [128, 1], F32)
        zero_t = sb("zero_t", [128, 1], F32)
        g_normT_s = sb("g_normT_s", [128, 48], F32)
        gvec_s = sb("gvec_s", [128, 16], F32)
        sink_s = sb("sink_s", [128, 8], F32)
        esink = sb("esink", [128, 8], F32)
        lam_s = sb("lam_s", [128, 256], F32)
        laminit_s = sb("laminit_s", [128, 2], F32)
        neglam = sb("neglam", [128, 1], F32)
        lamtmp = sb("lamtmp", [128, 64], F32)
        lamacc = sb("lamacc", [128, 8], F32)
        rb33_s = sb("rb33_s", [33, 16], F32)
        rbrep_s = sb("rbrep_s", [128, 512], F32)
        ohrev_s = sb("ohrev_s", [33, 512], F32)
        lhsG = sb("lhsG", [33, 128], F32)
        ones33 = sb("ones33", [33, 128], F32)
        indS_s = sb("indS_s", [128, 3 * 512], F32)
        hmask_s = sb("hmask_s", [128, 2], F32)
        modT = sb("modT_s", [128, 2, 72, 2], F32); modB = Buf("modT")
        Avec = sb("Avec", [128, 2 * 2 * 3 * 8], F32)
        Gvec = sb("Gvec", [128, 2 * 2 * 3 * 8], F32)
        miscB = Buf("misc")

        def nxt(state, n):
            i = state[0]
            state[0] = (i + 1) % n
            return i

        wB = {}
        inB = Buf("inputs")
        outB = Buf("outputs")
        GdB = Buf("G_d")

        def ld(dst, src, q="sp"):
            bb.dma(q, dst, src, reads=(), writes=(constB,), partial=True)

        ld(ident[:], ident_i)
        ld(modT[:].rearrange("p a f s -> p (a f s)"), modT_i); ld(g_normT_s[:], g_normT); ld(gvec_s[:], gvec)
        ld(sink_s[:], sinkrep); ld(lam_s[:], lamrep); ld(laminit_s[:], laminit)
        ld(rb33_s[:], rb33); ld(rbrep_s[:], rbrep); ld(ohrev_s[:], ohrev)
        ld(indS_s[:], indS); ld(hmask_s[:], hmask)
        ld(perm_bf[:], perm_i, q="pool")
        c2B = Buf("const2")
        bb.op("dve", lambda e: e.memset(ones_bf[:], 1.0), writes=(c2B,))
        bb.op("dve", lambda e: e.memset(ones_f32[:], 1.0), writes=(c2B,))
        bb.op("dve", lambda e: e.memset(onesC_bf[:], 0.0), writes=(c2B,))
        bb.op("dve", lambda e: e.memset(onesC_bf[0:64, 0:64], 1.0), writes=(c2B,))
        bb.op("dve", lambda e: e.memset(onesC_bf[64:128, 64:128], 1.0), writes=(c2B,))
        bb.op("dve", lambda e: e.memset(eps_t[:], EPS), writes=(c2B,))
        bb.op("dve", lambda e: e.memset(zero_t[:], 0.0), writes=(c2B,))
        bb.op("dve", lambda e: e.memset(ones33[:], 1.0), writes=(c2B,))
        CONST = (constB, c2B)

        order = [k for k in ("Bfi", "Bfo", "Bkv", "Aq", "Ao", "Afi", "Afo") if k in WIN]
        if has_attn:
            order = [k for k in ("Aq", "Ao", "Afi", "Afo", "Bfi", "Bfo", "Bkv") if k in WIN]
        for k in (order if 'cast' not in skip else []):
            src, shp = WIN[k]
            wb_ = wB.setdefault(k, Buf("w" + k))
            r0 = 0
            while r0 < shp[0]:
                r1 = min(r0 + 128, shp[0])
                bb.dma("pool", WSC[k][r0:r1, :], src[r0:r1, :], writes=(wb_,), partial=True)
                r0 = r1

        for l in range(2):
            for s in range(2):
                for j in range(3):
                    o = ((l * 2 + s) * 3 + j) * 8
                    bb.op("dve", lambda e, l=l, s=s, j=j, o=o: e.scalar_tensor_tensor(
                        out=Avec[:, o:o + 8], in0=modT[:, l, (3 * j + 1) * 8:(3 * j + 2) * 8, s], scalar=1.0,
                        in1=g_normT_s[:, (l * 3 + j) * 8:(l * 3 + j) * 8 + 8], op0=ALU.add, op1=ALU.mult),
                        reads=CONST, writes=(miscB,))
                    bb.op("dve", lambda e, l=l, s=s, j=j, o=o: e.tensor_scalar(
                        out=Gvec[:, o:o + 8], in0=modT[:, l, (3 * j + 2) * 8:(3 * j + 3) * 8, s],
                        scalar1=(1.0 if j == 1 else 0.5), scalar2=None, op0=ALU.mult),
                        reads=CONST, writes=(miscB,))
        for m in range(2):
            a0 = 2 * m * 64
            bb.op("dve", lambda e, a0=a0: e.tensor_tensor(out=lamtmp[:], in0=lam_s[:, a0:a0 + 64], in1=lam_s[:, a0 + 64:a0 + 128], op=ALU.mult),
                  reads=CONST + (miscB,), writes=(miscB,))
            bb.op("dve", lambda e, m=m: e.tensor_reduce(out=lamacc[:, m:m + 1], in_=lamtmp[:], axis=mybir.AxisListType.X, op=ALU.add),
                  reads=(miscB,), writes=(miscB,))
        bb.op("act", lambda e: e.activation(out=lamacc[:, 2:4], in_=lamacc[:, 0:2], func=AF.Exp), reads=(miscB,), writes=(miscB,))
        bb.op("dve", lambda e: e.tensor_tensor(out=lamacc[:, 4:5], in0=lamacc[:, 3:4], in1=lamacc[:, 2:3], op=ALU.subtract),
              reads=(miscB,), writes=(miscB,))
        bb.op("dve", lambda e: e.tensor_tensor(out=neglam[:, 0:1], in0=lamacc[:, 4:5], in1=laminit_s[:, 0:1], op=ALU.subtract),
              reads=(miscB,) + CONST, writes=(miscB,))
        bb.op("act", lambda e: e.activation(out=esink[:], in_=sink_s[:], func=AF.Exp), reads=CONST, writes=(miscB,))

        for hh in range(16 if 't5' not in skip else 0):
            bb.op("dve", lambda e, hh=hh: e.tensor_scalar(out=lhsG[:], in0=ones33[:], scalar1=rb33_s[:, hh:hh + 1], scalar2=None, op0=ALU.mult),
                  reads=CONST + (miscB,), writes=(miscB,))
            bi = 2 + hh % 2
            bb.op("pe", lambda e, bi=bi: e.matmul(bank[bi][:], lhsT=lhsG[:], rhs=ohrev_s[:], start=True, stop=True),
                  reads=(miscB,) + CONST, writes=(bankB[bi],))
            ti_ = nxt(tstate, NTMP)
            bb.op("dve", lambda e, bi=bi, ti_=ti_: e.tensor_copy(out=tmp[ti_][:], in_=bank[bi][:]), reads=(bankB[bi],), writes=(tmpB[ti_],))
            bb.dma("pool", G_d[hh], tmp[ti_][:], reads=(tmpB[ti_],), writes=(GdB,), partial=True)
        for hh in range(16 if 't5b' not in skip else 0):
            dst = MBt if hh < 8 else TCt
            for dd in range(3):
                d = dd - 1
                src = bass.AP(G_d.tensor, hh * 128 * 512 + 255 - d * 128, [[511, 128], [1, 128]])
                bb.dma("sp", dst[:, hh % 8, dd * 128:(dd + 1) * 128], src, reads=(GdB,), writes=(constB,), partial=True)

        def sclr(l, s, j, c, which):
            o = ((l * 2 + s) * 3 + j) * 8 + c
            return (Avec if which == "A" else Gvec)[:, o:o + 1]

        def rstd_from_bank(bk, dim):
            bb.op("act", lambda e: e.activation(out=rstd[:], in_=bank[bk][:], func=AF.Ln, bias=eps_t[:], scale=1.0 / dim),
                  reads=(bankB[bk],) + CONST, writes=(rstdB,))
            bb.op("act", lambda e: e.activation(out=rstd[:], in_=rstd[:], func=AF.Exp, scale=-0.5), reads=(rstdB,), writes=(rstdB,))

        def norm_mod(l, s, j, bk=7):
            for kc in range(8):
                si = nxt(sqstate, 2)
                bb.op("act", lambda e, kc=kc, si=si: e.activation(out=sq[si][:], in_=xT[:, kc, :], func=AF.Square),
                      reads=(xTB,), writes=(sqB[si],))
                bb.op("pe", lambda e, kc=kc, si=si: e.matmul(bank[bk][:], lhsT=ones_bf[:], rhs=sq[si][:], start=(kc == 0), stop=(kc == 7)),
                      reads=(sqB[si],) + CONST, writes=(bankB[bk],), mark=True)
            rstd_from_bank(bk, 1024.0)
            for kc in range(8):
                ti_ = nxt(tstate, NTMP)
                bb.op("dve", lambda e, kc=kc, ti_=ti_: e.tensor_tensor(out=tmp[ti_][:], in0=xT[:, kc, :], in1=rstd[:], op=ALU.mult),
                      reads=(xTB, rstdB), writes=(tmpB[ti_],))
                mo = 3 * j * 8 + kc
                bb.op("pool", lambda e, kc=kc, ti_=ti_, mo=mo: e.tensor_scalar(
                    out=hT[:, kc, :], in0=tmp[ti_][:], scalar1=sclr(l, s, j, kc, "A"), scalar2=modT[:, l, mo, s:s + 1],
                    op0=ALU.mult, op1=ALU.add), reads=(tmpB[ti_], miscB) + CONST, writes=(hTB,))

        def wload(srcs, wb):
            wi = nxt(wstate, NW)
            first = True
            for (o, a, b_, src) in srcs:
                dst = wring[wi][:, o:o + a * b_].rearrange("p (a b) -> p a b", a=a)
                bb.dma("sp", dst, src, reads=(wb,), writes=(wringB[wi],), partial=(not first))
                first = False
            return wi

        def resid_update(l, s, j, m, bk):
            bb.op("dve", lambda e: e.scalar_tensor_tensor(out=xT[:, m, :], in0=bank[bk][:], scalar=sclr(l, s, j, m, "G"),
                                                          in1=xT[:, m, :], op0=ALU.mult, op1=ALU.add),
                  reads=(bankB[bk], miscB, xTB), writes=(xTB,))

        def ffn(l, which, s, j):
            norm_mod(l, s, j)
            wk = "A" if l == 0 else "B"
            wsrc = WSC[wk + "fi"].rearrange("(kc p) n -> p kc n", p=128)
            wb = wB[wk + "fi"]
            groups = [(g0, min(2, 22 - g0)) for g0 in range(0, 22, 2)]
            loads = []

            def issue(gi):
                g0, n = groups[gi]
                return wload([(0, 8, n * 128, wsrc[:, :, g0 * 128:(g0 + n) * 128]),
                              (2048, 8, n * 128, wsrc[:, :, DFF + g0 * 128:DFF + (g0 + n) * 128])], wb)

            PF = 2
            for gi in range(min(PF, len(groups))):
                loads.append(issue(gi))
            for gi, (g0, n) in enumerate(groups):
                if gi + PF < len(groups):
                    loads.append(issue(gi + PF))
                wi = loads[gi]
                for jj in range(n):
                    jp = g0 + jj
                    bg, bu = (jp % 3) * 2, (jp % 3) * 2 + 1
                    for part, bk in ((0, bg), (1, bu)):
                        for kc in range(8):
                            o = part * 2048 + kc * n * 128 + jj * 128
                            bb.op("pe", lambda e, o=o, kc=kc, bk=bk, wi=wi: e.matmul(
                                bank[bk][:], lhsT=wring[wi][:, o:o + 128], rhs=hT[:, kc, :], start=(kc == 0), stop=(kc == 7)),
                                reads=(wringB[wi], hTB), writes=(bankB[bk],), mark=(kc == 7))
                    ti_ = nxt(tstate, NTMP)
                    bb.op("act", lambda e, bg=bg, ti_=ti_: e.activation(out=tmp[ti_][:], in_=bank[bg][:], func=AF.Silu),
                          reads=(bankB[bg],), writes=(tmpB[ti_],))
                    bb.op("dve", lambda e, bu=bu, ti_=ti_, jp=jp: e.tensor_tensor(out=scr[:, jp, :], in0=tmp[ti_][:], in1=bank[bu][:], op=ALU.mult),
                          reads=(tmpB[ti_], bankB[bu]), writes=(scrB,))
            wsrc2 = WSC[wk + "fo"].rearrange("(kc p) n -> p kc n", p=128)
            wb2 = wB[wk + "fo"]
            og = [(k0, min(4, 22 - k0)) for k0 in range(0, 22, 4)]
            loads = []

            def issue2(gi):
                k0, n = og[gi]
                return wload([(0, n, 1024, wsrc2[:, k0:k0 + n, :])], wb2)

            for gi in range(min(PF, len(og))):
                loads.append(issue2(gi))
            for gi, (k0, n) in enumerate(og):
                if gi + PF < len(og):
                    loads.append(issue2(gi + PF))
                wi = loads[gi]
                for kk in range(n):
                    kc = k0 + kk
                    for m in range(8):
                        bb.op("pe", lambda e, kk=kk, kc=kc, m=m, wi=wi: e.matmul(
                            bank[m][:], lhsT=wring[wi][:, kk * 1024 + m * 128:kk * 1024 + (m + 1) * 128], rhs=scr[:, kc, :],
                            start=(kc == 0), stop=(kc == 21)),
                            reads=(wringB[wi], scrB), writes=(bankB[m],), mark=(kc == 21 or (kk == n - 1 and m == 7)))
            for m in range(8):
                resid_update(l, s, j, m, m)

        def proj_chunks(wkey, cols, consumer, bks):
            wsrc = WSC[wkey].rearrange("(kc p) n -> p kc n", p=128)
            wb = wB[wkey]
            groups = [cols[i:i + 4] for i in range(0, len(cols), 4)]
            loads = []

            def issue(gi):
                g = groups[gi]
                srcs = []
                for ci, c0 in enumerate(g):
                    srcs.append((ci * 1024, 8, 128, wsrc[:, :, c0:c0 + 128]))
                return wload(srcs, wb)

            PF = 2
            for gi in range(min(PF, len(groups))):
                loads.append(issue(gi))
            n = 0
            for gi, g in enumerate(groups):
                if gi + PF < len(groups):
                    loads.append(issue(gi + PF))
                wi = loads[gi]
                for ci, c0 in enumerate(g):
                    bk = bks[n % len(bks)]
                    for kc in range(8):
                        o = ci * 1024 + kc * 128
                        bb.op("pe", lambda e, o=o, kc=kc, bk=bk, wi=wi: e.matmul(
                            bank[bk][:], lhsT=wring[wi][:, o:o + 128], rhs=hT[:, kc, :], start=(kc == 0), stop=(kc == 7)),
                            reads=(wringB[wi], hTB), writes=(bankB[bk],), mark=(kc == 7))
                    consumer(n, bk)
                    n += 1

        def qk_post(l, bk, gcol, br, out_ap, outB, bk2):
            si = nxt(sqstate, 2)
            bb.op("act", lambda e: e.activation(out=sq[si][:], in_=bank[bk][:], func=AF.Square), reads=(bankB[bk],), writes=(sqB[si],))
            bb.op("dve", lambda e: e.tensor_copy(out=qraw[:], in_=bank[bk][:]), reads=(bankB[bk],), writes=(qrawB,))
            on = onesC_bf if br == 2 else ones_bf
            bb.op("pe", lambda e: e.matmul(bank[bk2][:], lhsT=on[:], rhs=sq[si][:], start=True, stop=True),
                  reads=(sqB[si],) + CONST, writes=(bankB[bk2],))
            rstd_from_bank(bk2, 64.0 if br == 2 else 128.0)
            gap = gvec_s[:, l * 8 + gcol:l * 8 + gcol + 1]
            if br != 0:
                bb.op("dve", lambda e: e.scalar_tensor_tensor(out=out_ap, in0=qraw[:], scalar=gap, in1=rstd[:], op0=ALU.mult, op1=ALU.mult),
                      reads=(qrawB, rstdB) + CONST, writes=(outB,))
                return
            bb.op("dve", lambda e: e.scalar_tensor_tensor(out=qn32[:], in0=qraw[:], scalar=gap, in1=rstd[:], op0=ALU.mult, op1=ALU.mult),
                  reads=(qrawB, rstdB) + CONST, writes=(qn32B,))
            bb.op("act", lambda e: e.activation(out=qnb[:], in_=qn32[:], func=AF.Copy), reads=(qn32B,), writes=(qnbB,))
            bb.op("pe", lambda e: e.matmul(bank[bk2][:], lhsT=perm_bf[:], rhs=qnb[:], start=True, stop=True),
                  reads=(qnbB,) + CONST, writes=(bankB[bk2],))
            t1 = nxt(tstate, NTMP)
            bb.op("pool", lambda e: e.tensor_tensor(out=tmp[t1][:], in0=qn32[:], in1=ropeC[:], op=ALU.mult),
                  reads=(qn32B, ropeB), writes=(tmpB[t1],))
            t2 = nxt(tstate, NTMP)
            bb.op("dve", lambda e: e.tensor_tensor(out=tmp[t2][:], in0=bank[bk2][:], in1=ropeS[:], op=ALU.mult),
                  reads=(bankB[bk2], ropeB), writes=(tmpB[t2],))
            bb.op("dve", lambda e: e.tensor_tensor(out=out_ap, in0=tmp[t1][:], in1=tmp[t2][:], op=ALU.add),
                  reads=(tmpB[t1], tmpB[t2]), writes=(outB,))

        def load_rope(i):
            bb.dma("sp", ropeC[:], ropeT[0, :, i * TT:(i + 1) * TT], writes=(ropeB,))
            bb.dma("sp", ropeS[:], ropeT[1, :, i * TT:(i + 1) * TT], writes=(ropeB,), partial=True)

        def kv_part(l, i):
            seg_s = i < 4
            s = 0 if seg_s else 1
            ti = i % 4
            KT = KT_S_o if seg_s else KT_P_o
            VV = V_S_o if seg_s else V_P_o
            KTb = outB
            VVb = outB
            norm_mod(l, s, 1)
            load_rope(i)
            cols = [n * 128 for n in range(6)]

            def cons(n, bk):
                br = n // 2
                ki = nxt(sqstate, 2) if False else (n % 2)
                qk_post(l, bk, [1, 3, 5][br], br, kst[ki][:], kstB[ki], 6)
                bb.dma("pool", KT[n * 128:(n + 1) * 128, ti * TT:(ti + 1) * TT], kst[ki][:], reads=(kstB[ki],), writes=(KTb,), partial=True)

            if "kvK" not in skip:
                proj_chunks("Bkv", cols, cons, [4, 5])
            if "kvV" in skip:
                return
            wsrc = WSC["Bkv"].rearrange("(kc p) n -> p kc n", p=128)
            wi = wload([(0, 8, 256, wsrc[:, :, 768:1024]), (2048, 8, 256, wsrc[:, :, 1024:1280])], wB["Bkv"])
            wi2 = wload([(0, 8, 256, wsrc[:, :, 1280:1536])], wB["Bkv"])
            for a in range(4):
                b0, b1 = 4 + (a % 2) * 2, 5 + (a % 2) * 2
                for kc in range(8):
                    bb.op("pe", lambda e, a=a, kc=kc, b0=b0: e.matmul(bank[b0][:, 0:256], lhsT=hT[:, kc, a * 128:(a + 1) * 128],
                                                                      rhs=wring[wi][:, kc * 256:(kc + 1) * 256], start=(kc == 0), stop=(kc == 7)),
                          reads=(wringB[wi], hTB), writes=(bankB[b0],), mark=False)
                for kc in range(8):
                    bb.op("pe", lambda e, a=a, kc=kc, b0=b0: e.matmul(bank[b0][:, 256:512], lhsT=hT[:, kc, a * 128:(a + 1) * 128],
                                                                      rhs=wring[wi][:, 2048 + kc * 256:2048 + (kc + 1) * 256], start=(kc == 0), stop=(kc == 7)),
                          reads=(wringB[wi], hTB), writes=(bankB[b0],), mark=(kc == 7))
                for kc in range(8):
                    bb.op("pe", lambda e, a=a, kc=kc, b1=b1: e.matmul(bank[b1][:, 0:256], lhsT=hT[:, kc, a * 128:(a + 1) * 128],
                                                                      rhs=wring[wi2][:, kc * 256:(kc + 1) * 256], start=(kc == 0), stop=(kc == 7)),
                          reads=(wringB[wi2], hTB), writes=(bankB[b1],), mark=(kc == 7))
                vi = a % 2
                bb.op("act", lambda e, vi=vi, b0=b0: e.activation(out=vtok[vi][:, 0:512], in_=bank[b0][:], func=AF.Copy),
                      reads=(bankB[b0],), writes=(vtokB[vi],))
                bb.op("dve", lambda e, vi=vi, b1=b1: e.tensor_copy(out=vtok[vi][:, 512:768], in_=bank[b1][:, 0:256]),
                      reads=(bankB[b1], vtokB[vi]), writes=(vtokB[vi],))
                kt = ti * 4 + a
                dst = VV.rearrange("(g p) c -> p g c", p=128)[:, :, kt * 128:(kt + 1) * 128]
                bb.dma("pool", dst, vtok[vi][:].rearrange("p (g d) -> p g d", g=6), reads=(vtokB[vi],), writes=(VVb,), partial=True)

        def attn_part(L, i):
            seg_s = i < 4
            s = 0 if seg_s else 1
            ti = i % 4
            par = 0
            nblk = 8 if seg_s else 1
            norm_mod(L, s, 1)
            load_rope(i)
            if seg_s:
                t0, t1 = 4 * ti - 1, 4 * ti + 4
                KTl, VVl, off = KT_Sx, V_Sx, 128
            else:
                t0, t1 = max(4 * ti - 1, 0), min(4 * ti + 4, 15)
                KTl, VVl, off = KT_P_i, V_P_i, 0
            ncol = (t1 - t0 + 1) * 128
            for q4 in range(4):
                brg = 2 + q4
                bb.dma("sp", nearK[:, q4, 0:ncol], KTl[brg * 128:(brg + 1) * 128, off + t0 * 128:off + t0 * 128 + ncol], reads=(inB,), writes=(nearKB,), partial=(q4 > 0))
                bb.dma("sp", nearV[:, q4, 0:ncol], VVl[brg * 128:(brg + 1) * 128, off + t0 * 128:off + t0 * 128 + ncol], reads=(inB,), writes=(nearVB,), partial=(q4 > 0))
            if seg_s:
                for h in range(8):
                    ta = nxt(tstate, NTMP)
                    bb.op("pool", lambda e, ta=ta, h=h: e.tensor_scalar(out=tmp[ta][:, 0:128], in0=indS_s[:, ti * 128:(ti + 1) * 128],
                                                                        scalar1=rbrep_s[:, 15 * 16 + 8 + h:15 * 16 + 9 + h], scalar2=None, op0=ALU.mult),
                          reads=CONST, writes=(tmpB[ta],))
                    bb.op("dve", lambda e, ta=ta, h=h: e.scalar_tensor_tensor(out=tmp[ta][:, 128:256], in0=indS_s[:, 512 + ti * 128:512 + (ti + 1) * 128],
                                                                               scalar=rbrep_s[:, 31 * 16 + 8 + h:31 * 16 + 9 + h], in1=tmp[ta][:, 0:128],
                                                                               op0=ALU.mult, op1=ALU.add),
                          reads=CONST + (tmpB[ta],), writes=(tmpB[ta],))
                    bb.op("pool", lambda e, ta=ta, h=h: e.tensor_tensor(out=biasS[:, :, h], in0=tmp[ta][:, 128:256],
                                                                        in1=indS_s[:, 1024 + ti * 128:1024 + (ti + 1) * 128], op=ALU.add),
                          reads=CONST + (tmpB[ta],), writes=(biasSB,))

            def near_src(t, q4):
                if t < t0 or t > t1:
                    return None
                o = (t - t0) * 128
                if 0 <= t <= 15:
                    bias = zero_t[:]
                else:
                    side = 0 if t < 0 else 1
                    bias = hmask_s[:, side:side + 1]
                return nearK[:, q4, o:o + 128], nearV[:, q4, o:o + 128], bias, (nearKB, nearVB)

            for br in range(3):
                qcols = [br * 1024 + h * 128 for h in range(8)]
                gcols = [3072 + br * 1024 + h * 128 for h in range(8)]

                def consq(n, bk):
                    qk_post(L, bk, [0, 2, 4][br], br, scr[:, n, :], scrB, 6)

                def consg(n, bk):
                    ta = nxt(tstate, NTMP)
                    bb.op("act", lambda e: e.activation(out=tmp[ta][:], in_=bank[bk][:], func=AF.Exp, scale=-1.0), reads=(bankB[bk],), writes=(tmpB[ta],))
                    bb.op("pool", lambda e: e.tensor_scalar(out=tmp[ta][:], in0=tmp[ta][:], scalar1=1.0, scalar2=None, op0=ALU.add),
                          reads=(tmpB[ta],), writes=(tmpB[ta],))
                    bb.op("dve", lambda e: e.reciprocal(out=tmp[ta][:], in_=tmp[ta][:]), reads=(tmpB[ta],), writes=(tmpB[ta],))
                    bb.op("pool", lambda e: e.tensor_copy(out=scr[:, 8 + n, :], in_=tmp[ta][:]), reads=(tmpB[ta],), writes=(scrB,))

                proj_chunks("Aq", qcols, consq, [4, 5])
                proj_chunks("Aq", gcols, consg, [4, 5])
                for h in range(8):
                    g = h // 4
                    qTh = scr[:, h, :]
                    sgh = scr[:, 8 + h, :]
                    if br == 0:
                        attn_A(L, par, seg_s, nblk, h, g, qTh, sgh)
                    elif br == 1:
                        attn_B(L, ti, h, g, qTh, sgh, near_src)
                    else:
                        attn_C(L, par, seg_s, nblk, ti, h, g, qTh, sgh, near_src)
            for h in range(8):
                bb.op("act", lambda e, h=h: e.activation(out=hT[:, h, :], in_=big[:, h, :], func=AF.Copy), reads=(bigB,), writes=(hTB,))
            wsrc = WSC["Ao"].rearrange("(kc p) n -> p kc n", p=128)
            for half in range(2):
                wi = wload([(0, 8, 512, wsrc[:, :, half * 512:(half + 1) * 512])], wB["Ao"])
                for mm in range(4):
                    m = half * 4 + mm
                    bk = m % 4
                    for kc in range(8):
                        bb.op("pe", lambda e, kc=kc, mm=mm, bk=bk, wi=wi: e.matmul(
                            bank[bk][:], lhsT=wring[wi][:, kc * 512 + mm * 128:kc * 512 + (mm + 1) * 128], rhs=hT[:, kc, :],
                            start=(kc == 0), stop=(kc == 7)), reads=(wringB[wi], hTB), writes=(bankB[bk],), mark=(kc == 7))
                    resid_update(L, s, 1, m, bk)

        def kv_stream(par, seg_s, nblk, brg, blk):
            ki = nxt(kvstate, 2)
            if seg_s:
                ksrc = KT_all_i[blk * 768 + brg * 128:blk * 768 + (brg + 1) * 128, :]
                vsrc = V_all_i[blk * 768 + brg * 128:blk * 768 + (brg + 1) * 128, :]
            else:
                ksrc = KT_P_i[brg * 128:(brg + 1) * 128, :]
                vsrc = V_P_i[brg * 128:(brg + 1) * 128, :]
            bb.dma("sp", kblk[ki][:], ksrc, reads=(inB,), writes=(kblkB[ki],))
            bb.dma("sp", vblk[ki][:], vsrc, reads=(inB,), writes=(vblkB[ki],))
            return ki

        def finalize(bO, bZ, h, sgh, first, zadd=None, mulvec=None):
            tz = nxt(tstate, NTMP)
            if zadd is not None:
                bb.op("dve", lambda e: e.tensor_scalar(out=tmp[tz][:], in0=bank[bZ][:], scalar1=zadd, scalar2=None, op0=ALU.add),
                      reads=(bankB[bZ], miscB), writes=(tmpB[tz],))
                bb.op("dve", lambda e: e.reciprocal(out=tmp[tz][:], in_=tmp[tz][:]), reads=(tmpB[tz],), writes=(tmpB[tz],))
            else:
                bb.op("dve", lambda e: e.reciprocal(out=tmp[tz][:], in_=bank[bZ][:]), reads=(bankB[bZ],), writes=(tmpB[tz],))
            to = nxt(tstate, NTMP)
            bb.op("dve", lambda e: e.tensor_tensor(out=tmp[to][:], in0=bank[bO][:], in1=tmp[tz][:], op=ALU.mult),
                  reads=(bankB[bO], tmpB[tz]), writes=(tmpB[to],))
            if first:
                bb.op("pool", lambda e: e.tensor_tensor(out=big[:, h, :], in0=tmp[to][:], in1=sgh, op=ALU.mult),
                      reads=(tmpB[to], scrB), writes=(bigB,))
            else:
                bb.op("pool", lambda e: e.tensor_tensor(out=tmp[to][:], in0=tmp[to][:], in1=sgh, op=ALU.mult),
                      reads=(tmpB[to], scrB), writes=(tmpB[to],))
                bb.op("pool", lambda e: e.tensor_tensor(out=big[:, h, :], in0=big[:, h, :], in1=tmp[to][:], op=ALU.add),
                      reads=(tmpB[to], bigB), writes=(bigB,))

        def zacc_add(m, n_, pi):
            e = n_ % 2
            ek = "dve" if e == 0 else "pool"
            if n_ < 2:
                bb.op(ek, lambda en: en.tensor_copy(out=zacc[m][e][:], in_=pT[pi][:]), reads=(pTB[pi],), writes=(zaccB[m][e],))
            else:
                bb.op(ek, lambda en: en.tensor_tensor(out=zacc[m][e][:], in0=zacc[m][e][:], in1=pT[pi][:], op=ALU.add),
                      reads=(pTB[pi], zaccB[m][e]), writes=(zaccB[m][e],))

        def z_total(m, bk, nitems):
            if nitems >= 2:
                bb.op("dve", lambda en: en.tensor_tensor(out=zacc[m][0][:], in0=zacc[m][0][:], in1=zacc[m][1][:], op=ALU.add),
                      reads=(zaccB[m][0], zaccB[m][1]), writes=(zaccB[m][0],))
            bb.op("pe", lambda en: en.matmul(bank[bk][:], lhsT=ones_f32[:], rhs=zacc[m][0][:], start=True, stop=True),
                  reads=(zaccB[m][0],) + CONST, writes=(bankB[bk],))

        def attn_A(L, par, seg_s, nblk, h, g, qTh, sgh):
            bO, bZ = 2, 3
            SB = (0, 1, 4, 5)
            LA = 3
            nk = nblk * 16
            kis = {}
            kis[0] = kv_stream(par, seg_s, nblk, g, 0)
            pend = []
            for kt in range(nk):
                blk, kk = kt // 16, kt % 16
                if kk == LA and blk + 1 < nblk:
                    kis[blk + 1] = kv_stream(par, seg_s, nblk, g, blk + 1)
                ki = kis[blk]
                bs = SB[kt % 4]
                bb.op("pe", lambda e, ki=ki, kk=kk, bs=bs: e.matmul(bank[bs][:], lhsT=kblk[ki][:, kk * 128:(kk + 1) * 128], rhs=qTh, start=True, stop=True),
                      reads=(kblkB[ki], scrB), writes=(bankB[bs],))
                pi = nxt(pstate, NP_)
                bb.op("act", lambda e, bs=bs, pi=pi: e.activation(out=pT[pi][:], in_=bank[bs][:], func=AF.Exp, scale=SC_A),
                      reads=(bankB[bs],), writes=(pTB[pi],))
                pend.append((kt, ki, kk, pi))
                if len(pend) > LA:
                    _pv_A(pend.pop(0), nk, bO)
            while pend:
                _pv_A(pend.pop(0), nk, bO)
            z_total(0, bZ, nk)
            finalize(bO, bZ, h, sgh, True)

        def _pv_A(it, nk, bO):
            kt, ki, kk, pi = it
            bb.op("pe", lambda e: e.matmul(bank[bO][:], lhsT=vblk[ki][:, kk * 128:(kk + 1) * 128], rhs=pT[pi][:], start=(kt == 0), stop=(kt == nk - 1)),
                  reads=(vblkB[ki], pTB[pi]), writes=(bankB[bO],), mark=True)
            zacc_add(0, kt, pi)

        def attn_B(L, ti, h, g, qTh, sgh, near_src):
            bO, bZ = 2, 3
            q4 = g
            for qb in range(4):
                srcs = []
                for dd in range(3):
                    t = 4 * ti + qb + dd - 1
                    r = near_src(t, q4)
                    if r is not None:
                        srcs.append((dd, r))
                bs = qb % 2
                dd0, dd1 = srcs[0][0], srcs[-1][0]
                for dd, r in srcs:
                    bb.op("pe", lambda e, dd=dd, r=r: e.matmul(bank[bs][:, dd * 128:(dd + 1) * 128], lhsT=r[0], rhs=qTh[:, qb * 128:(qb + 1) * 128],
                                                               start=True, stop=True),
                          reads=r[3] + (scrB,), writes=(bankB[bs],), mark=(dd == dd1))
                ta = nxt(tstate, NTMP)
                c0, c1 = dd0 * 128, (dd1 + 1) * 128
                bb.op("dve", lambda e: e.scalar_tensor_tensor(out=tmp[ta][:, c0:c1], in0=bank[bs][:, c0:c1], scalar=SC_A, in1=MBt[:, h, c0:c1],
                                                              op0=ALU.mult, op1=ALU.add), reads=(bankB[bs],) + CONST, writes=(tmpB[ta],))
                pi = nxt(pstate, NP_)
                for n_, (dd, r) in enumerate(srcs):
                    bb.op("act", lambda e, dd=dd, r=r: e.activation(out=pT[pi][:, dd * 128:(dd + 1) * 128], in_=tmp[ta][:, dd * 128:(dd + 1) * 128],
                                                                    func=AF.Exp, bias=r[2], scale=1.0),
                          reads=(tmpB[ta],) + CONST + ((pTB[pi],) if n_ > 0 else ()), writes=(pTB[pi],))
                for dd, r in srcs:
                    bb.op("pe", lambda e, dd=dd, r=r: e.matmul(bank[bO][:, qb * 128:(qb + 1) * 128], lhsT=r[1], rhs=pT[pi][:, dd * 128:(dd + 1) * 128],
                                                               start=(dd == dd0), stop=(dd == dd1)),
                          reads=r[3] + (pTB[pi],), writes=(bankB[bO],), mark=False)
                    bb.op("pe", lambda e, dd=dd: e.matmul(bank[bZ][:, qb * 128:(qb + 1) * 128], lhsT=ones_bf[:], rhs=pT[pi][:, dd * 128:(dd + 1) * 128],
                                                          start=(dd == dd0), stop=(dd == dd1)),
                          reads=(pTB[pi],) + CONST, writes=(bankB[bZ],), mark=(dd == dd1))
            finalize(bO, bZ, h, sgh, False, zadd=esink[:, h:h + 1])

        def attn_C(L, par, seg_s, nblk, ti, h, g, qTh, sgh, near_src):
            bO = (4, 5)
            bZ = (0, 1)
            SETS = ((0, 1), (2, 3), (6, 7))
            LA = 2
            brg = 4 + g
            far = []
            for kt in range(nblk * 16):
                if seg_s:
                    far.append(kt)
                elif kt < 4 * ti - 1 or kt > 4 * ti + 4:
                    far.append(kt)
            near = []
            for rel in range(-1, 5):
                r = near_src(4 * ti + rel, 2 + g)
                if r is not None:
                    near.append((rel, r))
            ntot = len(far) + len(near)
            kis = {}
            loaded = set()
            pend = []
            idx = 0

            def pv(it):
                n_, lhsV, rdV, p1, p2 = it
                for m_, pi in enumerate((p1, p2)):
                    bb.op("pe", lambda e, m_=m_, pi=pi: e.matmul(bank[bO[m_]][:], lhsT=lhsV, rhs=pT[pi][:], start=(n_ == 0), stop=(n_ == ntot - 1)),
                          reads=rdV + (pTB[pi],), writes=(bankB[bO[m_]],), mark=True)
                    zacc_add(m_, n_, pi)

            if far:
                kis[far[0] // 16] = kv_stream(par, seg_s, nblk, brg, far[0] // 16)
                loaded.add(far[0] // 16)
            for fi, kt in enumerate(far):
                blk, kk = kt // 16, kt % 16
                if kk >= LA and blk + 1 < nblk and (blk + 1) not in loaded and any(f // 16 == blk + 1 for f in far):
                    kis[blk + 1] = kv_stream(par, seg_s, nblk, brg, blk + 1)
                    loaded.add(blk + 1)
                ki = kis[blk]
                ps_ = []
                for m_ in range(2):
                    bs = SETS[idx % 3][m_]
                    bb.op("pe", lambda e, m_=m_, bs=bs, ki=ki, kk=kk: e.matmul(
                        bank[bs][:], lhsT=kblk[ki][m_ * 64:(m_ + 1) * 64, kk * 128:(kk + 1) * 128], rhs=qTh[m_ * 64:(m_ + 1) * 64, :], start=True, stop=True),
                        reads=(kblkB[ki], scrB), writes=(bankB[bs],))
                    pi = nxt(pstate, NP_)
                    if seg_s:
                        bap = biasS[:, kt, h:h + 1]
                        rd = (biasSB,)
                    else:
                        col = (15 if kt < 4 * ti - 1 else 31) * 16 + 8 + h
                        bap = rbrep_s[:, col:col + 1]
                        rd = CONST
                    bb.op("act", lambda e, bs=bs, pi=pi, bap=bap: e.activation(out=pT[pi][:], in_=bank[bs][:], func=AF.Exp, bias=bap, scale=SC_C),
                          reads=(bankB[bs],) + rd, writes=(pTB[pi],))
                    ps_.append(pi)
                pend.append((idx, vblk[ki][:, kk * 128:(kk + 1) * 128], (vblkB[ki],), ps_[0], ps_[1]))
                idx += 1
                if len(pend) > LA:
                    pv(pend.pop(0))
            for rel, r in near:
                ps_ = []
                for m_ in range(2):
                    bs = SETS[idx % 3][m_]
                    bb.op("pe", lambda e, m_=m_, bs=bs, r=r: e.matmul(bank[bs][:], lhsT=r[0][m_ * 64:(m_ + 1) * 64, :], rhs=qTh[m_ * 64:(m_ + 1) * 64, :],
                                                                      start=True, stop=True),
                          reads=r[3] + (scrB,), writes=(bankB[bs],))
                    ta = nxt(tstate, NTMP)
                    for qb in range(4):
                        d = rel - qb
                        if abs(d) <= 1:
                            bb.op("dve", lambda e, qb=qb, d=d, bs=bs, ta=ta: e.scalar_tensor_tensor(
                                out=tmp[ta][:, qb * 128:(qb + 1) * 128], in0=bank[bs][:, qb * 128:(qb + 1) * 128], scalar=SC_C,
                                in1=TCt[:, h, (d + 1) * 128:(d + 2) * 128], op0=ALU.mult, op1=ALU.add),
                                reads=(bankB[bs],) + CONST + ((tmpB[ta],) if qb > 0 else ()), writes=(tmpB[ta],))
                        else:
                            col = (15 if d < 0 else 31) * 16 + 8 + h
                            bb.op("dve", lambda e, qb=qb, col=col, bs=bs, ta=ta: e.tensor_scalar(
                                out=tmp[ta][:, qb * 128:(qb + 1) * 128], in0=bank[bs][:, qb * 128:(qb + 1) * 128], scalar1=SC_C,
                                scalar2=rbrep_s[:, col:col + 1], op0=ALU.mult, op1=ALU.add),
                                reads=(bankB[bs],) + CONST + ((tmpB[ta],) if qb > 0 else ()), writes=(tmpB[ta],))
                    pi = nxt(pstate, NP_)
                    bb.op("act", lambda e, ta=ta, pi=pi, r=r: e.activation(out=pT[pi][:], in_=tmp[ta][:], func=AF.Exp, bias=r[2], scale=1.0),
                          reads=(tmpB[ta],) + CONST, writes=(pTB[pi],))
                    ps_.append(pi)
                pend.append((idx, r[1], r[3], ps_[0], ps_[1]))
                idx += 1
                if len(pend) > LA:
                    pv(pend.pop(0))
            while pend:
                pv(pend.pop(0))
            z_total(0, bZ[0], ntot)
            z_total(1, bZ[1], ntot)
            tz1 = nxt(tstate, NTMP)
            bb.op("dve", lambda e: e.reciprocal(out=tmp[tz1][:], in_=bank[bZ[0]][:]), reads=(bankB[bZ[0]],), writes=(tmpB[tz1],))
            to1 = nxt(tstate, NTMP)
            bb.op("dve", lambda e: e.tensor_tensor(out=tmp[to1][:], in0=bank[bO[0]][:], in1=tmp[tz1][:], op=ALU.mult),
                  reads=(bankB[bO[0]], tmpB[tz1]), writes=(tmpB[to1],))
            tz2 = nxt(tstate, NTMP)
            bb.op("dve", lambda e: e.reciprocal(out=tmp[tz2][:], in_=bank[bZ[1]][:]), reads=(bankB[bZ[1]],), writes=(tmpB[tz2],))
            bb.op("dve", lambda e: e.scalar_tensor_tensor(out=tmp[tz2][:], in0=bank[bO[1]][:], scalar=neglam[:, 0:1], in1=tmp[tz2][:],
                                                          op0=ALU.mult, op1=ALU.mult),
                  reads=(bankB[bO[1]], tmpB[tz2], miscB), writes=(tmpB[tz2],))
            bb.op("dve", lambda e: e.tensor_tensor(out=tmp[to1][:], in0=tmp[to1][:], in1=tmp[tz2][:], op=ALU.add),
                  reads=(tmpB[to1], tmpB[tz2]), writes=(tmpB[to1],))
            si = nxt(sqstate, 2)
            bb.op("act", lambda e: e.activation(out=sq[si][:], in_=tmp[to1][:], func=AF.Square), reads=(tmpB[to1],), writes=(sqB[si],))
            bb.op("pe", lambda e: e.matmul(bank[2][:], lhsT=ones_bf[:], rhs=sq[si][:], start=True, stop=True), reads=(sqB[si],) + CONST, writes=(bankB[2],))
            rstd_from_bank(2, 128.0)
            bb.op("dve", lambda e: e.scalar_tensor_tensor(out=tmp[to1][:], in0=tmp[to1][:], scalar=gvec_s[:, L * 8 + 6:L * 8 + 7], in1=rstd[:],
                                                          op0=ALU.mult, op1=ALU.mult), reads=(tmpB[to1], rstdB) + CONST, writes=(tmpB[to1],))
            bb.op("dve", lambda e: e.scalar_tensor_tensor(out=tmp[to1][:], in0=tmp[to1][:], scalar=laminit_s[:, 1:2], in1=sgh, op0=ALU.mult, op1=ALU.mult),
                  reads=(tmpB[to1], scrB) + CONST, writes=(tmpB[to1],))
            bb.op("pool", lambda e: e.tensor_tensor(out=big[:, h, :], in0=big[:, h, :], in1=tmp[to1][:], op=ALU.add), reads=(tmpB[to1], bigB), writes=(bigB,))

        for i in range(ntiles):
            s = 0 if i < 4 else 1
            if kind == "first":
                xt = big[:].rearrange("p a t -> p (a t)").rearrange("p (a f) -> p a f", a=4)
                bb.dma("sp", xt, xin[i * TT:(i + 1) * TT, :].rearrange("(a p) f -> p a f", p=128), writes=(bigB,))
                for kc in range(8):
                    bk = kc % 4
                    for a in range(4):
                        bb.op("pe", lambda e, kc=kc, a=a, bk=bk: e.transpose(bank[bk][:, a * 128:(a + 1) * 128], xt[:, a, kc * 128:(kc + 1) * 128], ident[:]),
                              reads=(bigB,) + CONST, writes=(bankB[bk],), mark=(a == 3))
                    if kc % 2 == 0:
                        bb.op("act", lambda e, kc=kc, bk=bk: e.activation(out=xT[:, kc, :], in_=bank[bk][:], func=AF.Copy), reads=(bankB[bk],), writes=(xTB,))
                    else:
                        bb.op("dve", lambda e, kc=kc, bk=bk: e.tensor_copy(out=xT[:, kc, :], in_=bank[bk][:]), reads=(bankB[bk], xTB), writes=(xTB,))
            else:
                bb.dma("sp", xT[:], xT_i[i].rearrange("(kc p) t -> p kc t", p=128), reads=(inB,), writes=(xTB,))
            if has_attn:
                attn_part(0, i)
                ffn(0, 1, s, 2)
            if has_kv:
                if 'ffn' not in skip:
                    ffn(1, 0, s, 0)
                if 'kv' not in skip:
                    kv_part(1, i)
                bb.dma("pool", xT_o[i].rearrange("(kc p) t -> p kc t", p=128), xT[:], reads=(xTB,), writes=(outB,), partial=True)
            else:
                yt = big[:].rearrange("p a t -> p (a t)").rearrange("p (a f) -> p a f", a=4)
                for a in range(4):
                    for half in range(2):
                        bk = (a * 2 + half) % 4
                        for kk in range(4):
                            kc = half * 4 + kk
                            bb.op("pe", lambda e, a=a, kk=kk, kc=kc, bk=bk: e.transpose(bank[bk][:, kk * 128:(kk + 1) * 128], xT[:, kc, a * 128:(a + 1) * 128], ident[:]),
                                  reads=(xTB,) + CONST, writes=(bankB[bk],), mark=(kk == 3))
                        if half == 0:
                            bb.op("act", lambda e, a=a, bk=bk: e.activation(out=yt[:, a, 0:512], in_=bank[bk][:], func=AF.Copy), reads=(bankB[bk],), writes=(bigB,))
                        else:
                            bb.op("dve", lambda e, a=a, bk=bk: e.tensor_copy(out=yt[:, a, 512:1024], in_=bank[bk][:]), reads=(bankB[bk], bigB), writes=(bigB,))
                bb.dma("pool", y[i * TT:(i + 1) * TT, :].rearrange("(a p) f -> p a f", p=128), yt, reads=(bigB,), writes=(outB,), partial=True)
        for qk in ("sp", "pool"):
            Q = bb.E[qk]
            for sdm in bb.pools[qk][0]:
                if sdm.val > 0:
                    Q.eng.wait_ge(sdm.h, sdm.val)
    return nc


def _host_consts(core):
    c = {}
    c["ident"] = np.eye(128, dtype=np.float32)
    P = np.zeros((128, 128), np.float32)
    for i in range(128):
        blk = (i % 64) // 32
        j = i + 32 if blk == 0 else i - 32
        P[j, i] = 1.0
    c["perm"] = P
    pos = np.concatenate([core * 2048 + np.arange(2048), np.arange(2048)])
    row = (pos // 64).astype(np.float32)
    col = (pos % 64).astype(np.float32)
    inv = (np.float32(10000.0) ** (-np.arange(32, dtype=np.float32) / np.float32(32))).astype(np.float32)
    ang_r = row[:, None] * inv
    ang_c = col[:, None] * inv
    ang = np.concatenate([ang_r, ang_r, ang_c, ang_c], axis=-1).astype(np.float32)
    cos = np.cos(ang).astype(np.float32).T
    sin = np.sin(ang).astype(np.float32).T
    sign = np.ones((128, 1), np.float32)
    for i in range(128):
        if (i % 64) // 32 == 0:
            sign[i] = -1.0
    c["ropeT"] = np.ascontiguousarray(np.stack([cos, sin * sign]))
    oh = np.zeros((33, 512), np.float32)
    for j in range(511):
        rel = 255 - j
        oh[int(t5_bucket_np(rel)), j] = 1.0
        oh[32, j] = NEGM if abs(rel) > 128 else 0.0
    c["ohrev"] = oh
    indm = np.zeros((4, 128), np.float32); indp = np.zeros((4, 128), np.float32); nearm = np.zeros((4, 128), np.float32)
    for ti in range(4):
        Q0 = core * 16 + 4 * ti
        for kt in range(128):
            if kt < Q0 - 1:
                indm[ti, kt] = 1.0
            elif kt > Q0 + 4:
                indp[ti, kt] = 1.0
            else:
                nearm[ti, kt] = NEGM
    ind = np.concatenate([indm.reshape(-1), indp.reshape(-1), nearm.reshape(-1)])
    c["indS"] = np.ascontiguousarray(np.broadcast_to(ind[None, :], (128, 1536))).astype(np.float32)
    hm = np.array([NEGM if core == 0 else 0.0, NEGM if core == 7 else 0.0], np.float32)
    c["hmask"] = np.ascontiguousarray(np.broadcast_to(hm[None, :], (128, 2))).astype(np.float32)
    return c


def build_mod():
    nc = bass.Bass("TRN2", target_bir_lowering=False)
    cT9 = nc.dram_tensor("cT9", [128, 72], F32, kind="ExternalInput").ap()
    wsl = nc.dram_tensor("wada_sl", [4, D, 1152], F32, kind="ExternalInput").ap()
    bada9 = nc.dram_tensor("bada9", [9, 4 * 1152], F32, kind="ExternalInput").ap()
    modp = nc.dram_tensor("modp", [9, 4 * 1152], F32, kind="ExternalOutput").ap()
    from contextlib import ExitStack
    with ExitStack() as st:
        bb = B(nc, st)
        sbt = lambda n, shp, dt: st.enter_context(nc.sbuf_tensor(n, list(shp), dt))
        c_s = sbt("c_s", [128, 72], F32); sc_s = sbt("sc_s", [128, 72], F32)
        b_s = sbt("b_s", [9, 4 * 1152], F32); o_s = sbt("o_s", [9, 4 * 1152], F32)
        wt = [sbt("wt%d" % i, [128, 8, 512], F32) for i in range(2)]
        wtB = [Buf("wt%d" % i) for i in range(2)]
        ps = [st.enter_context(nc.psum_tensor("ps%d" % i, [128, 512], F32)) for i in range(2)]
        psB = [Buf("ps%d" % i) for i in range(2)]
        cB, oB = Buf("c"), Buf("o")
        bb.dma("sp", c_s[:], cT9, writes=(cB,))
        bb.dma("sp", b_s[:], bada9, writes=(cB,), partial=True)
        bb.op("act", lambda e: e.activation(out=sc_s[:], in_=c_s[:], func=AF.Silu), reads=(cB,), writes=(cB,))
        n = 0
        for l in range(4):
            for (c0, w) in ((0, 512), (512, 512), (1024, 128)):
                k = n % 2
                n += 1
                bb.dma("sp", wt[k][:, :, 0:w], wsl[l].rearrange("(kc p) n -> p kc n", p=128)[:, :, c0:c0 + w], writes=(wtB[k],))
                for kc in range(8):
                    bb.op("pe", lambda e, kc=kc, k=k, w=w: e.matmul(ps[k][0:9, 0:w], lhsT=sc_s[:, kc * 9:(kc + 1) * 9], rhs=wt[k][:, kc, 0:w],
                                                                   start=(kc == 0), stop=(kc == 7)),
                          reads=(wtB[k], cB), writes=(psB[k],), mark=(kc == 7))
                o = l * 1152 + c0
                bb.op("dve", lambda e, k=k, w=w, o=o: e.tensor_tensor(out=o_s[:, o:o + w], in0=ps[k][0:9, 0:w], in1=b_s[:, o:o + w], op=ALU.add),
                      reads=(psB[k], cB), writes=(oB,))
        bb.dma("sp", modp, o_s[:], reads=(oB,), writes=(Buf("out"),))
        for sdm in bb.pools["sp"][0]:
            if sdm.val > 0:
                nc.sync.wait_ge(sdm.h, sdm.val)
    return nc


_NC_CACHE = {}


def _get_nc(kind):
    if kind not in _NC_CACHE:
        _NC_CACHE[kind] = build_mod() if kind == "mod" else build(kind)
    return _NC_CACHE[kind]


def _run(kind, maps):
    res = run_bass_kernel_spmd(_get_nc(kind), maps, core_ids=list(range(NCORE)))
    return res.results


def kernel(**inputs):
    import ml_dtypes
    bf = ml_dtypes.bfloat16
    f = lambda a: np.ascontiguousarray(np.asarray(a, dtype=np.float32))
    inp = {k: f(v) for k, v in inputs.items()}
    xs, xp = inp["x_sample"], inp["x_prompt"]
    cs, cp = inp["c_sample"], inp["c_prompt"]
    c9 = np.concatenate([cs, cp], axis=0)
    cT9 = np.ascontiguousarray(c9.reshape(9, 8, 128).transpose(2, 1, 0).reshape(128, 72))
    maps = []
    for core in range(NCORE):
        sl = slice(core * 1152, (core + 1) * 1152)
        maps.append({"cT9": cT9,
                     "wada_sl": np.ascontiguousarray(inp["w_ada"][:, :, sl]),
                     "bada9": np.ascontiguousarray(np.broadcast_to(inp["b_ada"][:, sl].reshape(1, 4 * 1152), (9, 4 * 1152)))})
    r = _run("mod", maps)
    mod = np.zeros((4, 9, 9 * D), np.float32)
    for core in range(NCORE):
        mp = np.asarray(r[core]["modp"], np.float32).reshape(9, 4, 1152)
        mod[:, :, core * 1152:(core + 1) * 1152] = mp.transpose(1, 0, 2)
    consts = [_host_consts(core) for core in range(NCORE)]
    rb = inp["rel_bias"]
    flag = np.concatenate([np.ones(8, np.float32), np.zeros(8, np.float32)])[None, :]
    rb33 = np.ascontiguousarray(np.concatenate([rb, flag], axis=0))
    rbrep = np.ascontiguousarray(np.broadcast_to(rb.reshape(1, 512), (128, 512)))

    def layer_small(la, lb, core):
        m = {}
        mt = np.zeros((128, 2, 72, 2), np.float32)
        for slot, l in ((0, la), (1, lb)):
            for s_, seq in ((0, 0), (1, 1 + core)):
                mt[:, slot, :, s_] = mod[l, seq].reshape(72, 128).T
        m["modT"] = mt.reshape(128, 288)
        gn = np.zeros((128, 2, 3, 8), np.float32)
        gv = np.zeros((128, 2, 8), np.float32)
        for slot, l in ((0, la), (1, lb)):
            gn[:, slot] = inp["g_norm"][l].reshape(3, 8, 128).transpose(2, 0, 1)
            gv[:, slot, 0] = inp["g_qa"][l]; gv[:, slot, 1] = inp["g_ka"][l]
            gv[:, slot, 2] = inp["g_qb"][l]; gv[:, slot, 3] = inp["g_kb"][l]
            gv[:, slot, 4] = np.tile(inp["g_qc"][l], 2); gv[:, slot, 5] = np.tile(inp["g_kc"][l], 2)
            gv[:, slot, 6] = inp["g_subln"][l]
        m["g_normT"] = gn.reshape(128, 48)
        m["gvec"] = gv.reshape(128, 16)
        m["sinkrep"] = np.ascontiguousarray(np.broadcast_to(inp["sink"][la].reshape(1, 8), (128, 8)))
        lam = np.concatenate([inp["lam_q1"][la], inp["lam_k1"][la], inp["lam_q2"][la], inp["lam_k2"][la]])
        m["lamrep"] = np.ascontiguousarray(np.broadcast_to(lam.reshape(1, 256), (128, 256)))
        li = 0.8 - 0.6 * math.exp(-0.3 * la)
        m["laminit"] = np.ascontiguousarray(np.broadcast_to(np.array([[li, 1.0 - li]], np.float32), (128, 2)))
        m["rb33"] = rb33
        m["rbrep"] = rbrep
        m.update(consts[core])
        return m

    win = inp["w_in"]

    def wA(l):
        q = np.concatenate([win[l][:, QBASE[b]:QBASE[b] + 1024] for b in range(3)] + [win[l][:, GBASE[b]:GBASE[b] + 1024] for b in range(3)], axis=1)
        return {"wAq": np.ascontiguousarray(q), "wAo": inp["w_o"][l], "wAfi": inp["w_ff_in"][l, 1], "wAfo": inp["w_ff_out"][l, 1]}

    def wBk(l):
        kv = np.concatenate([win[l][:, KBASE[b]:KBASE[b] + 256] for b in range(3)] + [win[l][:, VBASE[b]:VBASE[b] + 256] for b in range(3)], axis=1)
        return {"wBkv": np.ascontiguousarray(kv), "wBfi": inp["w_ff_in"][l, 0], "wBfo": inp["w_ff_out"][l, 0]}

    depth = _DEPTH[0]
    state = None
    for p in range(depth + 1):
        kind = "first" if p == 0 else ("last" if p == depth else "mid")
        la, lb = max(p - 1, 0), min(p, 3)
        shared = {}
        if p > 0:
            shared.update(wA(la))
        if p < depth:
            shared.update(wBk(lb))
        if p > 0:
            KT_all = np.ascontiguousarray(np.concatenate([state[c]["KT_S_o"] for c in range(NCORE)], axis=0))
            V_all = np.ascontiguousarray(np.concatenate([state[c]["V_S_o"] for c in range(NCORE)], axis=0))
            shared["KT_all_i"] = KT_all
            shared["V_all_i"] = V_all
        maps = []
        for core in range(NCORE):
            m = dict(shared)
            m.update(layer_small(la, lb, core))
            if p == 0:
                m["xin"] = np.ascontiguousarray(np.concatenate([xs[0, core * 2048:(core + 1) * 2048], xp[core]], axis=0))
            else:
                m["xT_i"] = state[core]["xT_o"]
                m["KT_P_i"] = state[core]["KT_P_o"]
                m["V_P_i"] = state[core]["V_P_o"]
                for nm, key in (("KT_Sx", "KT_S_o"), ("V_Sx", "V_S_o")):
                    ext = np.zeros((768, 18 * 128), bf)
                    ext[:, 128:2176] = state[core][key]
                    if core > 0:
                        ext[:, 0:128] = state[core - 1][key][:, 1920:2048]
                    if core < NCORE - 1:
                        ext[:, 2176:2304] = state[core + 1][key][:, 0:128]
                    m[nm] = ext
            maps.append(m)
        r = _run(kind, maps)
        state = [{k: np.asarray(v) for k, v in r[c].items()} for c in range(NCORE)]
    y_prompt = np.zeros((8, 2048, D), np.float32)
    y_sample = np.zeros((1, 16384, D), np.float32)
    for core in range(NCORE):
        yy = np.asarray(state[core]["y"], dtype=np.float32)
        y_sample[0, core * 2048:(core + 1) * 2048] = yy[:2048]
        y_prompt[core] = yy[2048:]
    return (y_prompt, y_sample)


_DEPTH = [4]
```

```python
import math
import numpy as np
import concourse.bass as bass
import concourse.mybir as mybir
from concourse.bass_utils import run_bass_kernel_spmd

F32 = mybir.dt.float32
BF16 = mybir.dt.bfloat16
AF = mybir.ActivationFunctionType
ALU = mybir.AluOpType

D = 1024
DFF = 2816
NCORE = 8
TT = 512
NT = 8
EPS = 1e-6
NEGM = -30000.0
QBASE = [0, 1536, 3072]
KBASE = [1024, 2560, 4096]
VBASE = [1280, 2816, 4352]
GBASE = [4608, 5632, 6656]
SC_A = 128.0 ** -0.5
SC_C = 64.0 ** -0.5


class Buf:
    __slots__ = ("name", "w", "r")

    def __init__(self, name):
        self.name = name
        self.w = {}
        self.r = {}


class Eng:
    def __init__(self, key, eng, sem):
        self.key = key
        self.eng = eng
        self.sem = sem
        self.cnt = 0
        self.waited = {}


class DSem:
    def __init__(self, key, h):
        self.key = key
        self.h = h
        self.val = 0


class B:
    def __init__(self, nc, stack):
        self.nc = nc
        self.semh = {}
        self.E = {}
        for key, eng in (("pe", nc.tensor), ("act", nc.scalar), ("dve", nc.vector), ("pool", nc.gpsimd)):
            h = stack.enter_context(nc.semaphore("s_" + key))
            self.semh[key] = h
            self.E[key] = Eng(key, eng, h)
        self.E["sp"] = Eng("sp", nc.sync, None)
        self.pools = {}
        for q, n in (("sp", 20), ("pool", 20)):
            lst = []
            for i in range(n):
                k = "d_%s_%d" % (q, i)
                h = stack.enter_context(nc.semaphore(k))
                self.semh[k] = h
                lst.append(DSem(k, h))
            self.pools[q] = [lst, 0]
        k = "cc"
        h = stack.enter_context(nc.semaphore(k))
        self.semh[k] = h
        self.ccsem = DSem(k, h)

    def _deps(self, E, reads, writes, partial):
        deps = {}
        for b in reads:
            for k, v in b.w.items():
                if deps.get(k, 0) < v:
                    deps[k] = v
        for b in writes:
            for k, v in b.r.items():
                if deps.get(k, 0) < v:
                    deps[k] = v
            if not partial:
                for k, v in b.w.items():
                    if deps.get(k, 0) < v:
                        deps[k] = v
        for k, v in deps.items():
            if E.waited.get(k, 0) >= v:
                continue
            if k == E.key and k == "pe":
                continue
            E.eng.wait_ge(self.semh[k], v)
            E.waited[k] = v

    def _upd(self, key, tok, reads, writes, partial):
        for b in reads:
            if b.r.get(key, 0) < tok:
                b.r[key] = tok
        for b in writes:
            if partial:
                if b.w.get(key, 0) < tok:
                    b.w[key] = tok
            else:
                b.w = {key: tok}
                b.r = {}

    def op(self, ek, fn, reads=(), writes=(), mark=True):
        E = self.E[ek]
        pr = tuple(b for b in reads if b.name.startswith("bank") and b not in writes)
        if pr:
            writes = tuple(writes) + pr
        self._deps(E, reads, writes, False)
        ins = fn(E.eng)
        if mark:
            E.cnt += 1
            ins.then_inc(E.sem, 1)
            tok = E.cnt
        else:
            tok = E.cnt + 1
        self._upd(E.key, tok, reads, writes, False)
        return ins

    def dma(self, qk, out, in_, reads=(), writes=(), partial=False):
        Q = self.E[qk]
        pool = self.pools[qk]
        s = pool[0][pool[1]]
        pool[1] = (pool[1] + 1) % len(pool[0])
        if s.val > 0 and Q.waited.get(s.key, 0) < s.val:
            Q.eng.wait_ge(s.h, s.val)
            Q.waited[s.key] = s.val
        self._deps(Q, reads, writes, partial)
        ins = Q.eng.dma_start(out=out, in_=in_)
        s.val += 16
        ins.then_inc(s.h, 16)
        self._upd(s.key, s.val, reads, writes, partial)
        return ins

    def collective(self, fn, reads=(), writes=()):
        Q = self.E["pool"]
        s = self.ccsem
        if s.val > 0 and Q.waited.get(s.key, 0) < s.val:
            Q.eng.wait_ge(s.h, s.val)
            Q.waited[s.key] = s.val
        self._deps(Q, reads, writes, False)
        ins = fn(Q.eng)
        s.val += 16
        ins.then_inc(s.h, 16)
        self._upd(s.key, s.val, reads, writes, False)

    def wait_all(self, ek, bufs):
        E = self.E[ek]
        self._deps(E, bufs, (), False)


def t5_bucket_np(rel):
    rel = np.asarray(rel, np.int64)
    half, max_exact = 16, 8
    ret = np.where(rel > 0, half, 0)
    n = np.abs(rel)
    lg = np.log(np.maximum(n, 1).astype(np.float32) / np.float32(max_exact)).astype(np.float32)
    large = max_exact + (lg / np.float32(math.log(128 / 8)) * np.float32(half - max_exact)).astype(np.int32)
    large = np.minimum(large, half - 1)
    return ret + np.where(n < max_exact, n, large)


_OPTS = {"skip": (), "ntiles": NT}


def build(kind, ntiles=None):
    skip = _OPTS["skip"]
    ntiles = _OPTS["ntiles"] if ntiles is None else ntiles
    from contextlib import ExitStack
    nc = bass.Bass("TRN2", target_bir_lowering=False)

    def din(name, shape, dt=F32):
        return nc.dram_tensor(name, list(shape), dt, kind="ExternalInput").ap()

    def dscr(name, shape, dt):
        return nc.dram_tensor(name, list(shape), dt).ap()

    has_attn = kind in ("mid", "last")
    has_kv = kind in ("first", "mid")
    if kind == "first":
        xin = din("xin", [4096, D])
    else:
        xT_i = din("xT_i", [NT, D, TT])
    if has_kv:
        xT_o = nc.dram_tensor("xT_o", [NT, D, TT], F32, kind="ExternalOutput").ap()
        KT_S_o = nc.dram_tensor("KT_S_o", [768, 2048], BF16, kind="ExternalOutput").ap()
        V_S_o = nc.dram_tensor("V_S_o", [768, 2048], BF16, kind="ExternalOutput").ap()
        KT_P_o = nc.dram_tensor("KT_P_o", [768, 2048], BF16, kind="ExternalOutput").ap()
        V_P_o = nc.dram_tensor("V_P_o", [768, 2048], BF16, kind="ExternalOutput").ap()
    else:
        y = nc.dram_tensor("y", [4096, D], F32, kind="ExternalOutput").ap()
    WIN = {}
    if has_attn:
        WIN["Aq"] = (din("wAq", [D, 6144]), [D, 6144])
        WIN["Ao"] = (din("wAo", [D, D]), [D, D])
        WIN["Afi"] = (din("wAfi", [D, 2 * DFF]), [D, 2 * DFF])
        WIN["Afo"] = (din("wAfo", [DFF, D]), [DFF, D])
        KT_all_i = din("KT_all_i", [8 * 768, 2048], BF16)
        V_all_i = din("V_all_i", [8 * 768, 2048], BF16)
        KT_P_i = din("KT_P_i", [768, 2048], BF16)
        V_P_i = din("V_P_i", [768, 2048], BF16)
        KT_Sx = din("KT_Sx", [768, 18 * 128], BF16)
        V_Sx = din("V_Sx", [768, 18 * 128], BF16)
    if has_kv:
        WIN["Bfi"] = (din("wBfi", [D, 2 * DFF]), [D, 2 * DFF])
        WIN["Bfo"] = (din("wBfo", [DFF, D]), [DFF, D])
        WIN["Bkv"] = (din("wBkv", [D, 1536]), [D, 1536])
    modT_i = din("modT", [128, 2 * 72 * 2])
    g_normT = din("g_normT", [128, 48])
    gvec = din("gvec", [128, 16])
    sinkrep = din("sinkrep", [128, 8])
    lamrep = din("lamrep", [128, 256])
    laminit = din("laminit", [128, 2])
    rb33 = din("rb33", [33, 16])
    rbrep = din("rbrep", [128, 512])
    ohrev = din("ohrev", [33, 512])
    ident_i = din("ident", [128, 128])
    perm_i = din("perm", [128, 128])
    ropeT = din("ropeT", [2, 128, 4096])
    indS = din("indS", [128, 3 * 4 * 128])
    hmask = din("hmask", [128, 2])
    WSC = {k: dscr("w%s_b" % k, shp, BF16) for k, (ap_, shp) in WIN.items()}
    G_d = dscr("G_d", [16, 128, 512], F32)

    with ExitStack() as st:
        bb = B(nc, st)

        def sb(name, shape, dt):
            return st.enter_context(nc.sbuf_tensor(name, list(shape), dt))

        bank2 = [st.enter_context(nc.psum_tensor("bankp%d" % i, [128, 1024], F32)) for i in range(4)]
        bank = []
        for i in range(4):
            bank.append(bank2[i][:, 0:512])
            bank.append(bank2[i][:, 512:1024])
        bankB = [Buf("bank%d" % i) for i in range(8)]

        xT = sb("xT", [128, 8, TT], F32); xTB = Buf("xT")
        hT = sb("hT", [128, 8, TT], BF16); hTB = Buf("hT")
        scr = sb("scr", [128, 22, TT], BF16); scrB = Buf("scr")
        big = sb("big", [128, 8, TT], F32); bigB = Buf("big")
        NW = 3
        wring = [sb("wring%d" % i, [128, 4096], BF16) for i in range(NW)]
        wringB = [Buf("wring%d" % i) for i in range(NW)]
        wstate = [0]
        kvstate = [0]
        kblk = [sb("kblk%d" % i, [128, 2048], BF16) for i in range(2)]
        kblkB = [Buf("kblk%d" % i) for i in range(2)]
        vblk = [sb("vblk%d" % i, [128, 2048], BF16) for i in range(2)]
        vblkB = [Buf("vblk%d" % i) for i in range(2)]
        nearK = sb("nearK", [128, 4, 768], BF16); nearKB = Buf("nearK")
        nearV = sb("nearV", [128, 4, 768], BF16); nearVB = Buf("nearV")
        NP_ = 4
        pT = [sb("pT%d" % i, [128, 2 * TT], BF16) for i in range(NP_)]
        pTB = [Buf("pT%d" % i) for i in range(NP_)]
        pstate = [0]
        NTMP = 4
        tmp = [sb("tmp%d" % i, [128, TT], F32) for i in range(NTMP)]
        tmpB = [Buf("tmp%d" % i) for i in range(NTMP)]
        tstate = [0]
        zacc = [sb("zacc%d" % e, [128, 2 * TT], F32) for e in range(3)]
        zaccB = [Buf("zacc%d" % e) for e in range(3)]
        ones_f32 = sb("ones_f32", [128, 128], F32)
        sq = [sb("sq%d" % i, [128, TT], BF16) for i in range(2)]
        sqB = [Buf("sq%d" % i) for i in range(2)]
        sqstate = [0]
        rstd = sb("rstd", [128, TT], F32); rstdB = Buf("rstd")
        qraw = sb("qraw", [128, TT], F32); qrawB = Buf("qraw")
        qn32 = sb("qn32", [128, TT], F32); qn32B = Buf("qn32")
        qnb = sb("qnb", [128, TT], BF16); qnbB = Buf("qnb")
        ropeC = sb("ropeC", [128, TT], F32); ropeS = sb("ropeS", [128, TT], F32); ropeB = Buf("rope")
        MBt = sb("MBt", [128, 8, 384], F32)
        TCt = sb("TCt", [128, 8, 384], F32)
        biasS = sb("biasS", [128, 128, 8], F32); biasSB = Buf("biasS")
        vtok = [sb("vtok%d" % i, [128, 768], BF16) for i in range(2)]
        vtokB = [Buf("vtok%d" % i) for i in range(2)]
        kst = [sb("kst%d" % i, [128, TT], BF16) for i in range(2)]
        kstB = [Buf("kst%d" % i) for i in range(2)]
        constB = Buf("const")
        ident = sb("ident_s", [128, 128], F32)
        ones_bf = sb("ones_bf", [128, 128], BF16)
        onesC_bf = sb("onesC_bf", [128, 128], BF16)
        perm_bf = sb("perm_bf", [128, 128], BF16)
        eps_t = sb("eps_t", [128, 1], F32)
        zero_t = sb("zero_t", [128, 1], F32)
        g_normT_s = sb("g_normT_s", [128, 48], F32)
        gvec_s = sb("gvec_s", [128, 16], F32)
        sink_s = sb("sink_s", [128, 8], F32)
        esink = sb("esink", [128, 8], F32)
        lam_s = sb("lam_s", [128, 256], F32)
        laminit_s = sb("laminit_s", [128, 2], F32)
        neglam = sb("neglam", [128, 1], F32)
        lamtmp = sb("lamtmp", [128, 64], F32)
        lamacc = sb("lamacc", [128, 8], F32)
        rb33_s = sb("rb33_s", [33, 16], F32)
        rbrep_s = sb("rbrep_s", [128, 512], F32)
        ohrev_s = sb("ohrev_s", [33, 512], F32)
        lhsG = sb("lhsG", [33, 128], F32)
        ones33 = sb("ones33", [33, 128], F32)
        indS_s = sb("indS_s", [128, 3 * 512], F32)
        hmask_s = sb("hmask_s", [128, 2], F32)
        modT = sb("modT_s", [128, 2, 72, 2], F32); modB = Buf("modT")
        Avec = sb("Avec", [128, 2 * 2 * 3 * 8], F32)
        Gvec = sb("Gvec", [128, 2 * 2 * 3 * 8], F32)
        miscB = Buf("misc")

        def nxt(state, n):
            i = state[0]
            state[0] = (i + 1) % n
            return i

        wB = {}
        inB = Buf("inputs")
        outB = Buf("outputs")
        GdB = Buf("G_d")

        def ld(dst, src, q="sp"):
            bb.dma(q, dst, src, reads=(), writes=(constB,), partial=True)

        ld(ident[:], ident_i)
        ld(modT[:].rearrange("p a f s -> p (a f s)"), modT_i); ld(g_normT_s[:], g_normT); ld(gvec_s[:], gvec)
        ld(sink_s[:], sinkrep); ld(lam_s[:], lamrep); ld(laminit_s[:], laminit)
        ld(rb33_s[:], rb33); ld(rbrep_s[:], rbrep); ld(ohrev_s[:], ohrev)
        ld(indS_s[:], indS); ld(hmask_s[:], hmask)
        ld(perm_bf[:], perm_i, q="pool")
        c2B = Buf("const2")
        bb.op("dve", lambda e: e.memset(ones_bf[:], 1.0), writes=(c2B,))
        bb.op("dve", lambda e: e.memset(ones_f32[:], 1.0), writes=(c2B,))
        bb.op("dve", lambda e: e.memset(onesC_bf[:], 0.0), writes=(c2B,))
        bb.op("dve", lambda e: e.memset(onesC_bf[0:64, 0:64], 1.0), writes=(c2B,))
        bb.op("dve", lambda e: e.memset(onesC_bf[64:128, 64:128], 1.0), writes=(c2B,))
        bb.op("dve", lambda e: e.memset(eps_t[:], EPS), writes=(c2B,))
        bb.op("dve", lambda e: e.memset(zero_t[:], 0.0), writes=(c2B,))
        bb.op("dve", lambda e: e.memset(ones33[:], 1.0), writes=(c2B,))
        CONST = (constB, c2B)

        order = [k for k in ("Bfi", "Bfo", "Bkv", "Aq", "Ao", "Afi", "Afo") if k in WIN]
        if has_attn:
            order = [k for k in ("Aq", "Ao", "Afi", "Afo", "Bfi", "Bfo", "Bkv") if k in WIN]
        for k in (order if 'cast' not in skip else []):
            src, shp = WIN[k]
            wb_ = wB.setdefault(k, Buf("w" + k))
            r0 = 0
            while r0 < shp[0]:
                r1 = min(r0 + 128, shp[0])
                bb.dma("pool", WSC[k][r0:r1, :], src[r0:r1, :], writes=(wb_,), partial=True)
                r0 = r1

        for l in range(2):
            for s in range(2):
                for j in range(3):
                    o = ((l * 2 + s) * 3 + j) * 8
                    bb.op("dve", lambda e, l=l, s=s, j=j, o=o: e.scalar_tensor_tensor(
                        out=Avec[:, o:o + 8], in0=modT[:, l, (3 * j + 1) * 8:(3 * j + 2) * 8, s], scalar=1.0,
                        in1=g_normT_s[:, (l * 3 + j) * 8:(l * 3 + j) * 8 + 8], op0=ALU.add, op1=ALU.mult),
                        reads=CONST, writes=(miscB,))
                    bb.op("dve", lambda e, l=l, s=s, j=j, o=o: e.tensor_scalar(
                        out=Gvec[:, o:o + 8], in0=modT[:, l, (3 * j + 2) * 8:(3 * j + 3) * 8, s],
                        scalar1=(1.0 if j == 1 else 0.5), scalar2=None, op0=ALU.mult),
                        reads=CONST, writes=(miscB,))
        for m in range(2):
            a0 = 2 * m * 64
            bb.op("dve", lambda e, a0=a0: e.tensor_tensor(out=lamtmp[:], in0=lam_s[:, a0:a0 + 64], in1=lam_s[:, a0 + 64:a0 + 128], op=ALU.mult),
                  reads=CONST + (miscB,), writes=(miscB,))
            bb.op("dve", lambda e, m=m: e.tensor_reduce(out=lamacc[:, m:m + 1], in_=lamtmp[:], axis=mybir.AxisListType.X, op=ALU.add),
                  reads=(miscB,), writes=(miscB,))
        bb.op("act", lambda e: e.activation(out=lamacc[:, 2:4], in_=lamacc[:, 0:2], func=AF.Exp), reads=(miscB,), writes=(miscB,))
        bb.op("dve", lambda e: e.tensor_tensor(out=lamacc[:, 4:5], in0=lamacc[:, 3:4], in1=lamacc[:, 2:3], op=ALU.subtract),
              reads=(miscB,), writes=(miscB,))
        bb.op("dve", lambda e: e.tensor_tensor(out=neglam[:, 0:1], in0=lamacc[:, 4:5], in1=laminit_s[:, 0:1], op=ALU.subtract),
              reads=(miscB,) + CONST, writes=(miscB,))
        bb.op("act", lambda e: e.activation(out=esink[:], in_=sink_s[:], func=AF.Exp), reads=CONST, writes=(miscB,))

        for hh in range(16 if 't5' not in skip else 0):
            bb.op("dve", lambda e, hh=hh: e.tensor_scalar(out=lhsG[:], in0=ones33[:], scalar1=rb33_s[:, hh:hh + 1], scalar2=None, op0=ALU.mult),
                  reads=CONST + (miscB,), writes=(miscB,))
            bi = 2 + hh % 2
            bb.op("pe", lambda e, bi=bi: e.matmul(bank[bi][:], lhsT=lhsG[:], rhs=ohrev_s[:], start=True, stop=True),
                  reads=(miscB,) + CONST, writes=(bankB[bi],))
            ti_ = nxt(tstate, NTMP)
            bb.op("dve", lambda e, bi=bi, ti_=ti_: e.tensor_copy(out=tmp[ti_][:], in_=bank[bi][:]), reads=(bankB[bi],), writes=(tmpB[ti_],))
            bb.dma("pool", G_d[hh], tmp[ti_][:], reads=(tmpB[ti_],), writes=(GdB,), partial=True)
        for hh in range(16 if 't5b' not in skip else 0):
            dst = MBt if hh < 8 else TCt
            for dd in range(3):
                d = dd - 1
                src = bass.AP(G_d.tensor, hh * 128 * 512 + 255 - d * 128, [[511, 128], [1, 128]])
                bb.dma("sp", dst[:, hh % 8, dd * 128:(dd + 1) * 128], src, reads=(GdB,), writes=(constB,), partial=True)

        def sclr(l, s, j, c, which):
            o = ((l * 2 + s) * 3 + j) * 8 + c
            return (Avec if which == "A" else Gvec)[:, o:o + 1]

        def rstd_from_bank(bk, dim):
            bb.op("act", lambda e: e.activation(out=rstd[:], in_=bank[bk][:], func=AF.Ln, bias=eps_t[:], scale=1.0 / dim),
                  reads=(bankB[bk],) + CONST, writes=(rstdB,))
            bb.op("act", lambda e: e.activation(out=rstd[:], in_=rstd[:], func=AF.Exp, scale=-0.5), reads=(rstdB,), writes=(rstdB,))

        def norm_mod(l, s, j, bk=7):
            for kc in range(8):
                si = nxt(sqstate, 2)
                bb.op("act", lambda e, kc=kc, si=si: e.activation(out=sq[si][:], in_=xT[:, kc, :], func=AF.Square),
                      reads=(xTB,), writes=(sqB[si],))
                bb.op("pe", lambda e, kc=kc, si=si: e.matmul(bank[bk][:], lhsT=ones_bf[:], rhs=sq[si][:], start=(kc == 0), stop=(kc == 7)),
                      reads=(sqB[si],) + CONST, writes=(bankB[bk],), mark=True)
            rstd_from_bank(bk, 1024.0)
            for kc in range(8):
                ti_ = nxt(tstate, NTMP)
                bb.op("dve", lambda e, kc=kc, ti_=ti_: e.tensor_tensor(out=tmp[ti_][:], in0=xT[:, kc, :], in1=rstd[:], op=ALU.mult),
                      reads=(xTB, rstdB), writes=(tmpB[ti_],))
                mo = 3 * j * 8 + kc
                bb.op("pool", lambda e, kc=kc, ti_=ti_, mo=mo: e.tensor_scalar(
                    out=hT[:, kc, :], in0=tmp[ti_][:], scalar1=sclr(l, s, j, kc, "A"), scalar2=modT[:, l, mo, s:s + 1],
                    op0=ALU.mult, op1=ALU.add), reads=(tmpB[ti_], miscB) + CONST, writes=(hTB,))

        def wload(srcs, wb):
            wi = nxt(wstate, NW)
            first = True
            for (o, a, b_, src) in srcs:
                dst = wring[wi][:, o:o + a * b_].rearrange("p (a b) -> p a b", a=a)
                bb.dma("sp", dst, src, reads=(wb,), writes=(wringB[wi],), partial=(not first))
                first = False
            return wi

        def resid_update(l, s, j, m, bk):
            bb.op("dve", lambda e: e.scalar_tensor_tensor(out=xT[:, m, :], in0=bank[bk][:], scalar=sclr(l, s, j, m, "G"),
                                                          in1=xT[:, m, :], op0=ALU.mult, op1=ALU.add),
                  reads=(bankB[bk], miscB, xTB), writes=(xTB,))

        def ffn(l, which, s, j):
            norm_mod(l, s, j)
            wk = "A" if l == 0 else "B"
            wsrc = WSC[wk + "fi"].rearrange("(kc p) n -> p kc n", p=128)
            wb = wB[wk + "fi"]
            groups = [(g0, min(2, 22 - g0)) for g0 in range(0, 22, 2)]
            loads = []

            def issue(gi):
                g0, n = groups[gi]
                return wload([(0, 8, n * 128, wsrc[:, :, g0 * 128:(g0 + n) * 128]),
                              (2048, 8, n * 128, wsrc[:, :, DFF + g0 * 128:DFF + (g0 + n) * 128])], wb)

            PF = 2
            for gi in range(min(PF, len(groups))):
                loads.append(issue(gi))
            for gi, (g0, n) in enumerate(groups):
                if gi + PF < len(groups):
                    loads.append(issue(gi + PF))
                wi = loads[gi]
                for jj in range(n):
                    jp = g0 + jj
                    bg, bu = (jp % 3) * 2, (jp % 3) * 2 + 1
                    for part, bk in ((0, bg), (1, bu)):
                        for kc in range(8):
                            o = part * 2048 + kc * n * 128 + jj * 128
                            bb.op("pe", lambda e, o=o, kc=kc, bk=bk, wi=wi: e.matmul(
                                bank[bk][:], lhsT=wring[wi][:, o:o + 128], rhs=hT[:, kc, :], start=(kc == 0), stop=(kc == 7)),
                                reads=(wringB[wi], hTB), writes=(bankB[bk],), mark=(kc == 7))
                    ti_ = nxt(tstate, NTMP)
                    bb.op("act", lambda e, bg=bg, ti_=ti_: e.activation(out=tmp[ti_][:], in_=bank[bg][:], func=AF.Silu),
                          reads=(bankB[bg],), writes=(tmpB[ti_],))
                    bb.op("dve", lambda e, bu=bu, ti_=ti_, jp=jp: e.tensor_tensor(out=scr[:, jp, :], in0=tmp[ti_][:], in1=bank[bu][:], op=ALU.mult),
                          reads=(tmpB[ti_], bankB[bu]), writes=(scrB,))
            wsrc2 = WSC[wk + "fo"].rearrange("(kc p) n -> p kc n", p=128)
            wb2 = wB[wk + "fo"]
            og = [(k0, min(4, 22 - k0)) for k0 in range(0, 22, 4)]
            loads = []

            def issue2(gi):
                k0, n = og[gi]
                return wload([(0, n, 1024, wsrc2[:, k0:k0 + n, :])], wb2)

            for gi in range(min(PF, len(og))):
                loads.append(issue2(gi))
            for gi, (k0, n) in enumerate(og):
                if gi + PF < len(og):
                    loads.append(issue2(gi + PF))
                wi = loads[gi]
                for kk in range(n):
                    kc = k0 + kk
                    for m in range(8):
                        bb.op("pe", lambda e, kk=kk, kc=kc, m=m, wi=wi: e.matmul(
                            bank[m][:], lhsT=wring[wi][:, kk * 1024 + m * 128:kk * 1024 + (m + 1) * 128], rhs=scr[:, kc, :],
                            start=(kc == 0), stop=(kc == 21)),
                            reads=(wringB[wi], scrB), writes=(bankB[m],), mark=(kc == 21 or (kk == n - 1 and m == 7)))
            for m in range(8):
                resid_update(l, s, j, m, m)

        def proj_chunks(wkey, cols, consumer, bks):
            wsrc = WSC[wkey].rearrange("(kc p) n -> p kc n", p=128)
            wb = wB[wkey]
            groups = [cols[i:i + 4] for i in range(0, len(cols), 4)]
            loads = []

            def issue(gi):
                g = groups[gi]
                srcs = []
                for ci, c0 in enumerate(g):
                    srcs.append((ci * 1024, 8, 128, wsrc[:, :, c0:c0 + 128]))
                return wload(srcs, wb)

            PF = 2
            for gi in range(min(PF, len(groups))):
                loads.append(issue(gi))
            n = 0
            for gi, g in enumerate(groups):
                if gi + PF < len(groups):
                    loads.append(issue(gi + PF))
                wi = loads[gi]
                for ci, c0 in enumerate(g):
                    bk = bks[n % len(bks)]
                    for kc in range(8):
                        o = ci * 1024 + kc * 128
                        bb.op("pe", lambda e, o=o, kc=kc, bk=bk, wi=wi: e.matmul(
                            bank[bk][:], lhsT=wring[wi][:, o:o + 128], rhs=hT[:, kc, :], start=(kc == 0), stop=(kc == 7)),
                            reads=(wringB[wi], hTB), writes=(bankB[bk],), mark=(kc == 7))
                    consumer(n, bk)
                    n += 1

        def qk_post(l, bk, gcol, br, out_ap, outB, bk2):
            si = nxt(sqstate, 2)
            bb.op("act", lambda e: e.activation(out=sq[si][:], in_=bank[bk][:], func=AF.Square), reads=(bankB[bk],), writes=(sqB[si],))
            bb.op("dve", lambda e: e.tensor_copy(out=qraw[:], in_=bank[bk][:]), reads=(bankB[bk],), writes=(qrawB,))
            on = onesC_bf if br == 2 else ones_bf
            bb.op("pe", lambda e: e.matmul(bank[bk2][:], lhsT=on[:], rhs=sq[si][:], start=True, stop=True),
                  reads=(sqB[si],) + CONST, writes=(bankB[bk2],))
            rstd_from_bank(bk2, 64.0 if br == 2 else 128.0)
            gap = gvec_s[:, l * 8 + gcol:l * 8 + gcol + 1]
            if br != 0:
                bb.op("dve", lambda e: e.scalar_tensor_tensor(out=out_ap, in0=qraw[:], scalar=gap, in1=rstd[:], op0=ALU.mult, op1=ALU.mult),
                      reads=(qrawB, rstdB) + CONST, writes=(outB,))
                return
            bb.op("dve", lambda e: e.scalar_tensor_tensor(out=qn32[:], in0=qraw[:], scalar=gap, in1=rstd[:], op0=ALU.mult, op1=ALU.mult),
                  reads=(qrawB, rstdB) + CONST, writes=(qn32B,))
            bb.op("act", lambda e: e.activation(out=qnb[:], in_=qn32[:], func=AF.Copy), reads=(qn32B,), writes=(qnbB,))
            bb.op("pe", lambda e: e.matmul(bank[bk2][:], lhsT=perm_bf[:], rhs=qnb[:], start=True, stop=True),
                  reads=(qnbB,) + CONST, writes=(bankB[bk2],))
            t1 = nxt(tstate, NTMP)
            bb.op("pool", lambda e: e.tensor_tensor(out=tmp[t1][:], in0=qn32[:], in1=ropeC[:], op=ALU.mult),
                  reads=(qn32B, ropeB), writes=(tmpB[t1],))
            t2 = nxt(tstate, NTMP)
            bb.op("dve", lambda e: e.tensor_tensor(out=tmp[t2][:], in0=bank[bk2][:], in1=ropeS[:], op=ALU.mult),
                  reads=(bankB[bk2], ropeB), writes=(tmpB[t2],))
            bb.op("dve", lambda e: e.tensor_tensor(out=out_ap, in0=tmp[t1][:], in1=tmp[t2][:], op=ALU.add),
                  reads=(tmpB[t1], tmpB[t2]), writes=(outB,))

        def load_rope(i):
            bb.dma("sp", ropeC[:], ropeT[0, :, i * TT:(i + 1) * TT], writes=(ropeB,))
            bb.dma("sp", ropeS[:], ropeT[1, :, i * TT:(i + 1) * TT], writes=(ropeB,), partial=True)

        def kv_part(l, i):
            seg_s = i < 4
            s = 0 if seg_s else 1
            ti = i % 4
            KT = KT_S_o if seg_s else KT_P_o
            VV = V_S_o if seg_s else V_P_o
            KTb = outB
            VVb = outB
            norm_mod(l, s, 1)
            load_rope(i)
            cols = [n * 128 for n in range(6)]

            def cons(n, bk):
                br = n // 2
                ki = nxt(sqstate, 2) if False else (n % 2)
                qk_post(l, bk, [1, 3, 5][br], br, kst[ki][:], kstB[ki], 6)
                bb.dma("pool", KT[n * 128:(n + 1) * 128, ti * TT:(ti + 1) * TT], kst[ki][:], reads=(kstB[ki],), writes=(KTb,), partial=True)

            if "kvK" not in skip:
                proj_chunks("Bkv", cols, cons, [4, 5])
            if "kvV" in skip:
                return
            wsrc = WSC["Bkv"].rearrange("(kc p) n -> p kc n", p=128)
            wi = wload([(0, 8, 256, wsrc[:, :, 768:1024]), (2048, 8, 256, wsrc[:, :, 1024:1280])], wB["Bkv"])
            wi2 = wload([(0, 8, 256, wsrc[:, :, 1280:1536])], wB["Bkv"])
            for a in range(4):
                b0, b1 = 4 + (a % 2) * 2, 5 + (a % 2) * 2
                for kc in range(8):
                    bb.op("pe", lambda e, a=a, kc=kc, b0=b0: e.matmul(bank[b0][:, 0:256], lhsT=hT[:, kc, a * 128:(a + 1) * 128],
                                                                      rhs=wring[wi][:, kc * 256:(kc + 1) * 256], start=(kc == 0), stop=(kc == 7)),
                          reads=(wringB[wi], hTB), writes=(bankB[b0],), mark=False)
                for kc in range(8):
                    bb.op("pe", lambda e, a=a, kc=kc, b0=b0: e.matmul(bank[b0][:, 256:512], lhsT=hT[:, kc, a * 128:(a + 1) * 128],
                                                                      rhs=wring[wi][:, 2048 + kc * 256:2048 + (kc + 1) * 256], start=(kc == 0), stop=(kc == 7)),
                          reads=(wringB[wi], hTB), writes=(bankB[b0],), mark=(kc == 7))
                for kc in range(8):
                    bb.op("pe", lambda e, a=a, kc=kc, b1=b1: e.matmul(bank[b1][:, 0:256], lhsT=hT[:, kc, a * 128:(a + 1) * 128],
                                                                      rhs=wring[wi2][:, kc * 256:(kc + 1) * 256], start=(kc == 0), stop=(kc == 7)),
                          reads=(wringB[wi2], hTB), writes=(bankB[b1],), mark=(kc == 7))
                vi = a % 2
                bb.op("act", lambda e, vi=vi, b0=b0: e.activation(out=vtok[vi][:, 0:512], in_=bank[b0][:], func=AF.Copy),
                      reads=(bankB[b0],), writes=(vtokB[vi],))
                bb.op("dve", lambda e, vi=vi, b1=b1: e.tensor_copy(out=vtok[vi][:, 512:768], in_=bank[b1][:, 0:256]),
                      reads=(bankB[b1], vtokB[vi]), writes=(vtokB[vi],))
                kt = ti * 4 + a
                dst = VV.rearrange("(g p) c -> p g c", p=128)[:, :, kt * 128:(kt + 1) * 128]
                bb.dma("pool", dst, vtok[vi][:].rearrange("p (g d) -> p g d", g=6), reads=(vtokB[vi],), writes=(VVb,), partial=True)

        def attn_part(L, i):
            seg_s = i < 4
            s = 0 if seg_s else 1
            ti = i % 4
            par = 0
            nblk = 8 if seg_s else 1
            norm_mod(L, s, 1)
            load_rope(i)
            if seg_s:
                t0, t1 = 4 * ti - 1, 4 * ti + 4
                KTl, VVl, off = KT_Sx, V_Sx, 128
            else:
                t0, t1 = max(4 * ti - 1, 0), min(4 * ti + 4, 15)
                KTl, VVl, off = KT_P_i, V_P_i, 0
            ncol = (t1 - t0 + 1) * 128
            for q4 in range(4):
                brg = 2 + q4
                bb.dma("sp", nearK[:, q4, 0:ncol], KTl[brg * 128:(brg + 1) * 128, off + t0 * 128:off + t0 * 128 + ncol], reads=(inB,), writes=(nearKB,), partial=(q4 > 0))
                bb.dma("sp", nearV[:, q4, 0:ncol], VVl[brg * 128:(brg + 1) * 128, off + t0 * 128:off + t0 * 128 + ncol], reads=(inB,), writes=(nearVB,), partial=(q4 > 0))
            if seg_s:
                for h in range(8):
                    ta = nxt(tstate, NTMP)
                    bb.op("pool", lambda e, ta=ta, h=h: e.tensor_scalar(out=tmp[ta][:, 0:128], in0=indS_s[:, ti * 128:(ti + 1) * 128],
                                                                        scalar1=rbrep_s[:, 15 * 16 + 8 + h:15 * 16 + 9 + h], scalar2=None, op0=ALU.mult),
                          reads=CONST, writes=(tmpB[ta],))
                    bb.op("dve", lambda e, ta=ta, h=h: e.scalar_tensor_tensor(out=tmp[ta][:, 128:256], in0=indS_s[:, 512 + ti * 128:512 + (ti + 1) * 128],
                                                                               scalar=rbrep_s[:, 31 * 16 + 8 + h:31 * 16 + 9 + h], in1=tmp[ta][:, 0:128],
                                                                               op0=ALU.mult, op1=ALU.add),
                          reads=CONST + (tmpB[ta],), writes=(tmpB[ta],))
                    bb.op("pool", lambda e, ta=ta, h=h: e.tensor_tensor(out=biasS[:, :, h], in0=tmp[ta][:, 128:256],
                                                                        in1=indS_s[:, 1024 + ti * 128:1024 + (ti + 1) * 128], op=ALU.add),
                          reads=CONST + (tmpB[ta],), writes=(biasSB,))

            def near_src(t, q4):
                if t < t0 or t > t1:
                    return None
                o = (t - t0) * 128
                if 0 <= t <= 15:
                    bias = zero_t[:]
                else:
                    side = 0 if t < 0 else 1
                    bias = hmask_s[:, side:side + 1]
                return nearK[:, q4, o:o + 128], nearV[:, q4, o:o + 128], bias, (nearKB, nearVB)

            for br in range(3):
                qcols = [br * 1024 + h * 128 for h in range(8)]
                gcols = [3072 + br * 1024 + h * 128 for h in range(8)]

                def consq(n, bk):
                    qk_post(L, bk, [0, 2, 4][br], br, scr[:, n, :], scrB, 6)

                def consg(n, bk):
                    ta = nxt(tstate, NTMP)
                    bb.op("act", lambda e: e.activation(out=tmp[ta][:], in_=bank[bk][:], func=AF.Exp, scale=-1.0), reads=(bankB[bk],), writes=(tmpB[ta],))
                    bb.op("pool", lambda e: e.tensor_scalar(out=tmp[ta][:], in0=tmp[ta][:], scalar1=1.0, scalar2=None, op0=ALU.add),
                          reads=(tmpB[ta],), writes=(tmpB[ta],))
                    bb.op("dve", lambda e: e.reciprocal(out=tmp[ta][:], in_=tmp[ta][:]), reads=(tmpB[ta],), writes=(tmpB[ta],))
                    bb.op("pool", lambda e: e.tensor_copy(out=scr[:, 8 + n, :], in_=tmp[ta][:]), reads=(tmpB[ta],), writes=(scrB,))

                proj_chunks("Aq", qcols, consq, [4, 5])
                proj_chunks("Aq", gcols, consg, [4, 5])
                for h in range(8):
                    g = h // 4
                    qTh = scr[:, h, :]
                    sgh = scr[:, 8 + h, :]
                    if br == 0:
                        if "attnA" in skip:
                            bb.op("pool", lambda e, h=h: e.memset(big[:, h, :], 0.0), writes=(bigB,))
                        else:
                            attn_A(L, par, seg_s, nblk, h, g, qTh, sgh)
                    elif br == 1:
                        if "attnB" not in skip:
                            attn_B(L, ti, h, g, qTh, sgh, near_src)
                    else:
                        if "attnC" not in skip:
                            attn_C(L, par, seg_s, nblk, ti, h, g, qTh, sgh, near_src)
            for h in range(8):
                bb.op("act", lambda e, h=h: e.activation(out=hT[:, h, :], in_=big[:, h, :], func=AF.Copy), reads=(bigB,), writes=(hTB,))
            wsrc = WSC["Ao"].rearrange("(kc p) n -> p kc n", p=128)
            for half in range(2):
                wi = wload([(0, 8, 512, wsrc[:, :, half * 512:(half + 1) * 512])], wB["Ao"])
                for mm in range(4):
                    m = half * 4 + mm
                    bk = m % 4
                    for kc in range(8):
                        bb.op("pe", lambda e, kc=kc, mm=mm, bk=bk, wi=wi: e.matmul(
                            bank[bk][:], lhsT=wring[wi][:, kc * 512 + mm * 128:kc * 512 + (mm + 1) * 128], rhs=hT[:, kc, :],
                            start=(kc == 0), stop=(kc == 7)), reads=(wringB[wi], hTB), writes=(bankB[bk],), mark=(kc == 7))
                    resid_update(L, s, 1, m, bk)

        def kv_stream(par, seg_s, nblk, brg, blk):
            ki = nxt(kvstate, 2)
            if seg_s:
                ksrc = KT_all_i[blk * 768 + brg * 128:blk * 768 + (brg + 1) * 128, :]
                vsrc = V_all_i[blk * 768 + brg * 128:blk * 768 + (brg + 1) * 128, :]
            else:
                ksrc = KT_P_i[brg * 128:(brg + 1) * 128, :]
                vsrc = V_P_i[brg * 128:(brg + 1) * 128, :]
            bb.dma("sp", kblk[ki][:], ksrc, reads=(inB,), writes=(kblkB[ki],))
            bb.dma("sp", vblk[ki][:], vsrc, reads=(inB,), writes=(vblkB[ki],))
            return ki

        def finalize(bO, bZ, h, sgh, first, zadd=None, mulvec=None):
            tz = nxt(tstate, NTMP)
            if zadd is not None:
                bb.op("dve", lambda e: e.tensor_scalar(out=tmp[tz][:], in0=bank[bZ][:], scalar1=zadd, scalar2=None, op0=ALU.add),
                      reads=(bankB[bZ], miscB), writes=(tmpB[tz],))
                bb.op("dve", lambda e: e.reciprocal(out=tmp[tz][:], in_=tmp[tz][:]), reads=(tmpB[tz],), writes=(tmpB[tz],))
            else:
                bb.op("dve", lambda e: e.reciprocal(out=tmp[tz][:], in_=bank[bZ][:]), reads=(bankB[bZ],), writes=(tmpB[tz],))
            to = nxt(tstate, NTMP)
            bb.op("dve", lambda e: e.tensor_tensor(out=tmp[to][:], in0=bank[bO][:], in1=tmp[tz][:], op=ALU.mult),
                  reads=(bankB[bO], tmpB[tz]), writes=(tmpB[to],))
            if first:
                bb.op("pool", lambda e: e.tensor_tensor(out=big[:, h, :], in0=tmp[to][:], in1=sgh, op=ALU.mult),
                      reads=(tmpB[to], scrB), writes=(bigB,))
            else:
                bb.op("pool", lambda e: e.tensor_tensor(out=tmp[to][:], in0=tmp[to][:], in1=sgh, op=ALU.mult),
                      reads=(tmpB[to], scrB), writes=(tmpB[to],))
                bb.op("pool", lambda e: e.tensor_tensor(out=big[:, h, :], in0=big[:, h, :], in1=tmp[to][:], op=ALU.add),
                      reads=(tmpB[to], bigB), writes=(bigB,))

        def zacc_add(n_, pi):
            if "noacc" in skip:
                return
            if n_ % 4 == 3:
                ek, a, first = "pool", 2, (n_ == 3)
            else:
                j = (n_ // 4) * 3 + (n_ % 4)
                ek, a, first = "dve", j % 2, (j < 2)
            if first:
                bb.op(ek, lambda en: en.tensor_copy(out=zacc[a][:], in_=pT[pi][:]), reads=(pTB[pi],), writes=(zaccB[a],))
            else:
                bb.op(ek, lambda en: en.tensor_tensor(out=zacc[a][:], in0=zacc[a][:], in1=pT[pi][:], op=ALU.add),
                      reads=(pTB[pi], zaccB[a]), writes=(zaccB[a],))

        def z_total(bks, nitems, fold):
            for a, need in ((1, 2), (2, 4)):
                if nitems >= need:
                    bb.op("dve", lambda en, a=a: en.tensor_tensor(out=zacc[0][:], in0=zacc[0][:], in1=zacc[a][:], op=ALU.add),
                          reads=(zaccB[0], zaccB[a]), writes=(zaccB[0],))
            if fold:
                bb.op("dve", lambda en: en.tensor_tensor(out=zacc[0][:, 0:TT], in0=zacc[0][:, 0:TT], in1=zacc[0][:, TT:2 * TT], op=ALU.add),
                      reads=(zaccB[0],), writes=(zaccB[0],))
                halves = ((0, bks[0]),)
            else:
                halves = ((0, bks[0]), (1, bks[1]))
            for m, bk in halves:
                bb.op("pe", lambda en, m=m, bk=bk: en.matmul(bank[bk][:], lhsT=ones_f32[:], rhs=zacc[0][:, m * TT:(m + 1) * TT], start=True, stop=True),
                      reads=(zaccB[0],) + CONST, writes=(bankB[bk],))

        def attn_A(L, par, seg_s, nblk, h, g, qTh, sgh):
            bO, bZ = 2, 3
            SP = (0, 2, 3)
            LA = 2
            nk = nblk * 16
            npair = nk // 2
            kis = {}
            kis[0] = kv_stream(par, seg_s, nblk, g, 0)
            pend = []
            for pr in range(npair):
                kt0 = 2 * pr
                blk, kk = kt0 // 16, kt0 % 16
                if kk == 2 * (LA + 1) and blk + 1 < nblk:
                    kis[blk + 1] = kv_stream(par, seg_s, nblk, g, blk + 1)
                ki = kis[blk]
                dbk = SP[pr % 3]
                for u in range(2):
                    bs = 2 * dbk + u
                    bb.op("pe", lambda e, ki=ki, kk=kk, bs=bs, u=u: e.matmul(bank[bs][:], lhsT=kblk[ki][:, (kk + u) * 128:(kk + u + 1) * 128], rhs=qTh,
                                                                          start=True, stop=True),
                          reads=(kblkB[ki], scrB), writes=(bankB[bs],), mark=(u == 1))
                pi = nxt(pstate, NP_)
                bb.op("act", lambda e, dbk=dbk, pi=pi: e.activation(out=pT[pi][:], in_=bank2[dbk][:], func=AF.Exp, scale=SC_A),
                      reads=(bankB[2 * dbk], bankB[2 * dbk + 1]), writes=(pTB[pi],))
                pend.append((pr, ki, kk, pi))
                if len(pend) > LA:
                    _pv_A(pend.pop(0), nk, bO)
            while pend:
                _pv_A(pend.pop(0), nk, bO)
            z_total((bZ,), npair, True)
            finalize(bO, bZ, h, sgh, True)

        def _pv_A(it, nk, bO):
            pr, ki, kk, pi = it
            for u in range(2):
                kt = 2 * pr + u
                bb.op("pe", lambda e, u=u, kt=kt: e.matmul(bank[bO][:], lhsT=vblk[ki][:, (kk + u) * 128:(kk + u + 1) * 128], rhs=pT[pi][:, u * TT:(u + 1) * TT],
                                                         start=(kt == 0), stop=(kt == nk - 1)),
                      reads=(vblkB[ki], pTB[pi]), writes=(bankB[bO],), mark=(u == 1))
            zacc_add(pr, pi)

        def attn_B(L, ti, h, g, qTh, sgh, near_src):
            bO, bZ = 2, 3
            q4 = g
            for qb in range(4):
                srcs = []
                for dd in range(3):
                    t = 4 * ti + qb + dd - 1
                    r = near_src(t, q4)
                    if r is not None:
                        srcs.append((dd, r))
                bs = qb % 2
                dd0, dd1 = srcs[0][0], srcs[-1][0]
                for dd, r in srcs:
                    bb.op("pe", lambda e, dd=dd, r=r: e.matmul(bank[bs][:, dd * 128:(dd + 1) * 128], lhsT=r[0], rhs=qTh[:, qb * 128:(qb + 1) * 128],
                                                               start=True, stop=True),
                          reads=r[3] + (scrB,), writes=(bankB[bs],), mark=(dd == dd1))
                ta = nxt(tstate, NTMP)
                c0, c1 = dd0 * 128, (dd1 + 1) * 128
                bb.op("dve", lambda e: e.scalar_tensor_tensor(out=tmp[ta][:, c0:c1], in0=bank[bs][:, c0:c1], scalar=SC_A, in1=MBt[:, h, c0:c1],
                                                              op0=ALU.mult, op1=ALU.add), reads=(bankB[bs],) + CONST, writes=(tmpB[ta],))
                pi = nxt(pstate, NP_)
                for n_, (dd, r) in enumerate(srcs):
                    bb.op("act", lambda e, dd=dd, r=r: e.activation(out=pT[pi][:, dd * 128:(dd + 1) * 128], in_=tmp[ta][:, dd * 128:(dd + 1) * 128],
                                                                    func=AF.Exp, bias=r[2], scale=1.0),
                          reads=(tmpB[ta],) + CONST + ((pTB[pi],) if n_ > 0 else ()), writes=(pTB[pi],))
                for dd, r in srcs:
                    bb.op("pe", lambda e, dd=dd, r=r: e.matmul(bank[bO][:, qb * 128:(qb + 1) * 128], lhsT=r[1], rhs=pT[pi][:, dd * 128:(dd + 1) * 128],
                                                               start=(dd == dd0), stop=(dd == dd1)),
                          reads=r[3] + (pTB[pi],), writes=(bankB[bO],), mark=False)
                    bb.op("pe", lambda e, dd=dd: e.matmul(bank[bZ][:, qb * 128:(qb + 1) * 128], lhsT=ones_bf[:], rhs=pT[pi][:, dd * 128:(dd + 1) * 128],
                                                          start=(dd == dd0), stop=(dd == dd1)),
                          reads=(pTB[pi],) + CONST, writes=(bankB[bZ],), mark=(dd == dd1))
            finalize(bO, bZ, h, sgh, False, zadd=esink[:, h:h + 1])

        def attn_C(L, par, seg_s, nblk, ti, h, g, qTh, sgh, near_src):
            bO = (4, 5)
            bZ = (0, 1)
            SP = (0, 1, 3)
            LA = 2
            brg = 4 + g
            far = []
            for kt in range(nblk * 16):
                if seg_s:
                    far.append(kt)
                elif kt < 4 * ti - 1 or kt > 4 * ti + 4:
                    far.append(kt)
            near = []
            for rel in range(-1, 5):
                r = near_src(4 * ti + rel, 2 + g)
                if r is not None:
                    near.append((rel, r))
            ntot = len(far) + len(near)
            kis = {}
            loaded = set()
            pend = []
            idx = 0

            def pv(it):
                n_, lhsV, rdV, pi = it
                for m_ in range(2 if "nopv" not in skip else 0):
                    bb.op("pe", lambda e, m_=m_: e.matmul(bank[bO[m_]][:], lhsT=lhsV, rhs=pT[pi][:, m_ * TT:(m_ + 1) * TT], start=(n_ == 0), stop=(n_ == ntot - 1)),
                          reads=rdV + (pTB[pi],), writes=(bankB[bO[m_]],), mark=(m_ == 1))
                zacc_add(n_, pi)

            if far:
                kis[far[0] // 16] = kv_stream(par, seg_s, nblk, brg, far[0] // 16)
                loaded.add(far[0] // 16)
            for fi, kt in enumerate(far):
                blk, kk = kt // 16, kt % 16
                if kk >= LA and blk + 1 < nblk and (blk + 1) not in loaded and any(f // 16 == blk + 1 for f in far):
                    kis[blk + 1] = kv_stream(par, seg_s, nblk, brg, blk + 1)
                    loaded.add(blk + 1)
                ki = kis[blk]
                dbk = SP[idx % 3]
                for m_ in range(2):
                    bs = 2 * dbk + m_
                    bb.op("pe", lambda e, m_=m_, bs=bs, ki=ki, kk=kk: e.matmul(
                        bank[bs][:], lhsT=kblk[ki][m_ * 64:(m_ + 1) * 64, kk * 128:(kk + 1) * 128], rhs=qTh[m_ * 64:(m_ + 1) * 64, :], start=True, stop=True),
                        reads=(kblkB[ki], scrB), writes=(bankB[bs],), mark=(m_ == 1))
                pi = nxt(pstate, NP_)
                if seg_s:
                    bap = biasS[:, kt, h:h + 1]
                    rd = (biasSB,)
                else:
                    col = (15 if kt < 4 * ti - 1 else 31) * 16 + 8 + h
                    bap = rbrep_s[:, col:col + 1]
                    rd = CONST
                bb.op("act", lambda e, dbk=dbk, pi=pi, bap=bap: e.activation(out=pT[pi][:], in_=bank2[dbk][:], func=AF.Exp, bias=bap, scale=SC_C),
                      reads=(bankB[2 * dbk], bankB[2 * dbk + 1]) + rd, writes=(pTB[pi],))
                pend.append((idx, vblk[ki][:, kk * 128:(kk + 1) * 128], (vblkB[ki],), pi))
                idx += 1
                if len(pend) > LA:
                    pv(pend.pop(0))
            for rel, r in near:
                dbk = SP[idx % 3]
                pi = nxt(pstate, NP_)
                for m_ in range(2):
                    bs = 2 * dbk + m_
                    bb.op("pe", lambda e, m_=m_, bs=bs, r=r: e.matmul(bank[bs][:], lhsT=r[0][m_ * 64:(m_ + 1) * 64, :], rhs=qTh[m_ * 64:(m_ + 1) * 64, :],
                                                                      start=True, stop=True),
                          reads=r[3] + (scrB,), writes=(bankB[bs],))
                    ta = nxt(tstate, NTMP)
                    for qb in range(4):
                        d = rel - qb
                        if abs(d) <= 1:
                            bb.op("dve", lambda e, qb=qb, d=d, bs=bs, ta=ta: e.scalar_tensor_tensor(
                                out=tmp[ta][:, qb * 128:(qb + 1) * 128], in0=bank[bs][:, qb * 128:(qb + 1) * 128], scalar=SC_C,
                                in1=TCt[:, h, (d + 1) * 128:(d + 2) * 128], op0=ALU.mult, op1=ALU.add),
                                reads=(bankB[bs],) + CONST + ((tmpB[ta],) if qb > 0 else ()), writes=(tmpB[ta],))
                        else:
                            col = (15 if d < 0 else 31) * 16 + 8 + h
                            bb.op("dve", lambda e, qb=qb, col=col, bs=bs, ta=ta: e.tensor_scalar(
                                out=tmp[ta][:, qb * 128:(qb + 1) * 128], in0=bank[bs][:, qb * 128:(qb + 1) * 128], scalar1=SC_C,
                                scalar2=rbrep_s[:, col:col + 1], op0=ALU.mult, op1=ALU.add),
                                reads=(bankB[bs],) + CONST + ((tmpB[ta],) if qb > 0 else ()), writes=(tmpB[ta],))
                    bb.op("act", lambda e, ta=ta, pi=pi, r=r, m_=m_: e.activation(out=pT[pi][:, m_ * TT:(m_ + 1) * TT], in_=tmp[ta][:], func=AF.Exp, bias=r[2], scale=1.0),
                          reads=(tmpB[ta],) + CONST + ((pTB[pi],) if m_ == 1 else ()), writes=(pTB[pi],))
                pend.append((idx, r[1], r[3], pi))
                idx += 1
                if len(pend) > LA:
                    pv(pend.pop(0))
            while pend:
                pv(pend.pop(0))
            z_total(bZ, ntot, False)
            tz1 = nxt(tstate, NTMP)
            bb.op("dve", lambda e: e.reciprocal(out=tmp[tz1][:], in_=bank[bZ[0]][:]), reads=(bankB[bZ[0]],), writes=(tmpB[tz1],))
            to1 = nxt(tstate, NTMP)
            bb.op("dve", lambda e: e.tensor_tensor(out=tmp[to1][:], in0=bank[bO[0]][:], in1=tmp[tz1][:], op=ALU.mult),
                  reads=(bankB[bO[0]], tmpB[tz1]), writes=(tmpB[to1],))
            tz2 = nxt(tstate, NTMP)
            bb.op("dve", lambda e: e.reciprocal(out=tmp[tz2][:], in_=bank[bZ[1]][:]), reads=(bankB[bZ[1]],), writes=(tmpB[tz2],))
            bb.op("dve", lambda e: e.scalar_tensor_tensor(out=tmp[tz2][:], in0=bank[bO[1]][:], scalar=neglam[:, 0:1], in1=tmp[tz2][:],
                                                          op0=ALU.mult, op1=ALU.mult),
                  reads=(bankB[bO[1]], tmpB[tz2], miscB), writes=(tmpB[tz2],))
            bb.op("dve", lambda e: e.tensor_tensor(out=tmp[to1][:], in0=tmp[to1][:], in1=tmp[tz2][:], op=ALU.add),
                  reads=(tmpB[to1], tmpB[tz2]), writes=(tmpB[to1],))
            si = nxt(sqstate, 2)
            bb.op("act", lambda e: e.activation(out=sq[si][:], in_=tmp[to1][:], func=AF.Square), reads=(tmpB[to1],), writes=(sqB[si],))
            bb.op("pe", lambda e: e.matmul(bank[2][:], lhsT=ones_bf[:], rhs=sq[si][:], start=True, stop=True), reads=(sqB[si],) + CONST, writes=(bankB[2],))
            rstd_from_bank(2, 128.0)
            bb.op("dve", lambda e: e.scalar_tensor_tensor(out=tmp[to1][:], in0=tmp[to1][:], scalar=gvec_s[:, L * 8 + 6:L * 8 + 7], in1=rstd[:],
                                                          op0=ALU.mult, op1=ALU.mult), reads=(tmpB[to1], rstdB) + CONST, writes=(tmpB[to1],))
            bb.op("dve", lambda e: e.scalar_tensor_tensor(out=tmp[to1][:], in0=tmp[to1][:], scalar=laminit_s[:, 1:2], in1=sgh, op0=ALU.mult, op1=ALU.mult),
                  reads=(tmpB[to1], scrB) + CONST, writes=(tmpB[to1],))
            bb.op("pool", lambda e: e.tensor_tensor(out=big[:, h, :], in0=big[:, h, :], in1=tmp[to1][:], op=ALU.add), reads=(tmpB[to1], bigB), writes=(bigB,))

        for i in range(ntiles):
            s = 0 if i < 4 else 1
            if kind == "first":
                xt = big[:].rearrange("p a t -> p (a t)").rearrange("p (a f) -> p a f", a=4)
                bb.dma("sp", xt, xin[i * TT:(i + 1) * TT, :].rearrange("(a p) f -> p a f", p=128), writes=(bigB,))
                for kc in range(8):
                    bk = kc % 4
                    for a in range(4):
                        bb.op("pe", lambda e, kc=kc, a=a, bk=bk: e.transpose(bank[bk][:, a * 128:(a + 1) * 128], xt[:, a, kc * 128:(kc + 1) * 128], ident[:]),
                              reads=(bigB,) + CONST, writes=(bankB[bk],), mark=(a == 3))
                    if kc % 2 == 0:
                        bb.op("act", lambda e, kc=kc, bk=bk: e.activation(out=xT[:, kc, :], in_=bank[bk][:], func=AF.Copy), reads=(bankB[bk],), writes=(xTB,))
                    else:
                        bb.op("dve", lambda e, kc=kc, bk=bk: e.tensor_copy(out=xT[:, kc, :], in_=bank[bk][:]), reads=(bankB[bk], xTB), writes=(xTB,))
            else:
                bb.dma("sp", xT[:], xT_i[i].rearrange("(kc p) t -> p kc t", p=128), reads=(inB,), writes=(xTB,))
            if has_attn:
                attn_part(0, i)
                ffn(0, 1, s, 2)
            if has_kv:
                if 'ffn' not in skip:
                    ffn(1, 0, s, 0)
                if 'kv' not in skip:
                    kv_part(1, i)
                bb.dma("pool", xT_o[i].rearrange("(kc p) t -> p kc t", p=128), xT[:], reads=(xTB,), writes=(outB,), partial=True)
            else:
                yt = big[:].rearrange("p a t -> p (a t)").rearrange("p (a f) -> p a f", a=4)
                for a in range(4):
                    for half in range(2):
                        bk = (a * 2 + half) % 4
                        for kk in range(4):
                            kc = half * 4 + kk
                            bb.op("pe", lambda e, a=a, kk=kk, kc=kc, bk=bk: e.transpose(bank[bk][:, kk * 128:(kk + 1) * 128], xT[:, kc, a * 128:(a + 1) * 128], ident[:]),
                                  reads=(xTB,) + CONST, writes=(bankB[bk],), mark=(kk == 3))
                        if half == 0:
                            bb.op("act", lambda e, a=a, bk=bk: e.activation(out=yt[:, a, 0:512], in_=bank[bk][:], func=AF.Copy), reads=(bankB[bk],), writes=(bigB,))
                        else:
                            bb.op("dve", lambda e, a=a, bk=bk: e.tensor_copy(out=yt[:, a, 512:1024], in_=bank[bk][:]), reads=(bankB[bk], bigB), writes=(bigB,))
                bb.dma("pool", y[i * TT:(i + 1) * TT, :].rearrange("(a p) f -> p a f", p=128), yt, reads=(bigB,), writes=(outB,), partial=True)
        for qk in ("sp", "pool"):
            Q = bb.E[qk]
            for sdm in bb.pools[qk][0]:
                if sdm.val > 0:
                    Q.eng.wait_ge(sdm.h, sdm.val)
    return nc


def _host_consts(core):
    c = {}
    c["ident"] = np.eye(128, dtype=np.float32)
    P = np.zeros((128, 128), np.float32)
    for i in range(128):
        blk = (i % 64) // 32
        j = i + 32 if blk == 0 else i - 32
        P[j, i] = 1.0
    c["perm"] = P
    pos = np.concatenate([core * 2048 + np.arange(2048), np.arange(2048)])
    row = (pos // 64).astype(np.float32)
    col = (pos % 64).astype(np.float32)
    inv = (np.float32(10000.0) ** (-np.arange(32, dtype=np.float32) / np.float32(32))).astype(np.float32)
    ang_r = row[:, None] * inv
    ang_c = col[:, None] * inv
    ang = np.concatenate([ang_r, ang_r, ang_c, ang_c], axis=-1).astype(np.float32)
    cos = np.cos(ang).astype(np.float32).T
    sin = np.sin(ang).astype(np.float32).T
    sign = np.ones((128, 1), np.float32)
    for i in range(128):
        if (i % 64) // 32 == 0:
            sign[i] = -1.0
    c["ropeT"] = np.ascontiguousarray(np.stack([cos, sin * sign]))
    oh = np.zeros((33, 512), np.float32)
    for j in range(511):
        rel = 255 - j
        oh[int(t5_bucket_np(rel)), j] = 1.0
        oh[32, j] = NEGM if abs(rel) > 128 else 0.0
    c["ohrev"] = oh
    indm = np.zeros((4, 128), np.float32); indp = np.zeros((4, 128), np.float32); nearm = np.zeros((4, 128), np.float32)
    for ti in range(4):
        Q0 = core * 16 + 4 * ti
        for kt in range(128):
            if kt < Q0 - 1:
                indm[ti, kt] = 1.0
            elif kt > Q0 + 4:
                indp[ti, kt] = 1.0
            else:
                nearm[ti, kt] = NEGM
    ind = np.concatenate([indm.reshape(-1), indp.reshape(-1), nearm.reshape(-1)])
    c["indS"] = np.ascontiguousarray(np.broadcast_to(ind[None, :], (128, 1536))).astype(np.float32)
    hm = np.array([NEGM if core == 0 else 0.0, NEGM if core == 7 else 0.0], np.float32)
    c["hmask"] = np.ascontiguousarray(np.broadcast_to(hm[None, :], (128, 2))).astype(np.float32)
    return c


def build_mod():
    nc = bass.Bass("TRN2", target_bir_lowering=False)
    cT9 = nc.dram_tensor("cT9", [128, 72], F32, kind="ExternalInput").ap()
    wsl = nc.dram_tensor("wada_sl", [4, D, 1152], F32, kind="ExternalInput").ap()
    bada9 = nc.dram_tensor("bada9", [9, 4 * 1152], F32, kind="ExternalInput").ap()
    modp = nc.dram_tensor("modp", [9, 4 * 1152], F32, kind="ExternalOutput").ap()
    from contextlib import ExitStack
    with ExitStack() as st:
        bb = B(nc, st)
        sbt = lambda n, shp, dt: st.enter_context(nc.sbuf_tensor(n, list(shp), dt))
        c_s = sbt("c_s", [128, 72], F32); sc_s = sbt("sc_s", [128, 72], F32)
        b_s = sbt("b_s", [9, 4 * 1152], F32); o_s = sbt("o_s", [9, 4 * 1152], F32)
        wt = [sbt("wt%d" % i, [128, 8, 512], F32) for i in range(2)]
        wtB = [Buf("wt%d" % i) for i in range(2)]
        ps = [st.enter_context(nc.psum_tensor("ps%d" % i, [128, 512], F32)) for i in range(2)]
        psB = [Buf("ps%d" % i) for i in range(2)]
        cB, oB = Buf("c"), Buf("o")
        bb.dma("sp", c_s[:], cT9, writes=(cB,))
        bb.dma("sp", b_s[:], bada9, writes=(cB,), partial=True)
        bb.op("act", lambda e: e.activation(out=sc_s[:], in_=c_s[:], func=AF.Silu), reads=(cB,), writes=(cB,))
        n = 0
        for l in range(4):
            for (c0, w) in ((0, 512), (512, 512), (1024, 128)):
                k = n % 2
                n += 1
                bb.dma("sp", wt[k][:, :, 0:w], wsl[l].rearrange("(kc p) n -> p kc n", p=128)[:, :, c0:c0 + w], writes=(wtB[k],))
                for kc in range(8):
                    bb.op("pe", lambda e, kc=kc, k=k, w=w: e.matmul(ps[k][0:9, 0:w], lhsT=sc_s[:, kc * 9:(kc + 1) * 9], rhs=wt[k][:, kc, 0:w],
                                                                   start=(kc == 0), stop=(kc == 7)),
                          reads=(wtB[k], cB), writes=(psB[k],), mark=(kc == 7))
                o = l * 1152 + c0
                bb.op("dve", lambda e, k=k, w=w, o=o: e.tensor_tensor(out=o_s[:, o:o + w], in0=ps[k][0:9, 0:w], in1=b_s[:, o:o + w], op=ALU.add),
                      reads=(psB[k], cB), writes=(oB,))
        bb.dma("sp", modp, o_s[:], reads=(oB,), writes=(Buf("out"),))
        for sdm in bb.pools["sp"][0]:
            if sdm.val > 0:
                nc.sync.wait_ge(sdm.h, sdm.val)
    return nc


_NC_CACHE = {}


def _get_nc(kind):
    if kind not in _NC_CACHE:
        _NC_CACHE[kind] = build_mod() if kind == "mod" else build(kind)
    return _NC_CACHE[kind]


def _run(kind, maps):
    res = run_bass_kernel_spmd(_get_nc(kind), maps, core_ids=list(range(NCORE)))
    return res.results


def kernel(**inputs):
    import ml_dtypes
    bf = ml_dtypes.bfloat16
    f = lambda a: np.ascontiguousarray(np.asarray(a, dtype=np.float32))
    inp = {k: f(v) for k, v in inputs.items()}
    xs, xp = inp["x_sample"], inp["x_prompt"]
    cs, cp = inp["c_sample"], inp["c_prompt"]
    c9 = np.concatenate([cs, cp], axis=0)
    cT9 = np.ascontiguousarray(c9.reshape(9, 8, 128).transpose(2, 1, 0).reshape(128, 72))
    maps = []
    for core in range(NCORE):
        sl = slice(core * 1152, (core + 1) * 1152)
        maps.append({"cT9": cT9,
                     "wada_sl": np.ascontiguousarray(inp["w_ada"][:, :, sl]),
                     "bada9": np.ascontiguousarray(np.broadcast_to(inp["b_ada"][:, sl].reshape(1, 4 * 1152), (9, 4 * 1152)))})
    r = _run("mod", maps)
    mod = np.zeros((4, 9, 9 * D), np.float32)
    for core in range(NCORE):
        mp = np.asarray(r[core]["modp"], np.float32).reshape(9, 4, 1152)
        mod[:, :, core * 1152:(core + 1) * 1152] = mp.transpose(1, 0, 2)
    consts = [_host_consts(core) for core in range(NCORE)]
    rb = inp["rel_bias"]
    flag = np.concatenate([np.ones(8, np.float32), np.zeros(8, np.float32)])[None, :]
    rb33 = np.ascontiguousarray(np.concatenate([rb, flag], axis=0))
    rbrep = np.ascontiguousarray(np.broadcast_to(rb.reshape(1, 512), (128, 512)))

    def layer_small(la, lb, core):
        m = {}
        mt = np.zeros((128, 2, 72, 2), np.float32)
        for slot, l in ((0, la), (1, lb)):
            for s_, seq in ((0, 0), (1, 1 + core)):
                mt[:, slot, :, s_] = mod[l, seq].reshape(72, 128).T
        m["modT"] = mt.reshape(128, 288)
        gn = np.zeros((128, 2, 3, 8), np.float32)
        gv = np.zeros((128, 2, 8), np.float32)
        for slot, l in ((0, la), (1, lb)):
            gn[:, slot] = inp["g_norm"][l].reshape(3, 8, 128).transpose(2, 0, 1)
            gv[:, slot, 0] = inp["g_qa"][l]; gv[:, slot, 1] = inp["g_ka"][l]
            gv[:, slot, 2] = inp["g_qb"][l]; gv[:, slot, 3] = inp["g_kb"][l]
            gv[:, slot, 4] = np.tile(inp["g_qc"][l], 2); gv[:, slot, 5] = np.tile(inp["g_kc"][l], 2)
            gv[:, slot, 6] = inp["g_subln"][l]
        m["g_normT"] = gn.reshape(128, 48)
        m["gvec"] = gv.reshape(128, 16)
        m["sinkrep"] = np.ascontiguousarray(np.broadcast_to(inp["sink"][la].reshape(1, 8), (128, 8)))
        lam = np.concatenate([inp["lam_q1"][la], inp["lam_k1"][la], inp["lam_q2"][la], inp["lam_k2"][la]])
        m["lamrep"] = np.ascontiguousarray(np.broadcast_to(lam.reshape(1, 256), (128, 256)))
        li = 0.8 - 0.6 * math.exp(-0.3 * la)
        m["laminit"] = np.ascontiguousarray(np.broadcast_to(np.array([[li, 1.0 - li]], np.float32), (128, 2)))
        m["rb33"] = rb33
        m["rbrep"] = rbrep
        m.update(consts[core])
        return m

    win = inp["w_in"]

    def wA(l):
        q = np.concatenate([win[l][:, QBASE[b]:QBASE[b] + 1024] for b in range(3)] + [win[l][:, GBASE[b]:GBASE[b] + 1024] for b in range(3)], axis=1)
        return {"wAq": np.ascontiguousarray(q), "wAo": inp["w_o"][l], "wAfi": inp["w_ff_in"][l, 1], "wAfo": inp["w_ff_out"][l, 1]}

    def wBk(l):
        kv = np.concatenate([win[l][:, KBASE[b]:KBASE[b] + 256] for b in range(3)] + [win[l][:, VBASE[b]:VBASE[b] + 256] for b in range(3)], axis=1)
        return {"wBkv": np.ascontiguousarray(kv), "wBfi": inp["w_ff_in"][l, 0], "wBfo": inp["w_ff_out"][l, 0]}

    depth = _DEPTH[0]
    state = None
    for p in range(depth + 1):
        kind = "first" if p == 0 else ("last" if p == depth else "mid")
        la, lb = max(p - 1, 0), min(p, 3)
        shared = {}
        if p > 0:
            shared.update(wA(la))
        if p < depth:
            shared.update(wBk(lb))
        if p > 0:
            KT_all = np.ascontiguousarray(np.concatenate([state[c]["KT_S_o"] for c in range(NCORE)], axis=0))
            V_all = np.ascontiguousarray(np.concatenate([state[c]["V_S_o"] for c in range(NCORE)], axis=0))
            shared["KT_all_i"] = KT_all
            shared["V_all_i"] = V_all
        maps = []
        for core in range(NCORE):
            m = dict(shared)
            m.update(layer_small(la, lb, core))
            if p == 0:
                m["xin"] = np.ascontiguousarray(np.concatenate([xs[0, core * 2048:(core + 1) * 2048], xp[core]], axis=0))
            else:
                m["xT_i"] = state[core]["xT_o"]
                m["KT_P_i"] = state[core]["KT_P_o"]
                m["V_P_i"] = state[core]["V_P_o"]
                for nm, key in (("KT_Sx", "KT_S_o"), ("V_Sx", "V_S_o")):
                    ext = np.zeros((768, 18 * 128), bf)
                    ext[:, 128:2176] = state[core][key]
                    if core > 0:
                        ext[:, 0:128] = state[core - 1][key][:, 1920:2048]
                    if core < NCORE - 1:
                        ext[:, 2176:2304] = state[core + 1][key][:, 0:128]
                    m[nm] = ext
            maps.append(m)
        r = _run(kind, maps)
        state = [{k: np.asarray(v) for k, v in r[c].items()} for c in range(NCORE)]
    y_prompt = np.zeros((8, 2048, D), np.float32)
    y_sample = np.zeros((1, 16384, D), np.float32)
    for core in range(NCORE):
        yy = np.asarray(state[core]["y"], dtype=np.float32)
        y_sample[0, core * 2048:(core + 1) * 2048] = yy[:2048]
        y_prompt[core] = yy[2048:]
    return (y_prompt, y_sample)


_DEPTH = [4]
```

```python
import math
import numpy as np
import concourse.bass as bass
import concourse.mybir as mybir
from concourse.bass_utils import run_bass_kernel_spmd

F32 = mybir.dt.float32
BF16 = mybir.dt.bfloat16
AF = mybir.ActivationFunctionType
ALU = mybir.AluOpType

D = 1024
DFF = 2816
NCORE = 8
TT = 512
NT = 8
EPS = 1e-6
NEGM = -30000.0
QBASE = [0, 1536, 3072]
KBASE = [1024, 2560, 4096]
VBASE = [1280, 2816, 4352]
GBASE = [4608, 5632, 6656]
SC_A = 128.0 ** -0.5
SC_C = 64.0 ** -0.5


class Buf:
    __slots__ = ("name", "w", "r")

    def __init__(self, name):
        self.name = name
        self.w = {}
        self.r = {}


class Eng:
    def __init__(self, key, eng, sem):
        self.key = key
        self.eng = eng
        self.sem = sem
        self.cnt = 0
        self.waited = {}


class DSem:
    def __init__(self, key, h):
        self.key = key
        self.h = h
        self.val = 0


class B:
    def __init__(self, nc, stack):
        self.nc = nc
        self.semh = {}
        self.E = {}
        for key, eng in (("pe", nc.tensor), ("act", nc.scalar), ("dve", nc.vector), ("pool", nc.gpsimd)):
            h = stack.enter_context(nc.semaphore("s_" + key))
            self.semh[key] = h
            self.E[key] = Eng(key, eng, h)
        self.E["sp"] = Eng("sp", nc.sync, None)
        self.pools = {}
        for q, n in (("sp", 20), ("pool", 20)):
            lst = []
            for i in range(n):
                k = "d_%s_%d" % (q, i)
                h = stack.enter_context(nc.semaphore(k))
                self.semh[k] = h
                lst.append(DSem(k, h))
            self.pools[q] = [lst, 0]
        k = "cc"
        h = stack.enter_context(nc.semaphore(k))
        self.semh[k] = h
        self.ccsem = DSem(k, h)

    def _deps(self, E, reads, writes, partial):
        deps = {}
        for b in reads:
            for k, v in b.w.items():
                if deps.get(k, 0) < v:
                    deps[k] = v
        for b in writes:
            for k, v in b.r.items():
                if deps.get(k, 0) < v:
                    deps[k] = v
            if not partial:
                for k, v in b.w.items():
                    if deps.get(k, 0) < v:
                        deps[k] = v
        for k, v in deps.items():
            if E.waited.get(k, 0) >= v:
                continue
            if k == E.key and k == "pe":
                continue
            E.eng.wait_ge(self.semh[k], v)
            E.waited[k] = v

    def _upd(self, key, tok, reads, writes, partial):
        for b in reads:
            if b.r.get(key, 0) < tok:
                b.r[key] = tok
        for b in writes:
            if partial:
                if b.w.get(key, 0) < tok:
                    b.w[key] = tok
            else:
                b.w = {key: tok}
                b.r = {}

    def op(self, ek, fn, reads=(), writes=(), mark=True):
        E = self.E[ek]
        pr = tuple(b for b in reads if b.name.startswith("bank") and b not in writes)
        if pr:
            writes = tuple(writes) + pr
        self._deps(E, reads, writes, False)
        ins = fn(E.eng)
        if mark:
            E.cnt += 1
            ins.then_inc(E.sem, 1)
            tok = E.cnt
        else:
            tok = E.cnt + 1
        self._upd(E.key, tok, reads, writes, False)
        return ins

    def dma(self, qk, out, in_, reads=(), writes=(), partial=False):
        Q = self.E[qk]
        pool = self.pools[qk]
        s = pool[0][pool[1]]
        pool[1] = (pool[1] + 1) % len(pool[0])
        if s.val > 0 and Q.waited.get(s.key, 0) < s.val:
            Q.eng.wait_ge(s.h, s.val)
            Q.waited[s.key] = s.val
        self._deps(Q, reads, writes, partial)
        ins = Q.eng.dma_start(out=out, in_=in_)
        s.val += 16
        ins.then_inc(s.h, 16)
        self._upd(s.key, s.val, reads, writes, partial)
        return ins

    def collective(self, fn, reads=(), writes=()):
        Q = self.E["pool"]
        s = self.ccsem
        if s.val > 0 and Q.waited.get(s.key, 0) < s.val:
            Q.eng.wait_ge(s.h, s.val)
            Q.waited[s.key] = s.val
        self._deps(Q, reads, writes, False)
        ins = fn(Q.eng)
        s.val += 16
        ins.then_inc(s.h, 16)
        self._upd(s.key, s.val, reads, writes, False)

    def wait_all(self, ek, bufs):
        E = self.E[ek]
        self._deps(E, bufs, (), False)


def t5_bucket_np(rel):
    rel = np.asarray(rel, np.int64)
    half, max_exact = 16, 8
    ret = np.where(rel > 0, half, 0)
    n = np.abs(rel)
    lg = np.log(np.maximum(n, 1).astype(np.float32) / np.float32(max_exact)).astype(np.float32)
    large = max_exact + (lg / np.float32(math.log(128 / 8)) * np.float32(half - max_exact)).astype(np.int32)
    large = np.minimum(large, half - 1)
    return ret + np.where(n < max_exact, n, large)


_OPTS = {"skip": (), "ntiles": NT}


def build(kind, ntiles=None):
    skip = _OPTS["skip"]
    ntiles = _OPTS["ntiles"] if ntiles is None else ntiles
    from contextlib import ExitStack
    nc = bass.Bass("TRN2", target_bir_lowering=False)

    def din(name, shape, dt=F32):
        return nc.dram_tensor(name, list(shape), dt, kind="ExternalInput").ap()

    def dscr(name, shape, dt):
        return nc.dram_tensor(name, list(shape), dt).ap()

    has_attn = kind in ("mid", "last")
    has_kv = kind in ("first", "mid")
    if kind == "first":
        xin = din("xin", [4096, D])
    else:
        xT_i = din("xT_i", [NT, D, TT])
    if has_kv:
        xT_o = nc.dram_tensor("xT_o", [NT, D, TT], F32, kind="ExternalOutput").ap()
        KT_S_o = nc.dram_tensor("KT_S_o", [768, 2048], BF16, kind="ExternalOutput").ap()
        V_S_o = nc.dram_tensor("V_S_o", [768, 2048], BF16, kind="ExternalOutput").ap()
        KT_P_o = nc.dram_tensor("KT_P_o", [768, 2048], BF16, kind="ExternalOutput").ap()
        V_P_o = nc.dram_tensor("V_P_o", [768, 2048], BF16, kind="ExternalOutput").ap()
    else:
        y = nc.dram_tensor("y", [4096, D], F32, kind="ExternalOutput").ap()
    WIN = {}
    if has_attn:
        WIN["Aq"] = (din("wAq", [D, 6144]), [D, 6144])
        WIN["Ao"] = (din("wAo", [D, D]), [D, D])
        WIN["Afi"] = (din("wAfi", [D, 2 * DFF]), [D, 2 * DFF])
        WIN["Afo"] = (din("wAfo", [DFF, D]), [DFF, D])
        KT_all_i = din("KT_all_i", [8 * 768, 2048], BF16)
        V_all_i = din("V_all_i", [8 * 768, 2048], BF16)
        KT_P_i = din("KT_P_i", [768, 2048], BF16)
        V_P_i = din("V_P_i", [768, 2048], BF16)
        KT_Sx = din("KT_Sx", [768, 18 * 128], BF16)
        V_Sx = din("V_Sx", [768, 18 * 128], BF16)
    if has_kv:
        WIN["Bfi"] = (din("wBfi", [D, 2 * DFF]), [D, 2 * DFF])
        WIN["Bfo"] = (din("wBfo", [DFF, D]), [DFF, D])
        WIN["Bkv"] = (din("wBkv", [D, 1536]), [D, 1536])
    modT_i = din("modT", [128, 2 * 72 * 2])
    g_normT = din("g_normT", [128, 48])
    gvec = din("gvec", [128, 16])
    sinkrep = din("sinkrep", [128, 8])
    lamrep = din("lamrep", [128, 256])
    laminit = din("laminit", [128, 2])
    rb33 = din("rb33", [33, 16])
    rbrep = din("rbrep", [128, 512])
    ohrev = din("ohrev", [33, 512])
    ident_i = din("ident", [128, 128])
    perm_i = din("perm", [128, 128])
    ropeT = din("ropeT", [2, 128, 4096])
    indS = din("indS", [128, 3 * 4 * 128])
    hmask = din("hmask", [128, 2])
    TSHP = {"Afi": [11, 128, 4096], "Bfi": [11, 128, 4096], "Aq": [12, 128, 4096], "Ao": [2, 128, 4096],
            "Bkv": [2, 128, 4096], "Afo": [DFF, D], "Bfo": [DFF, D]}
    WSC = {k: dscr("w%s_b" % k, TSHP[k], BF16) for k in WIN}
    if "Bkv" in WIN:
        WSC["Bv"] = dscr("wBv_b", [3, 128, 2048], BF16)
    G_d = dscr("G_d", [16, 128, 512], F32)

    with ExitStack() as st:
        bb = B(nc, st)

        def sb(name, shape, dt):
            return st.enter_context(nc.sbuf_tensor(name, list(shape), dt))

        bank2 = [st.enter_context(nc.psum_tensor("bankp%d" % i, [128, 1024], F32)) for i in range(4)]
        bank = []
        for i in range(4):
            bank.append(bank2[i][:, 0:512])
            bank.append(bank2[i][:, 512:1024])
        bankB = [Buf("bank%d" % i) for i in range(8)]

        xT = sb("xT", [128, 8, TT], F32); xTB = Buf("xT")
        hT = sb("hT", [128, 8, TT], BF16); hTB = Buf("hT")
        scr = sb("scr", [128, 22, TT], BF16); scrB = Buf("scr")
        big = sb("big", [128, 8, TT], F32); bigB = Buf("big")
        NW = 3
        wring = [sb("wring%d" % i, [128, 4096], BF16) for i in range(NW)]
        wringB = [Buf("wring%d" % i) for i in range(NW)]
        wstate = [0]
        kvstate = [0]
        kblk = [sb("kblk%d" % i, [128, 2048], BF16) for i in range(2)]
        kblkB = [Buf("kblk%d" % i) for i in range(2)]
        vblk = [sb("vblk%d" % i, [128, 2048], BF16) for i in range(2)]
        vblkB = [Buf("vblk%d" % i) for i in range(2)]
        nearK = sb("nearK", [128, 4, 768], BF16); nearKB = Buf("nearK")
        nearV = sb("nearV", [128, 4, 768], BF16); nearVB = Buf("nearV")
        NP_ = 4
        pT = [sb("pT%d" % i, [128, 2 * TT], BF16) for i in range(NP_)]
        pTB = [Buf("pT%d" % i) for i in range(NP_)]
        pstate = [0]
        NTMP = 4
        tmp = [sb("tmp%d" % i, [128, TT], F32) for i in range(NTMP)]
        tmpB = [Buf("tmp%d" % i) for i in range(NTMP)]
        tstate = [0]
        zacc = [sb("zacc%d" % e, [128, 2 * TT], F32) for e in range(3)]
        zaccB = [Buf("zacc%d" % e) for e in range(3)]
        ones_f32 = sb("ones_f32", [128, 128], F32)
        sq = [sb("sq%d" % i, [128, TT], BF16) for i in range(2)]
        sqB = [Buf("sq%d" % i) for i in range(2)]
        sqstate = [0]
        rstd = sb("rstd", [128, TT], F32); rstdB = Buf("rstd")
        qraw = sb("qraw", [128, TT], F32); qrawB = Buf("qraw")
        qn32 = sb("qn32", [128, TT], F32); qn32B = Buf("qn32")
        qnb = sb("qnb", [128, TT], BF16); qnbB = Buf("qnb")
        ropeC = sb("ropeC", [128, TT], F32); ropeS = sb("ropeS", [128, TT], F32); ropeB = Buf("rope")
        MBt = sb("MBt", [128, 8, 384], F32)
        TCt = sb("TCt", [128, 8, 384], F32)
        biasS = sb("biasS", [128, 128, 8], F32); biasSB = Buf("biasS")
        vtok = [sb("vtok%d" % i, [128, 768], BF16) for i in range(2)]
        vtokB = [Buf("vtok%d" % i) for i in range(2)]
        kst = [sb("kst%d" % i, [128, TT], BF16) for i in range(2)]
        kstB = [Buf("kst%d" % i) for i in range(2)]
        constB = Buf("const")
        ident = sb("ident_s", [128, 128], F32)
        ones_bf = sb("ones_bf", [128, 128], BF16)
        onesC_bf = sb("onesC_bf", [128, 128], BF16)
        perm_bf = sb("perm_bf", [128, 128], BF16)
        eps_t = sb("eps_t", [128, 1], F32)
        zero_t = sb("zero_t", [128, 1], F32)
        g_normT_s = sb("g_normT_s", [128, 48], F32)
        gvec_s = sb("gvec_s", [128, 16], F32)
        sink_s = sb("sink_s", [128, 8], F32)
        esink = sb("esink", [128, 8], F32)
        lam_s = sb("lam_s", [128, 256], F32)
        laminit_s = sb("laminit_s", [128, 2], F32)
        neglam = sb("neglam", [128, 1], F32)
        lamtmp = sb("lamtmp", [128, 64], F32)
        lamacc = sb("lamacc", [128, 8], F32)
        rb33_s = sb("rb33_s", [33, 16], F32)
        rbrep_s = sb("rbrep_s", [128, 512], F32)
        ohrev_s = sb("ohrev_s", [33, 512], F32)
        lhsG = sb("lhsG", [33, 128], F32)
        ones33 = sb("ones33", [33, 128], F32)
        indS_s = sb("indS_s", [128, 3 * 512], F32)
        hmask_s = sb("hmask_s", [128, 2], F32)
        modT = sb("modT_s", [128, 2, 72, 2], F32); modB = Buf("modT")
        Avec = sb("Avec", [128, 2 * 2 * 3 * 8], F32)
        Gvec = sb("Gvec", [128, 2 * 2 * 3 * 8], F32)
        miscB = Buf("misc")

        def nxt(state, n):
            i = state[0]
            state[0] = (i + 1) % n
            return i

        wB = {}
        inB = Buf("inputs")
        outB = Buf("outputs")
        GdB = Buf("G_d")

        def ld(dst, src, q="sp"):
            bb.dma(q, dst, src, reads=(), writes=(constB,), partial=True)

        ld(ident[:], ident_i)
        ld(modT[:].rearrange("p a f s -> p (a f s)"), modT_i); ld(g_normT_s[:], g_normT); ld(gvec_s[:], gvec)
        ld(sink_s[:], sinkrep); ld(lam_s[:], lamrep); ld(laminit_s[:], laminit)
        ld(rb33_s[:], rb33); ld(rbrep_s[:], rbrep); ld(ohrev_s[:], ohrev)
        ld(indS_s[:], indS); ld(hmask_s[:], hmask)
        ld(perm_bf[:], perm_i, q="pool")
        c2B = Buf("const2")
        bb.op("dve", lambda e: e.memset(ones_bf[:], 1.0), writes=(c2B,))
        bb.op("dve", lambda e: e.memset(ones_f32[:], 1.0), writes=(c2B,))
        bb.op("dve", lambda e: e.memset(onesC_bf[:], 0.0), writes=(c2B,))
        bb.op("dve", lambda e: e.memset(onesC_bf[0:64, 0:64], 1.0), writes=(c2B,))
        bb.op("dve", lambda e: e.memset(onesC_bf[64:128, 64:128], 1.0), writes=(c2B,))
        bb.op("dve", lambda e: e.memset(eps_t[:], EPS), writes=(c2B,))
        bb.op("dve", lambda e: e.memset(zero_t[:], 0.0), writes=(c2B,))
        bb.op("dve", lambda e: e.memset(ones33[:], 1.0), writes=(c2B,))
        CONST = (constB, c2B)

        order = [k for k in ("Bfi", "Bfo", "Bkv", "Aq", "Ao", "Afi", "Afo") if k in WIN]
        if has_attn:
            order = [k for k in ("Aq", "Ao", "Afi", "Afo", "Bfi", "Bfo", "Bkv") if k in WIN]
        for k in (order if 'cast' not in skip else []):
            src, shp = WIN[k]
            wb_ = wB.setdefault(k, Buf("w" + k))
            srcv = src.rearrange("(kc p) n -> p kc n", p=128) if shp[0] == D else None
            if k in ("Afo", "Bfo"):
                r0 = 0
                while r0 < shp[0]:
                    r1 = min(r0 + 128, shp[0])
                    bb.dma("pool", WSC[k][r0:r1, :], src[r0:r1, :], writes=(wb_,), partial=True)
                    r0 = r1
            elif k in ("Afi", "Bfi"):
                for g in range(11):
                    for part in range(2):
                        c0 = part * DFF + g * 256
                        dst = WSC[k][g][:, part * 2048:(part + 1) * 2048].rearrange("p (a b) -> p a b", a=8)
                        bb.dma("pool", dst, srcv[:, :, c0:c0 + 256], writes=(wb_,), partial=True)
            elif k == "Ao":
                for half in range(2):
                    dst = WSC[k][half].rearrange("p (a b) -> p a b", a=8)
                    bb.dma("pool", dst, srcv[:, :, half * 512:(half + 1) * 512], writes=(wb_,), partial=True)
            else:
                nch = 48 if k == "Aq" else 6
                for ch in range(nch):
                    g, ci = ch // 4, ch % 4
                    dst = WSC[k][g][:, ci * 1024:(ci + 1) * 1024].rearrange("p (a b) -> p a b", a=8)
                    bb.dma("pool", dst, srcv[:, :, ch * 128:(ch + 1) * 128], writes=(wb_,), partial=True)
                if k == "Bkv":
                    for j in range(3):
                        dst = WSC["Bv"][j].rearrange("p (a b) -> p a b", a=8)
                        bb.dma("pool", dst, srcv[:, :, 768 + j * 256:768 + (j + 1) * 256], writes=(wb_,), partial=True)

        for l in range(2):
            for s in range(2):
                for j in range(3):
                    o = ((l * 2 + s) * 3 + j) * 8
                    bb.op("dve", lambda e, l=l, s=s, j=j, o=o: e.scalar_tensor_tensor(
                        out=Avec[:, o:o + 8], in0=modT[:, l, (3 * j + 1) * 8:(3 * j + 2) * 8, s], scalar=1.0,
                        in1=g_normT_s[:, (l * 3 + j) * 8:(l * 3 + j) * 8 + 8], op0=ALU.add, op1=ALU.mult),
                        reads=CONST, writes=(miscB,))
                    bb.op("dve", lambda e, l=l, s=s, j=j, o=o: e.tensor_scalar(
                        out=Gvec[:, o:o + 8], in0=modT[:, l, (3 * j + 2) * 8:(3 * j + 3) * 8, s],
                        scalar1=(1.0 if j == 1 else 0.5), scalar2=None, op0=ALU.mult),
                        reads=CONST, writes=(miscB,))
        for m in range(2):
            a0 = 2 * m * 64
            bb.op("dve", lambda e, a0=a0: e.tensor_tensor(out=lamtmp[:], in0=lam_s[:, a0:a0 + 64], in1=lam_s[:, a0 + 64:a0 + 128], op=ALU.mult),
                  reads=CONST + (miscB,), writes=(miscB,))
            bb.op("dve", lambda e, m=m: e.tensor_reduce(out=lamacc[:, m:m + 1], in_=lamtmp[:], axis=mybir.AxisListType.X, op=ALU.add),
                  reads=(miscB,), writes=(miscB,))
        bb.op("act", lambda e: e.activation(out=lamacc[:, 2:4], in_=lamacc[:, 0:2], func=AF.Exp), reads=(miscB,), writes=(miscB,))
        bb.op("dve", lambda e: e.tensor_tensor(out=lamacc[:, 4:5], in0=lamacc[:, 3:4], in1=lamacc[:, 2:3], op=ALU.subtract),
              reads=(miscB,), writes=(miscB,))
        bb.op("dve", lambda e: e.tensor_tensor(out=neglam[:, 0:1], in0=lamacc[:, 4:5], in1=laminit_s[:, 0:1], op=ALU.subtract),
              reads=(miscB,) + CONST, writes=(miscB,))
        bb.op("act", lambda e: e.activation(out=esink[:], in_=sink_s[:], func=AF.Exp), reads=CONST, writes=(miscB,))

        for hh in range(16 if 't5' not in skip else 0):
            bb.op("dve", lambda e, hh=hh: e.tensor_scalar(out=lhsG[:], in0=ones33[:], scalar1=rb33_s[:, hh:hh + 1], scalar2=None, op0=ALU.mult),
                  reads=CONST + (miscB,), writes=(miscB,))
            bi = 2 + hh % 2
            bb.op("pe", lambda e, bi=bi: e.matmul(bank[bi][:], lhsT=lhsG[:], rhs=ohrev_s[:], start=True, stop=True),
                  reads=(miscB,) + CONST, writes=(bankB[bi],))
            ti_ = nxt(tstate, NTMP)
            bb.op("dve", lambda e, bi=bi, ti_=ti_: e.tensor_copy(out=tmp[ti_][:], in_=bank[bi][:]), reads=(bankB[bi],), writes=(tmpB[ti_],))
            bb.dma("pool", G_d[hh], tmp[ti_][:], reads=(tmpB[ti_],), writes=(GdB,), partial=True)
        for hh in range(16 if 't5b' not in skip else 0):
            dst = MBt if hh < 8 else TCt
            for dd in range(3):
                d = dd - 1
                src = bass.AP(G_d.tensor, hh * 128 * 512 + 255 - d * 128, [[511, 128], [1, 128]])
                bb.dma("sp", dst[:, hh % 8, dd * 128:(dd + 1) * 128], src, reads=(GdB,), writes=(constB,), partial=True)

        def sclr(l, s, j, c, which):
            o = ((l * 2 + s) * 3 + j) * 8 + c
            return (Avec if which == "A" else Gvec)[:, o:o + 1]

        def rstd_from_bank(bk, dim):
            bb.op("act", lambda e: e.activation(out=rstd[:], in_=bank[bk][:], func=AF.Ln, bias=eps_t[:], scale=1.0 / dim),
                  reads=(bankB[bk],) + CONST, writes=(rstdB,))
            bb.op("act", lambda e: e.activation(out=rstd[:], in_=rstd[:], func=AF.Exp, scale=-0.5), reads=(rstdB,), writes=(rstdB,))

        def norm_mod(l, s, j, bk=7):
            for kc in range(8):
                si = nxt(sqstate, 2)
                bb.op("act", lambda e, kc=kc, si=si: e.activation(out=sq[si][:], in_=xT[:, kc, :], func=AF.Square),
                      reads=(xTB,), writes=(sqB[si],))
                bb.op("pe", lambda e, kc=kc, si=si: e.matmul(bank[bk][:], lhsT=ones_bf[:], rhs=sq[si][:], start=(kc == 0), stop=(kc == 7)),
                      reads=(sqB[si],) + CONST, writes=(bankB[bk],), mark=True)
            rstd_from_bank(bk, 1024.0)
            for kc in range(8):
                ti_ = nxt(tstate, NTMP)
                bb.op("dve", lambda e, kc=kc, ti_=ti_: e.tensor_tensor(out=tmp[ti_][:], in0=xT[:, kc, :], in1=rstd[:], op=ALU.mult),
                      reads=(xTB, rstdB), writes=(tmpB[ti_],))
                mo = 3 * j * 8 + kc
                bb.op("pool", lambda e, kc=kc, ti_=ti_, mo=mo: e.tensor_scalar(
                    out=hT[:, kc, :], in0=tmp[ti_][:], scalar1=sclr(l, s, j, kc, "A"), scalar2=modT[:, l, mo, s:s + 1],
                    op0=ALU.mult, op1=ALU.add), reads=(tmpB[ti_], miscB) + CONST, writes=(hTB,))

        def wload(srcs, wb):
            wi = nxt(wstate, NW)
            first = True
            for (o, a, b_, src) in srcs:
                dst = wring[wi][:, o:o + a * b_].rearrange("p (a b) -> p a b", a=a)
                bb.dma("sp", dst, src, reads=(wb,), writes=(wringB[wi],), partial=(not first))
                first = False
            return wi

        def resid_update(l, s, j, m, bk):
            bb.op("dve", lambda e: e.scalar_tensor_tensor(out=xT[:, m, :], in0=bank[bk][:], scalar=sclr(l, s, j, m, "G"),
                                                          in1=xT[:, m, :], op0=ALU.mult, op1=ALU.add),
                  reads=(bankB[bk], miscB, xTB), writes=(xTB,))

        def ffn(l, which, s, j):
            norm_mod(l, s, j)
            wk = "A" if l == 0 else "B"
            wsrc = WSC[wk + "fi"]
            wb = wB[wk + "fi"]
            groups = [(g0, min(2, 22 - g0)) for g0 in range(0, 22, 2)]
            loads = []

            def issue(gi):
                g0, n = groups[gi]
                return wload([(0, 1, 4096, wsrc[gi].rearrange("p (a b) -> p a b", a=1))], wb)

            PF = 2
            for gi in range(min(PF, len(groups))):
                loads.append(issue(gi))
            for gi, (g0, n) in enumerate(groups):
                if gi + PF < len(groups):
                    loads.append(issue(gi + PF))
                wi = loads[gi]
                for jj in range(n):
                    jp = g0 + jj
                    bg, bu = (jp % 3) * 2, (jp % 3) * 2 + 1
                    for part, bk in ((0, bg), (1, bu)):
                        for kc in range(8):
                            o = part * 2048 + kc * n * 128 + jj * 128
                            bb.op("pe", lambda e, o=o, kc=kc, bk=bk, wi=wi: e.matmul(
                                bank[bk][:], lhsT=wring[wi][:, o:o + 128], rhs=hT[:, kc, :], start=(kc == 0), stop=(kc == 7)),
                                reads=(wringB[wi], hTB), writes=(bankB[bk],), mark=(kc == 7))
                    ti_ = nxt(tstate, NTMP)
                    bb.op("act", lambda e, bg=bg, ti_=ti_: e.activation(out=tmp[ti_][:], in_=bank[bg][:], func=AF.Silu),
                          reads=(bankB[bg],), writes=(tmpB[ti_],))
                    bb.op("dve", lambda e, bu=bu, ti_=ti_, jp=jp: e.tensor_tensor(out=scr[:, jp, :], in0=tmp[ti_][:], in1=bank[bu][:], op=ALU.mult),
                          reads=(tmpB[ti_], bankB[bu]), writes=(scrB,))
            wsrc2 = WSC[wk + "fo"].rearrange("(kc p) n -> p kc n", p=128)
            wb2 = wB[wk + "fo"]
            og = [(k0, min(4, 22 - k0)) for k0 in range(0, 22, 4)]
            loads = []

            def issue2(gi):
                k0, n = og[gi]
                return wload([(0, n, 1024, wsrc2[:, k0:k0 + n, :])], wb2)

            for gi in range(min(PF, len(og))):
                loads.append(issue2(gi))
            for gi, (k0, n) in enumerate(og):
                if gi + PF < len(og):
                    loads.append(issue2(gi + PF))
                wi = loads[gi]
                for kk in range(n):
                    kc = k0 + kk
                    for m in range(8):
                        bb.op("pe", lambda e, kk=kk, kc=kc, m=m, wi=wi: e.matmul(
                            bank[m][:], lhsT=wring[wi][:, kk * 1024 + m * 128:kk * 1024 + (m + 1) * 128], rhs=scr[:, kc, :],
                            start=(kc == 0), stop=(kc == 21)),
                            reads=(wringB[wi], scrB), writes=(bankB[m],), mark=(kc == 21 or (kk == n - 1 and m == 7)))
            for m in range(8):
                resid_update(l, s, j, m, m)

        def proj_chunks(wkey, cols, consumer, bks):
            wsrc = WSC[wkey]
            wb = wB[wkey]
            groups = [cols[i:i + 4] for i in range(0, len(cols), 4)]
            loads = []

            def issue(gi):
                g = groups[gi]
                tg = g[0] // 512
                assert g[0] % 512 == 0
                n_ = len(g) * 1024
                return wload([(0, 1, n_, wsrc[tg][:, 0:n_].rearrange("p (a b) -> p a b", a=1))], wb)

            PF = 2
            for gi in range(min(PF, len(groups))):
                loads.append(issue(gi))
            n = 0
            for gi, g in enumerate(groups):
                if gi + PF < len(groups):
                    loads.append(issue(gi + PF))
                wi = loads[gi]
                for ci, c0 in enumerate(g):
                    bk = bks[n % len(bks)]
                    for kc in range(8):
                        o = ci * 1024 + kc * 128
                        bb.op("pe", lambda e, o=o, kc=kc, bk=bk, wi=wi: e.matmul(
                            bank[bk][:], lhsT=wring[wi][:, o:o + 128], rhs=hT[:, kc, :], start=(kc == 0), stop=(kc == 7)),
                            reads=(wringB[wi], hTB), writes=(bankB[bk],), mark=(kc == 7))
                    consumer(n, bk)
                    n += 1

        def qk_post(l, bk, gcol, br, out_ap, outB, bk2):
            si = nxt(sqstate, 2)
            bb.op("act", lambda e: e.activation(out=sq[si][:], in_=bank[bk][:], func=AF.Square), reads=(bankB[bk],), writes=(sqB[si],))
            bb.op("dve", lambda e: e.tensor_copy(out=qraw[:], in_=bank[bk][:]), reads=(bankB[bk],), writes=(qrawB,))
            on = onesC_bf if br == 2 else ones_bf
            bb.op("pe", lambda e: e.matmul(bank[bk2][:], lhsT=on[:], rhs=sq[si][:], start=True, stop=True),
                  reads=(sqB[si],) + CONST, writes=(bankB[bk2],))
            rstd_from_bank(bk2, 64.0 if br == 2 else 128.0)
            gap = gvec_s[:, l * 8 + gcol:l * 8 + gcol + 1]
            if br != 0:
                bb.op("dve", lambda e: e.scalar_tensor_tensor(out=out_ap, in0=qraw[:], scalar=gap, in1=rstd[:], op0=ALU.mult, op1=ALU.mult),
                      reads=(qrawB, rstdB) + CONST, writes=(outB,))
                return
            bb.op("dve", lambda e: e.scalar_tensor_tensor(out=qn32[:], in0=qraw[:], scalar=gap, in1=rstd[:], op0=ALU.mult, op1=ALU.mult),
                  reads=(qrawB, rstdB) + CONST, writes=(qn32B,))
            bb.op("act", lambda e: e.activation(out=qnb[:], in_=qn32[:], func=AF.Copy), reads=(qn32B,), writes=(qnbB,))
            bb.op("pe", lambda e: e.matmul(bank[bk2][:], lhsT=perm_bf[:], rhs=qnb[:], start=True, stop=True),
                  reads=(qnbB,) + CONST, writes=(bankB[bk2],))
            t1 = nxt(tstate, NTMP)
            bb.op("pool", lambda e: e.tensor_tensor(out=tmp[t1][:], in0=qn32[:], in1=ropeC[:], op=ALU.mult),
                  reads=(qn32B, ropeB), writes=(tmpB[t1],))
            t2 = nxt(tstate, NTMP)
            bb.op("dve", lambda e: e.tensor_tensor(out=tmp[t2][:], in0=bank[bk2][:], in1=ropeS[:], op=ALU.mult),
                  reads=(bankB[bk2], ropeB), writes=(tmpB[t2],))
            bb.op("dve", lambda e: e.tensor_tensor(out=out_ap, in0=tmp[t1][:], in1=tmp[t2][:], op=ALU.add),
                  reads=(tmpB[t1], tmpB[t2]), writes=(outB,))

        def load_rope(i):
            bb.dma("sp", ropeC[:], ropeT[0, :, i * TT:(i + 1) * TT], writes=(ropeB,))
            bb.dma("sp", ropeS[:], ropeT[1, :, i * TT:(i + 1) * TT], writes=(ropeB,), partial=True)

        def kv_part(l, i):
            seg_s = i < 4
            s = 0 if seg_s else 1
            ti = i % 4
            KT = KT_S_o if seg_s else KT_P_o
            VV = V_S_o if seg_s else V_P_o
            KTb = outB
            VVb = outB
            norm_mod(l, s, 1)
            load_rope(i)
            cols = [n * 128 for n in range(6)]

            def cons(n, bk):
                br = n // 2
                ki = nxt(sqstate, 2) if False else (n % 2)
                qk_post(l, bk, [1, 3, 5][br], br, kst[ki][:], kstB[ki], 6)
                bb.dma("pool", KT[n * 128:(n + 1) * 128, ti * TT:(ti + 1) * TT], kst[ki][:], reads=(kstB[ki],), writes=(KTb,), partial=True)

            if "kvK" not in skip:
                proj_chunks("Bkv", cols, cons, [4, 5])
            if "kvV" in skip:
                return
            wv = WSC["Bv"]
            wi = wload([(0, 1, 2048, wv[0].rearrange("p (a b) -> p a b", a=1)), (2048, 1, 2048, wv[1].rearrange("p (a b) -> p a b", a=1))], wB["Bkv"])
            wi2 = wload([(0, 1, 2048, wv[2].rearrange("p (a b) -> p a b", a=1))], wB["Bkv"])
            for a in range(4):
                b0, b1 = 4 + (a % 2) * 2, 5 + (a % 2) * 2
                for kc in range(8):
                    bb.op("pe", lambda e, a=a, kc=kc, b0=b0: e.matmul(bank[b0][:, 0:256], lhsT=hT[:, kc, a * 128:(a + 1) * 128],
                                                                      rhs=wring[wi][:, kc * 256:(kc + 1) * 256], start=(kc == 0), stop=(kc == 7)),
                          reads=(wringB[wi], hTB), writes=(bankB[b0],), mark=False)
                for kc in range(8):
                    bb.op("pe", lambda e, a=a, kc=kc, b0=b0: e.matmul(bank[b0][:, 256:512], lhsT=hT[:, kc, a * 128:(a + 1) * 128],
                                                                      rhs=wring[wi][:, 2048 + kc * 256:2048 + (kc + 1) * 256], start=(kc == 0), stop=(kc == 7)),
                          reads=(wringB[wi], hTB), writes=(bankB[b0],), mark=(kc == 7))
                for kc in range(8):
                    bb.op("pe", lambda e, a=a, kc=kc, b1=b1: e.matmul(bank[b1][:, 0:256], lhsT=hT[:, kc, a * 128:(a + 1) * 128],
                                                                      rhs=wring[wi2][:, kc * 256:(kc + 1) * 256], start=(kc == 0), stop=(kc == 7)),
                          reads=(wringB[wi2], hTB), writes=(bankB[b1],), mark=(kc == 7))
                vi = a % 2
                bb.op("act", lambda e, vi=vi, b0=b0: e.activation(out=vtok[vi][:, 0:512], in_=bank[b0][:], func=AF.Copy),
                      reads=(bankB[b0],), writes=(vtokB[vi],))
                bb.op("dve", lambda e, vi=vi, b1=b1: e.tensor_copy(out=vtok[vi][:, 512:768], in_=bank[b1][:, 0:256]),
                      reads=(bankB[b1], vtokB[vi]), writes=(vtokB[vi],))
                kt = ti * 4 + a
                dst = VV.rearrange("(g p) c -> p g c", p=128)[:, :, kt * 128:(kt + 1) * 128]
                bb.dma("pool", dst, vtok[vi][:].rearrange("p (g d) -> p g d", g=6), reads=(vtokB[vi],), writes=(VVb,), partial=True)

        def attn_part(L, i):
            seg_s = i < 4
            s = 0 if seg_s else 1
            ti = i % 4
            par = 0
            nblk = 8 if seg_s else 1
            norm_mod(L, s, 1)
            load_rope(i)
            if seg_s:
                t0, t1 = 4 * ti - 1, 4 * ti + 4
                KTl, VVl, off = KT_Sx, V_Sx, 128
            else:
                t0, t1 = max(4 * ti - 1, 0), min(4 * ti + 4, 15)
                KTl, VVl, off = KT_P_i, V_P_i, 0
            ncol = (t1 - t0 + 1) * 128
            for q4 in range(4):
                brg = 2 + q4
                bb.dma("sp", nearK[:, q4, 0:ncol], KTl[brg * 128:(brg + 1) * 128, off + t0 * 128:off + t0 * 128 + ncol], reads=(inB,), writes=(nearKB,), partial=(q4 > 0))
                bb.dma("sp", nearV[:, q4, 0:ncol], VVl[brg * 128:(brg + 1) * 128, off + t0 * 128:off + t0 * 128 + ncol], reads=(inB,), writes=(nearVB,), partial=(q4 > 0))
            if seg_s:
                for h in range(8):
                    ta = nxt(tstate, NTMP)
                    bb.op("pool", lambda e, ta=ta, h=h: e.tensor_scalar(out=tmp[ta][:, 0:128], in0=indS_s[:, ti * 128:(ti + 1) * 128],
                                                                        scalar1=rbrep_s[:, 15 * 16 + 8 + h:15 * 16 + 9 + h], scalar2=None, op0=ALU.mult),
                          reads=CONST, writes=(tmpB[ta],))
                    bb.op("dve", lambda e, ta=ta, h=h: e.scalar_tensor_tensor(out=tmp[ta][:, 128:256], in0=indS_s[:, 512 + ti * 128:512 + (ti + 1) * 128],
                                                                               scalar=rbrep_s[:, 31 * 16 + 8 + h:31 * 16 + 9 + h], in1=tmp[ta][:, 0:128],
                                                                               op0=ALU.mult, op1=ALU.add),
                          reads=CONST + (tmpB[ta],), writes=(tmpB[ta],))
                    bb.op("pool", lambda e, ta=ta, h=h: e.tensor_tensor(out=biasS[:, :, h], in0=tmp[ta][:, 128:256],
                                                                        in1=indS_s[:, 1024 + ti * 128:1024 + (ti + 1) * 128], op=ALU.add),
                          reads=CONST + (tmpB[ta],), writes=(biasSB,))

            def near_src(t, q4):
                if t < t0 or t > t1:
                    return None
                o = (t - t0) * 128
                if 0 <= t <= 15:
                    bias = zero_t[:]
                else:
                    side = 0 if t < 0 else 1
                    bias = hmask_s[:, side:side + 1]
                return nearK[:, q4, o:o + 128], nearV[:, q4, o:o + 128], bias, (nearKB, nearVB)

            for br in range(3):
                qcols = [br * 1024 + h * 128 for h in range(8)]
                gcols = [3072 + br * 1024 + h * 128 for h in range(8)]

                def consq(n, bk):
                    qk_post(L, bk, [0, 2, 4][br], br, scr[:, n, :], scrB, 6)

                def consg(n, bk):
                    ta = nxt(tstate, NTMP)
                    bb.op("act", lambda e: e.activation(out=tmp[ta][:], in_=bank[bk][:], func=AF.Exp, scale=-1.0), reads=(bankB[bk],), writes=(tmpB[ta],))
                    bb.op("pool", lambda e: e.tensor_scalar(out=tmp[ta][:], in0=tmp[ta][:], scalar1=1.0, scalar2=None, op0=ALU.add),
                          reads=(tmpB[ta],), writes=(tmpB[ta],))
                    bb.op("dve", lambda e: e.reciprocal(out=tmp[ta][:], in_=tmp[ta][:]), reads=(tmpB[ta],), writes=(tmpB[ta],))
                    bb.op("pool", lambda e: e.tensor_copy(out=scr[:, 8 + n, :], in_=tmp[ta][:]), reads=(tmpB[ta],), writes=(scrB,))

                proj_chunks("Aq", qcols, consq, [4, 5])
                proj_chunks("Aq", gcols, consg, [4, 5])
                for h in range(8):
                    g = h // 4
                    qTh = scr[:, h, :]
                    sgh = scr[:, 8 + h, :]
                    if br == 0:
                        if "attnA" in skip:
                            bb.op("pool", lambda e, h=h: e.memset(big[:, h, :], 0.0), writes=(bigB,))
                        else:
                            attn_A(L, par, seg_s, nblk, h, g, qTh, sgh)
                    elif br == 1:
                        if "attnB" not in skip:
                            attn_B(L, ti, h, g, qTh, sgh, near_src)
                    else:
                        if "attnC" not in skip:
                            attn_C(L, par, seg_s, nblk, ti, h, g, qTh, sgh, near_src)
            for h in range(8):
                bb.op("act", lambda e, h=h: e.activation(out=hT[:, h, :], in_=big[:, h, :], func=AF.Copy), reads=(bigB,), writes=(hTB,))
            for half in range(2):
                wi = wload([(0, 1, 4096, WSC["Ao"][half].rearrange("p (a b) -> p a b", a=1))], wB["Ao"])
                for mm in range(4):
                    m = half * 4 + mm
                    bk = m % 4
                    for kc in range(8):
                        bb.op("pe", lambda e, kc=kc, mm=mm, bk=bk, wi=wi: e.matmul(
                            bank[bk][:], lhsT=wring[wi][:, kc * 512 + mm * 128:kc * 512 + (mm + 1) * 128], rhs=hT[:, kc, :],
                            start=(kc == 0), stop=(kc == 7)), reads=(wringB[wi], hTB), writes=(bankB[bk],), mark=(kc == 7))
                    resid_update(L, s, 1, m, bk)

        def kv_stream(par, seg_s, nblk, brg, blk):
            ki = nxt(kvstate, 2)
            if seg_s:
                ksrc = KT_all_i[blk * 768 + brg * 128:blk * 768 + (brg + 1) * 128, :]
                vsrc = V_all_i[blk * 768 + brg * 128:blk * 768 + (brg + 1) * 128, :]
            else:
                ksrc = KT_P_i[brg * 128:(brg + 1) * 128, :]
                vsrc = V_P_i[brg * 128:(brg + 1) * 128, :]
            bb.dma("sp", kblk[ki][:], ksrc, reads=(inB,), writes=(kblkB[ki],))
            bb.dma("sp", vblk[ki][:], vsrc, reads=(inB,), writes=(vblkB[ki],))
            return ki

        def finalize(bO, bZ, h, sgh, first, zadd=None, mulvec=None):
            tz = nxt(tstate, NTMP)
            if zadd is not None:
                bb.op("dve", lambda e: e.tensor_scalar(out=tmp[tz][:], in0=bank[bZ][:], scalar1=zadd, scalar2=None, op0=ALU.add),
                      reads=(bankB[bZ], miscB), writes=(tmpB[tz],))
                bb.op("dve", lambda e: e.reciprocal(out=tmp[tz][:], in_=tmp[tz][:]), reads=(tmpB[tz],), writes=(tmpB[tz],))
            else:
                bb.op("dve", lambda e: e.reciprocal(out=tmp[tz][:], in_=bank[bZ][:]), reads=(bankB[bZ],), writes=(tmpB[tz],))
            to = nxt(tstate, NTMP)
            bb.op("dve", lambda e: e.tensor_tensor(out=tmp[to][:], in0=bank[bO][:], in1=tmp[tz][:], op=ALU.mult),
                  reads=(bankB[bO], tmpB[tz]), writes=(tmpB[to],))
            if first:
                bb.op("pool", lambda e: e.tensor_tensor(out=big[:, h, :], in0=tmp[to][:], in1=sgh, op=ALU.mult),
                      reads=(tmpB[to], scrB), writes=(bigB,))
            else:
                bb.op("pool", lambda e: e.tensor_tensor(out=tmp[to][:], in0=tmp[to][:], in1=sgh, op=ALU.mult),
                      reads=(tmpB[to], scrB), writes=(tmpB[to],))
                bb.op("pool", lambda e: e.tensor_tensor(out=big[:, h, :], in0=big[:, h, :], in1=tmp[to][:], op=ALU.add),
                      reads=(tmpB[to], bigB), writes=(bigB,))

        def zacc_add(n_, pi):
            if "noacc" in skip:
                return
            if n_ % 4 == 3:
                ek, a, first = "pool", 2, (n_ == 3)
            else:
                j = (n_ // 4) * 3 + (n_ % 4)
                ek, a, first = "dve", j % 2, (j < 2)
            if first:
                bb.op(ek, lambda en: en.tensor_copy(out=zacc[a][:], in_=pT[pi][:]), reads=(pTB[pi],), writes=(zaccB[a],))
            else:
                bb.op(ek, lambda en: en.tensor_tensor(out=zacc[a][:], in0=zacc[a][:], in1=pT[pi][:], op=ALU.add),
                      reads=(pTB[pi], zaccB[a]), writes=(zaccB[a],))

        def z_total(bks, nitems, fold):
            for a, need in ((1, 2), (2, 4)):
                if nitems >= need:
                    bb.op("dve", lambda en, a=a: en.tensor_tensor(out=zacc[0][:], in0=zacc[0][:], in1=zacc[a][:], op=ALU.add),
                          reads=(zaccB[0], zaccB[a]), writes=(zaccB[0],))
            if fold:
                bb.op("dve", lambda en: en.tensor_tensor(out=zacc[0][:, 0:TT], in0=zacc[0][:, 0:TT], in1=zacc[0][:, TT:2 * TT], op=ALU.add),
                      reads=(zaccB[0],), writes=(zaccB[0],))
                halves = ((0, bks[0]),)
            else:
                halves = ((0, bks[0]), (1, bks[1]))
            for m, bk in halves:
                bb.op("pe", lambda en, m=m, bk=bk: en.matmul(bank[bk][:], lhsT=ones_f32[:], rhs=zacc[0][:, m * TT:(m + 1) * TT], start=True, stop=True),
                      reads=(zaccB[0],) + CONST, writes=(bankB[bk],))

        def attn_A(L, par, seg_s, nblk, h, g, qTh, sgh):
            bO, bZ = 2, 3
            SP = (0, 2, 3)
            LA = 2
            nk = nblk * 16
            npair = nk // 2
            kis = {}
            kis[0] = kv_stream(par, seg_s, nblk, g, 0)
            pend = []
            for pr in range(npair):
                kt0 = 2 * pr
                blk, kk = kt0 // 16, kt0 % 16
                if kk == 2 * (LA + 1) and blk + 1 < nblk:
                    kis[blk + 1] = kv_stream(par, seg_s, nblk, g, blk + 1)
                ki = kis[blk]
                dbk = SP[pr % 3]
                for u in range(2):
                    bs = 2 * dbk + u
                    bb.op("pe", lambda e, ki=ki, kk=kk, bs=bs, u=u: e.matmul(bank[bs][:], lhsT=kblk[ki][:, (kk + u) * 128:(kk + u + 1) * 128], rhs=qTh,
                                                                          start=True, stop=True),
                          reads=(kblkB[ki], scrB), writes=(bankB[bs],), mark=(u == 1))
                pi = nxt(pstate, NP_)
                bb.op("act", lambda e, dbk=dbk, pi=pi: e.activation(out=pT[pi][:], in_=bank2[dbk][:], func=AF.Exp, scale=SC_A),
                      reads=(bankB[2 * dbk], bankB[2 * dbk + 1]), writes=(pTB[pi],))
                pend.append((pr, ki, kk, pi))
                if len(pend) > LA:
                    _pv_A(pend.pop(0), nk, bO)
            while pend:
                _pv_A(pend.pop(0), nk, bO)
            z_total((bZ,), npair, True)
            finalize(bO, bZ, h, sgh, True)

        def _pv_A(it, nk, bO):
            pr, ki, kk, pi = it
            for u in range(2):
                kt = 2 * pr + u
                bb.op("pe", lambda e, u=u, kt=kt: e.matmul(bank[bO][:], lhsT=vblk[ki][:, (kk + u) * 128:(kk + u + 1) * 128], rhs=pT[pi][:, u * TT:(u + 1) * TT],
                                                         start=(kt == 0), stop=(kt == nk - 1)),
                      reads=(vblkB[ki], pTB[pi]), writes=(bankB[bO],), mark=(u == 1))
            zacc_add(pr, pi)

        def attn_B(L, ti, h, g, qTh, sgh, near_src):
            bO, bZ = 2, 3
            q4 = g
            for qb in range(4):
                srcs = []
                for dd in range(3):
                    t = 4 * ti + qb + dd - 1
                    r = near_src(t, q4)
                    if r is not None:
                        srcs.append((dd, r))
                bs = qb % 2
                dd0, dd1 = srcs[0][0], srcs[-1][0]
                for dd, r in srcs:
                    bb.op("pe", lambda e, dd=dd, r=r: e.matmul(bank[bs][:, dd * 128:(dd + 1) * 128], lhsT=r[0], rhs=qTh[:, qb * 128:(qb + 1) * 128],
                                                               start=True, stop=True),
                          reads=r[3] + (scrB,), writes=(bankB[bs],), mark=(dd == dd1))
                ta = nxt(tstate, NTMP)
                c0, c1 = dd0 * 128, (dd1 + 1) * 128
                bb.op("dve", lambda e: e.scalar_tensor_tensor(out=tmp[ta][:, c0:c1], in0=bank[bs][:, c0:c1], scalar=SC_A, in1=MBt[:, h, c0:c1],
                                                              op0=ALU.mult, op1=ALU.add), reads=(bankB[bs],) + CONST, writes=(tmpB[ta],))
                pi = nxt(pstate, NP_)
                for n_, (dd, r) in enumerate(srcs):
                    bb.op("act", lambda e, dd=dd, r=r: e.activation(out=pT[pi][:, dd * 128:(dd + 1) * 128], in_=tmp[ta][:, dd * 128:(dd + 1) * 128],
                                                                    func=AF.Exp, bias=r[2], scale=1.0),
                          reads=(tmpB[ta],) + CONST + ((pTB[pi],) if n_ > 0 else ()), writes=(pTB[pi],))
                for dd, r in srcs:
                    bb.op("pe", lambda e, dd=dd, r=r: e.matmul(bank[bO][:, qb * 128:(qb + 1) * 128], lhsT=r[1], rhs=pT[pi][:, dd * 128:(dd + 1) * 128],
                                                               start=(dd == dd0), stop=(dd == dd1)),
                          reads=r[3] + (pTB[pi],), writes=(bankB[bO],), mark=False)
                    bb.op("pe", lambda e, dd=dd: e.matmul(bank[bZ][:, qb * 128:(qb + 1) * 128], lhsT=ones_bf[:], rhs=pT[pi][:, dd * 128:(dd + 1) * 128],
                                                          start=(dd == dd0), stop=(dd == dd1)),
                          reads=(pTB[pi],) + CONST, writes=(bankB[bZ],), mark=(dd == dd1))
            finalize(bO, bZ, h, sgh, False, zadd=esink[:, h:h + 1])

        def attn_C(L, par, seg_s, nblk, ti, h, g, qTh, sgh, near_src):
            bO = (4, 5)
            bZ = (0, 1)
            SP = (0, 1, 3)
            LA = 2
            brg = 4 + g
            far = []
            for kt in range(nblk * 16):
                if seg_s:
                    far.append(kt)
                elif kt < 4 * ti - 1 or kt > 4 * ti + 4:
                    far.append(kt)
            near = []
            for rel in range(-1, 5):
                r = near_src(4 * ti + rel, 2 + g)
                if r is not None:
                    near.append((rel, r))
            ntot = len(far) + len(near)
            kis = {}
            loaded = set()
            pend = []
            idx = 0

            def pv(it):
                n_, lhsV, rdV, pi = it
                for m_ in range(2 if "nopv" not in skip else 0):
                    bb.op("pe", lambda e, m_=m_: e.matmul(bank[bO[m_]][:], lhsT=lhsV, rhs=pT[pi][:, m_ * TT:(m_ + 1) * TT], start=(n_ == 0), stop=(n_ == ntot - 1)),
                          reads=rdV + (pTB[pi],), writes=(bankB[bO[m_]],), mark=(m_ == 1))
                zacc_add(n_, pi)

            if far:
                kis[far[0] // 16] = kv_stream(par, seg_s, nblk, brg, far[0] // 16)
                loaded.add(far[0] // 16)
            for fi, kt in enumerate(far):
                blk, kk = kt // 16, kt % 16
                if kk >= LA and blk + 1 < nblk and (blk + 1) not in loaded and any(f // 16 == blk + 1 for f in far):
                    kis[blk + 1] = kv_stream(par, seg_s, nblk, brg, blk + 1)
                    loaded.add(blk + 1)
                ki = kis[blk]
                dbk = SP[idx % 3]
                for m_ in range(2):
                    bs = 2 * dbk + m_
                    bb.op("pe", lambda e, m_=m_, bs=bs, ki=ki, kk=kk: e.matmul(
                        bank[bs][:], lhsT=kblk[ki][m_ * 64:(m_ + 1) * 64, kk * 128:(kk + 1) * 128], rhs=qTh[m_ * 64:(m_ + 1) * 64, :], start=True, stop=True),
                        reads=(kblkB[ki], scrB), writes=(bankB[bs],), mark=(m_ == 1))
                pi = nxt(pstate, NP_)
                if seg_s:
                    bap = biasS[:, kt, h:h + 1]
                    rd = (biasSB,)
                else:
                    col = (15 if kt < 4 * ti - 1 else 31) * 16 + 8 + h
                    bap = rbrep_s[:, col:col + 1]
                    rd = CONST
                bb.op("act", lambda e, dbk=dbk, pi=pi, bap=bap: e.activation(out=pT[pi][:], in_=bank2[dbk][:], func=AF.Exp, bias=bap, scale=SC_C),
                      reads=(bankB[2 * dbk], bankB[2 * dbk + 1]) + rd, writes=(pTB[pi],))
                pend.append((idx, vblk[ki][:, kk * 128:(kk + 1) * 128], (vblkB[ki],), pi))
                idx += 1
                if len(pend) > LA:
                    pv(pend.pop(0))
            for rel, r in near:
                dbk = SP[idx % 3]
                pi = nxt(pstate, NP_)
                for m_ in range(2):
                    bs = 2 * dbk + m_
                    bb.op("pe", lambda e, m_=m_, bs=bs, r=r: e.matmul(bank[bs][:], lhsT=r[0][m_ * 64:(m_ + 1) * 64, :], rhs=qTh[m_ * 64:(m_ + 1) * 64, :],
                                                                      start=True, stop=True),
                          reads=r[3] + (scrB,), writes=(bankB[bs],))
                    ta = nxt(tstate, NTMP)
                    for qb in range(4):
                        d = rel - qb
                        if abs(d) <= 1:
                            bb.op("dve", lambda e, qb=qb, d=d, bs=bs, ta=ta: e.scalar_tensor_tensor(
                                out=tmp[ta][:, qb * 128:(qb + 1) * 128], in0=bank[bs][:, qb * 128:(qb + 1) * 128], scalar=SC_C,
                                in1=TCt[:, h, (d + 1) * 128:(d + 2) * 128], op0=ALU.mult, op1=ALU.add),
                                reads=(bankB[bs],) + CONST + ((tmpB[ta],) if qb > 0 else ()), writes=(tmpB[ta],))
                        else:
                            col = (15 if d < 0 else 31) * 16 + 8 + h
                            bb.op("dve", lambda e, qb=qb, col=col, bs=bs, ta=ta: e.tensor_scalar(
                                out=tmp[ta][:, qb * 128:(qb + 1) * 128], in0=bank[bs][:, qb * 128:(qb + 1) * 128], scalar1=SC_C,
                                scalar2=rbrep_s[:, col:col + 1], op0=ALU.mult, op1=ALU.add),
                                reads=(bankB[bs],) + CONST + ((tmpB[ta],) if qb > 0 else ()), writes=(tmpB[ta],))
                    bb.op("act", lambda e, ta=ta, pi=pi, r=r, m_=m_: e.activation(out=pT[pi][:, m_ * TT:(m_ + 1) * TT], in_=tmp[ta][:], func=AF.Exp, bias=r[2], scale=1.0),
                          reads=(tmpB[ta],) + CONST + ((pTB[pi],) if m_ == 1 else ()), writes=(pTB[pi],))
                pend.append((idx, r[1], r[3], pi))
                idx += 1
                if len(pend) > LA:
                    pv(pend.pop(0))
            while pend:
                pv(pend.pop(0))
            z_total(bZ, ntot, False)
            tz1 = nxt(tstate, NTMP)
            bb.op("dve", lambda e: e.reciprocal(out=tmp[tz1][:], in_=bank[bZ[0]][:]), reads=(bankB[bZ[0]],), writes=(tmpB[tz1],))
            to1 = nxt(tstate, NTMP)
            bb.op("dve", lambda e: e.tensor_tensor(out=tmp[to1][:], in0=bank[bO[0]][:], in1=tmp[tz1][:], op=ALU.mult),
                  reads=(bankB[bO[0]], tmpB[tz1]), writes=(tmpB[to1],))
            tz2 = nxt(tstate, NTMP)
            bb.op("dve", lambda e: e.reciprocal(out=tmp[tz2][:], in_=bank[bZ[1]][:]), reads=(bankB[bZ[1]],), writes=(tmpB[tz2],))
            bb.op("dve", lambda e: e.scalar_tensor_tensor(out=tmp[tz2][:], in0=bank[bO[1]][:], scalar=neglam[:, 0:1], in1=tmp[tz2][:],
                                                          op0=ALU.mult, op1=ALU.mult),
                  reads=(bankB[bO[1]], tmpB[tz2], miscB), writes=(tmpB[tz2],))
            bb.op("dve", lambda e: e.tensor_tensor(out=tmp[to1][:], in0=tmp[to1][:], in1=tmp[tz2][:], op=ALU.add),
                  reads=(tmpB[to1], tmpB[tz2]), writes=(tmpB[to1],))
            si = nxt(sqstate, 2)
            bb.op("act", lambda e: e.activation(out=sq[si][:], in_=tmp[to1][:], func=AF.Square), reads=(tmpB[to1],), writes=(sqB[si],))
            bb.op("pe", lambda e: e.matmul(bank[2][:], lhsT=ones_bf[:], rhs=sq[si][:], start=True, stop=True), reads=(sqB[si],) + CONST, writes=(bankB[2],))
            rstd_from_bank(2, 128.0)
            bb.op("dve", lambda e: e.scalar_tensor_tensor(out=tmp[to1][:], in0=tmp[to1][:], scalar=gvec_s[:, L * 8 + 6:L * 8 + 7], in1=rstd[:],
                                                          op0=ALU.mult, op1=ALU.mult), reads=(tmpB[to1], rstdB) + CONST, writes=(tmpB[to1],))
            bb.op("dve", lambda e: e.scalar_tensor_tensor(out=tmp[to1][:], in0=tmp[to1][:], scalar=laminit_s[:, 1:2], in1=sgh, op0=ALU.mult, op1=ALU.mult),
                  reads=(tmpB[to1], scrB) + CONST, writes=(tmpB[to1],))
            bb.op("pool", lambda e: e.tensor_tensor(out=big[:, h, :], in0=big[:, h, :], in1=tmp[to1][:], op=ALU.add), reads=(tmpB[to1], bigB), writes=(bigB,))

        for i in range(ntiles):
            s = 0 if i < 4 else 1
            if kind == "first":
                xt = big[:].rearrange("p a t -> p (a t)").rearrange("p (a f) -> p a f", a=4)
                bb.dma("sp", xt, xin[i * TT:(i + 1) * TT, :].rearrange("(a p) f -> p a f", p=128), writes=(bigB,))
                for kc in range(8):
                    bk = kc % 4
                    for a in range(4):
                        bb.op("pe", lambda e, kc=kc, a=a, bk=bk: e.transpose(bank[bk][:, a * 128:(a + 1) * 128], xt[:, a, kc * 128:(kc + 1) * 128], ident[:]),
                              reads=(bigB,) + CONST, writes=(bankB[bk],), mark=(a == 3))
                    if kc % 2 == 0:
                        bb.op("act", lambda e, kc=kc, bk=bk: e.activation(out=xT[:, kc, :], in_=bank[bk][:], func=AF.Copy), reads=(bankB[bk],), writes=(xTB,))
                    else:
                        bb.op("dve", lambda e, kc=kc, bk=bk: e.tensor_copy(out=xT[:, kc, :], in_=bank[bk][:]), reads=(bankB[bk], xTB), writes=(xTB,))
            else:
                bb.dma("sp", xT[:], xT_i[i].rearrange("(kc p) t -> p kc t", p=128), reads=(inB,), writes=(xTB,))
            if has_attn:
                attn_part(0, i)
                ffn(0, 1, s, 2)
            if has_kv:
                if 'ffn' not in skip:
                    ffn(1, 0, s, 0)
                if 'kv' not in skip:
                    kv_part(1, i)
                bb.dma("pool", xT_o[i].rearrange("(kc p) t -> p kc t", p=128), xT[:], reads=(xTB,), writes=(outB,), partial=True)
            else:
                yt = big[:].rearrange("p a t -> p (a t)").rearrange("p (a f) -> p a f", a=4)
                for a in range(4):
                    for half in range(2):
                        bk = (a * 2 + half) % 4
                        for kk in range(4):
                            kc = half * 4 + kk
                            bb.op("pe", lambda e, a=a, kk=kk, kc=kc, bk=bk: e.transpose(bank[bk][:, kk * 128:(kk + 1) * 128], xT[:, kc, a * 128:(a + 1) * 128], ident[:]),
                                  reads=(xTB,) + CONST, writes=(bankB[bk],), mark=(kk == 3))
                        if half == 0:
                            bb.op("act", lambda e, a=a, bk=bk: e.activation(out=yt[:, a, 0:512], in_=bank[bk][:], func=AF.Copy), reads=(bankB[bk],), writes=(bigB,))
                        else:
                            bb.op("dve", lambda e, a=a, bk=bk: e.tensor_copy(out=yt[:, a, 512:1024], in_=bank[bk][:]), reads=(bankB[bk], bigB), writes=(bigB,))
                bb.dma("pool", y[i * TT:(i + 1) * TT, :].rearrange("(a p) f -> p a f", p=128), yt, reads=(bigB,), writes=(outB,), partial=True)
        for qk in ("sp", "pool"):
            Q = bb.E[qk]
            for sdm in bb.pools[qk][0]:
                if sdm.val > 0:
                    Q.eng.wait_ge(sdm.h, sdm.val)
    return nc


def _host_consts(core):
    c = {}
    c["ident"] = np.eye(128, dtype=np.float32)
    P = np.zeros((128, 128), np.float32)
    for i in range(128):
        blk = (i % 64) // 32
        j = i + 32 if blk == 0 else i - 32
        P[j, i] = 1.0
    c["perm"] = P
    pos = np.concatenate([core * 2048 + np.arange(2048), np.arange(2048)])
    row = (pos // 64).astype(np.float32)
    col = (pos % 64).astype(np.float32)
    inv = (np.float32(10000.0) ** (-np.arange(32, dtype=np.float32) / np.float32(32))).astype(np.float32)
    ang_r = row[:, None] * inv
    ang_c = col[:, None] * inv
    ang = np.concatenate([ang_r, ang_r, ang_c, ang_c], axis=-1).astype(np.float32)
    cos = np.cos(ang).astype(np.float32).T
    sin = np.sin(ang).astype(np.float32).T
    sign = np.ones((128, 1), np.float32)
    for i in range(128):
        if (i % 64) // 32 == 0:
            sign[i] = -1.0
    c["ropeT"] = np.ascontiguousarray(np.stack([cos, sin * sign]))
    oh = np.zeros((33, 512), np.float32)
    for j in range(511):
        rel = 255 - j
        oh[int(t5_bucket_np(rel)), j] = 1.0
        oh[32, j] = NEGM if abs(rel) > 128 else 0.0
    c["ohrev"] = oh
    indm = np.zeros((4, 128), np.float32); indp = np.zeros((4, 128), np.float32); nearm = np.zeros((4, 128), np.float32)
    for ti in range(4):
        Q0 = core * 16 + 4 * ti
        for kt in range(128):
            if kt < Q0 - 1:
                indm[ti, kt] = 1.0
            elif kt > Q0 + 4:
                indp[ti, kt] = 1.0
            else:
                nearm[ti, kt] = NEGM
    ind = np.concatenate([indm.reshape(-1), indp.reshape(-1), nearm.reshape(-1)])
    c["indS"] = np.ascontiguousarray(np.broadcast_to(ind[None, :], (128, 1536))).astype(np.float32)
    hm = np.array([NEGM if core == 0 else 0.0, NEGM if core == 7 else 0.0], np.float32)
    c["hmask"] = np.ascontiguousarray(np.broadcast_to(hm[None, :], (128, 2))).astype(np.float32)
    return c


def build_mod():
    nc = bass.Bass("TRN2", target_bir_lowering=False)
    cT9 = nc.dram_tensor("cT9", [128, 72], F32, kind="ExternalInput").ap()
    wsl = nc.dram_tensor("wada_sl", [4, D, 1152], F32, kind="ExternalInput").ap()
    bada9 = nc.dram_tensor("bada9", [9, 4 * 1152], F32, kind="ExternalInput").ap()
    modp = nc.dram_tensor("modp", [9, 4 * 1152], F32, kind="ExternalOutput").ap()
    from contextlib import ExitStack
    with ExitStack() as st:
        bb = B(nc, st)
        sbt = lambda n, shp, dt: st.enter_context(nc.sbuf_tensor(n, list(shp), dt))
        c_s = sbt("c_s", [128, 72], F32); sc_s = sbt("sc_s", [128, 72], F32)
        b_s = sbt("b_s", [9, 4 * 1152], F32); o_s = sbt("o_s", [9, 4 * 1152], F32)
        wt = [sbt("wt%d" % i, [128, 8, 512], F32) for i in range(2)]
        wtB = [Buf("wt%d" % i) for i in range(2)]
        ps = [st.enter_context(nc.psum_tensor("ps%d" % i, [128, 512], F32)) for i in range(2)]
        psB = [Buf("ps%d" % i) for i in range(2)]
        cB, oB = Buf("c"), Buf("o")
        bb.dma("sp", c_s[:], cT9, writes=(cB,))
        bb.dma("sp", b_s[:], bada9, writes=(cB,), partial=True)
        bb.op("act", lambda e: e.activation(out=sc_s[:], in_=c_s[:], func=AF.Silu), reads=(cB,), writes=(cB,))
        n = 0
        for l in range(4):
            for (c0, w) in ((0, 512), (512, 512), (1024, 128)):
                k = n % 2
                n += 1
                bb.dma("sp", wt[k][:, :, 0:w], wsl[l].rearrange("(kc p) n -> p kc n", p=128)[:, :, c0:c0 + w], writes=(wtB[k],))
                for kc in range(8):
                    bb.op("pe", lambda e, kc=kc, k=k, w=w: e.matmul(ps[k][0:9, 0:w], lhsT=sc_s[:, kc * 9:(kc + 1) * 9], rhs=wt[k][:, kc, 0:w],
                                                                   start=(kc == 0), stop=(kc == 7)),
                          reads=(wtB[k], cB), writes=(psB[k],), mark=(kc == 7))
                o = l * 1152 + c0
                bb.op("dve", lambda e, k=k, w=w, o=o: e.tensor_tensor(out=o_s[:, o:o + w], in0=ps[k][0:9, 0:w], in1=b_s[:, o:o + w], op=ALU.add),
                      reads=(psB[k], cB), writes=(oB,))
        bb.dma("sp", modp, o_s[:], reads=(oB,), writes=(Buf("out"),))
        for sdm in bb.pools["sp"][0]:
            if sdm.val > 0:
                nc.sync.wait_ge(sdm.h, sdm.val)
    return nc


_NC_CACHE = {}


def _get_nc(kind):
    if kind not in _NC_CACHE:
        _NC_CACHE[kind] = build_mod() if kind == "mod" else build(kind)
    return _NC_CACHE[kind]


def _run(kind, maps):
    res = run_bass_kernel_spmd(_get_nc(kind), maps, core_ids=list(range(NCORE)))
    return res.results


def kernel(**inputs):
    import ml_dtypes
    bf = ml_dtypes.bfloat16
    f = lambda a: np.ascontiguousarray(np.asarray(a, dtype=np.float32))
    inp = {k: f(v) for k, v in inputs.items()}
    xs, xp = inp["x_sample"], inp["x_prompt"]
    cs, cp = inp["c_sample"], inp["c_prompt"]
    c9 = np.concatenate([cs, cp], axis=0)
    cT9 = np.ascontiguousarray(c9.reshape(9, 8, 128).transpose(2, 1, 0).reshape(128, 72))
    maps = []
    for core in range(NCORE):
        sl = slice(core * 1152, (core + 1) * 1152)
        maps.append({"cT9": cT9,
                     "wada_sl": np.ascontiguousarray(inp["w_ada"][:, :, sl]),
                     "bada9": np.ascontiguousarray(np.broadcast_to(inp["b_ada"][:, sl].reshape(1, 4 * 1152), (9, 4 * 1152)))})
    r = _run("mod", maps)
    mod = np.zeros((4, 9, 9 * D), np.float32)
    for core in range(NCORE):
        mp = np.asarray(r[core]["modp"], np.float32).reshape(9, 4, 1152)
        mod[:, :, core * 1152:(core + 1) * 1152] = mp.transpose(1, 0, 2)
    consts = [_host_consts(core) for core in range(NCORE)]
    rb = inp["rel_bias"]
    flag = np.concatenate([np.ones(8, np.float32), np.zeros(8, np.float32)])[None, :]
    rb33 = np.ascontiguousarray(np.concatenate([rb, flag], axis=0))
    rbrep = np.ascontiguousarray(np.broadcast_to(rb.reshape(1, 512), (128, 512)))

    def layer_small(la, lb, core):
        m = {}
        mt = np.zeros((128, 2, 72, 2), np.float32)
        for slot, l in ((0, la), (1, lb)):
            for s_, seq in ((0, 0), (1, 1 + core)):
                mt[:, slot, :, s_] = mod[l, seq].reshape(72, 128).T
        m["modT"] = mt.reshape(128, 288)
        gn = np.zeros((128, 2, 3, 8), np.float32)
        gv = np.zeros((128, 2, 8), np.float32)
        for slot, l in ((0, la), (1, lb)):
            gn[:, slot] = inp["g_norm"][l].reshape(3, 8, 128).transpose(2, 0, 1)
            gv[:, slot, 0] = inp["g_qa"][l]; gv[:, slot, 1] = inp["g_ka"][l]
            gv[:, slot, 2] = inp["g_qb"][l]; gv[:, slot, 3] = inp["g_kb"][l]
            gv[:, slot, 4] = np.tile(inp["g_qc"][l], 2); gv[:, slot, 5] = np.tile(inp["g_kc"][l], 2)
            gv[:, slot, 6] = inp["g_subln"][l]
        m["g_normT"] = gn.reshape(128, 48)
        m["gvec"] = gv.reshape(128, 16)
        m["sinkrep"] = np.ascontiguousarray(np.broadcast_to(inp["sink"][la].reshape(1, 8), (128, 8)))
        lam = np.concatenate([inp["lam_q1"][la], inp["lam_k1"][la], inp["lam_q2"][la], inp["lam_k2"][la]])
        m["lamrep"] = np.ascontiguousarray(np.broadcast_to(lam.reshape(1, 256), (128, 256)))
        li = 0.8 - 0.6 * math.exp(-0.3 * la)
        m["laminit"] = np.ascontiguousarray(np.broadcast_to(np.array([[li, 1.0 - li]], np.float32), (128, 2)))
        m["rb33"] = rb33
        m["rbrep"] = rbrep
        m.update(consts[core])
        return m

    win = inp["w_in"]

    def wA(l):
        q = np.concatenate([win[l][:, QBASE[b]:QBASE[b] + 1024] for b in range(3)] + [win[l][:, GBASE[b]:GBASE[b] + 1024] for b in range(3)], axis=1)
        return {"wAq": np.ascontiguousarray(q), "wAo": inp["w_o"][l], "wAfi": inp["w_ff_in"][l, 1], "wAfo": inp["w_ff_out"][l, 1]}

    def wBk(l):
        kv = np.concatenate([win[l][:, KBASE[b]:KBASE[b] + 256] for b in range(3)] + [win[l][:, VBASE[b]:VBASE[b] + 256] for b in range(3)], axis=1)
        return {"wBkv": np.ascontiguousarray(kv), "wBfi": inp["w_ff_in"][l, 0], "wBfo": inp["w_ff_out"][l, 0]}

    depth = _DEPTH[0]
    state = None
    for p in range(depth + 1):
        kind = "first" if p == 0 else ("last" if p == depth else "mid")
        la, lb = max(p - 1, 0), min(p, 3)
        shared = {}
        if p > 0:
            shared.update(wA(la))
        if p < depth:
            shared.update(wBk(lb))
        if p > 0:
            KT_all = np.ascontiguousarray(np.concatenate([state[c]["KT_S_o"] for c in range(NCORE)], axis=0))
            V_all = np.ascontiguousarray(np.concatenate([state[c]["V_S_o"] for c in range(NCORE)], axis=0))
            shared["KT_all_i"] = KT_all
            shared["V_all_i"] = V_all
        maps = []
        for core in range(NCORE):
            m = dict(shared)
            m.update(layer_small(la, lb, core))
            if p == 0:
                m["xin"] = np.ascontiguousarray(np.concatenate([xs[0, core * 2048:(core + 1) * 2048], xp[core]], axis=0))
            else:
                m["xT_i"] = state[core]["xT_o"]
                m["KT_P_i"] = state[core]["KT_P_o"]
                m["V_P_i"] = state[core]["V_P_o"]
                for nm, key in (("KT_Sx", "KT_S_o"), ("V_Sx", "V_S_o")):
                    ext = np.zeros((768, 18 * 128), bf)
                    ext[:, 128:2176] = state[core][key]
                    if core > 0:
                        ext[:, 0:128] = state[core - 1][key][:, 1920:2048]
                    if core < NCORE - 1:
                        ext[:, 2176:2304] = state[core + 1][key][:, 0:128]
                    m[nm] = ext
            maps.append(m)
        r = _run(kind, maps)
        state = [{k: np.asarray(v) for k, v in r[c].items()} for c in range(NCORE)]
    y_prompt = np.zeros((8, 2048, D), np.float32)
    y_sample = np.zeros((1, 16384, D), np.float32)
    for core in range(NCORE):
        yy = np.asarray(state[core]["y"], dtype=np.float32)
        y_sample[0, core * 2048:(core + 1) * 2048] = yy[:2048]
        y_prompt[core] = yy[2048:]
    return (y_prompt, y_sample)


_DEPTH = [4]
```
